# Optimizing a Trainium2 kernel written in Bass

```python
import math
import jax
import jax.numpy as jnp
from jax import lax
import numpy as np

D_MODEL = 1024
BATCH = 8
SEQ = 2048
DEPTH = 4

PLE_DIM = 256
MIX_WIDTH = 256
N_BRANCH = 4
S5_WIDTH = MIX_WIDTH
S5_GROUP = 16
S5_GROUPS = S5_WIDTH // S5_GROUP
S5_STATE = 64
MLA_HEADS = 4
MLA_NOPE = 64
MLA_ROPE = 32
MLA_V = 64
MLA_Q_LORA = 192
MLA_KV_LORA = 128
Q_BLOCK = 128
HG_HEADS = 4
HG_K = 64
HG_V = 64
HG_CHUNK = 16
RET_HEADS = 4
RET_K = 64
RET_V = 64
RET_CHUNK = 64
D_FF = 3584
N_EXPERTS = 8
TOP_K = 2
D_FF_EXPERT = 3584
MOE_BLOCK = 128
N_DENSE = (DEPTH + 1) // 2
N_MOE = DEPTH // 2
ROPE_BASE = 10000.0
EPS = 1e-5
NEG_INF = -1e30
ALPHA = (2 * DEPTH) ** 0.25
BETA = (8 * DEPTH) ** -0.25
IN_SPLITS = (S5_WIDTH, MLA_Q_LORA, MLA_KV_LORA, MLA_ROPE,
             HG_HEADS * HG_K, HG_HEADS * HG_K, HG_HEADS * HG_V, HG_HEADS * HG_V,
             RET_HEADS * RET_K, RET_HEADS * RET_K, RET_HEADS * RET_V, RET_HEADS * RET_V,
             N_BRANCH * D_MODEL)
D_IN = sum(IN_SPLITS)

kernel_name = 'hybrid_gated_s5_mla_hgrn2_retnet_deepnorm'


def layer_norm(x, g, b):
    xf = x.astype(jnp.float32)
    mu = jnp.mean(xf, axis=-1, keepdims=True)
    var = jnp.mean(jnp.square(xf - mu), axis=-1, keepdims=True)
    return ((xf - mu) * lax.rsqrt(var + EPS) * g + b).astype(x.dtype)


def rms_norm(x, g):
    xf = x.astype(jnp.float32)
    return (xf * lax.rsqrt(jnp.mean(xf * xf, axis=-1, keepdims=True) + EPS) * g).astype(x.dtype)


def split_columns(h, sizes):
    out, start = [], 0
    for n in sizes:
        out.append(h[..., start:start + n])
        start += n
    return out


def rope_tables(positions, dim):
    half = dim // 2
    inv_freq = ROPE_BASE ** (-jnp.arange(half, dtype=jnp.float32) / half)
    ang = positions.astype(jnp.float32)[..., None] * inv_freq
    return jnp.cos(ang)[:, :, None, :], jnp.sin(ang)[:, :, None, :]


def apply_rope(x, cos, sin):
    half = x.shape[-1] // 2
    x1, x2 = x[..., :half], x[..., half:]
    return jnp.concatenate([x1 * cos - x2 * sin, x2 * cos + x1 * sin], axis=-1).astype(x.dtype)


def _complex_affine_combine(left, right):
    a1r, a1i, b1r, b1i = left
    a2r, a2i, b2r, b2i = right
    return (a2r * a1r - a2i * a1i, a2r * a1i + a2i * a1r,
            a2r * b1r - a2i * b1i + b2r, a2r * b1i + a2i * b1r + b2i)


def s5_mixer(u, a_re, a_im, log_dt, b_re, b_im, c_re, c_im, d_skip, w_glu):
    f32 = jnp.float32
    bsz, s, _ = u.shape
    uf = u.astype(f32)
    ug = uf.reshape(bsz, s, S5_GROUPS, S5_GROUP)
    dt = jnp.exp(log_dt.astype(f32))[:, None]
    lr, li = a_re.astype(f32), a_im.astype(f32)
    mag = jnp.exp(lr * dt)
    abar_re, abar_im = mag * jnp.cos(li * dt), mag * jnp.sin(li * dt)
    den = lr * lr + li * li
    num_re, num_im = abar_re - 1.0, abar_im
    coef_re = (num_re * lr + num_im * li) / den
    coef_im = (num_im * lr - num_re * li) / den
    br, bi = b_re.astype(f32), b_im.astype(f32)
    bbar_re = coef_re[..., None] * br - coef_im[..., None] * bi
    bbar_im = coef_re[..., None] * bi + coef_im[..., None] * br
    bu_re = jnp.einsum('bsgc,gpc->bsgp', ug, bbar_re)
    bu_im = jnp.einsum('bsgc,gpc->bsgp', ug, bbar_im)
    ar = jnp.broadcast_to(abar_re, bu_re.shape)
    ai = jnp.broadcast_to(abar_im, bu_re.shape)
    _, _, s_re, s_im = lax.associative_scan(_complex_affine_combine, (ar, ai, bu_re, bu_im), axis=1)
    y = (jnp.einsum('bsgp,gcp->bsgc', s_re, c_re.astype(f32))
         - jnp.einsum('bsgp,gcp->bsgc', s_im, c_im.astype(f32)))
    y = y.reshape(bsz, s, S5_WIDTH) + d_skip.astype(f32) * uf
    y = jax.nn.gelu(y)
    return (y * jax.nn.sigmoid(y @ w_glu.astype(f32))).astype(u.dtype)


def causal_mla_attention(q_nope, q_pe, k_nope, k_pe, v):
    bsz, s, h, _ = q_nope.shape
    nb = s // Q_BLOCK
    scale = (MLA_NOPE + MLA_ROPE) ** -0.5
    qn = q_nope.reshape(bsz, nb, Q_BLOCK, h, MLA_NOPE).swapaxes(0, 1)
    qp = q_pe.reshape(bsz, nb, Q_BLOCK, h, MLA_ROPE).swapaxes(0, 1)
    k_idx = jnp.arange(s)

    def one_block(args):
        blk, qn_b, qp_b = args
        sc = (jnp.einsum('bqhd,bkhd->bhqk', qn_b, k_nope)
              + jnp.einsum('bqhr,bkr->bhqk', qp_b, k_pe)).astype(jnp.float32) * scale
        q_idx = blk * Q_BLOCK + jnp.arange(Q_BLOCK)
        sc = jnp.where(k_idx[None, :] <= q_idx[:, None], sc, NEG_INF)
        w = jax.nn.softmax(sc, axis=-1).astype(v.dtype)
        return jnp.einsum('bhqk,bkhd->bqhd', w, v)

    out = lax.map(one_block, (jnp.arange(nb), qn, qp))
    return out.swapaxes(0, 1).reshape(bsz, s, h * MLA_V)


def mla_mixer(c_q, c_kv, k_rope, cos, sin, q_norm_g, kv_norm_g, w_uq, w_ukv):
    bsz, s, _ = c_q.shape
    q = (rms_norm(c_q, q_norm_g) @ w_uq).reshape(bsz, s, MLA_HEADS, MLA_NOPE + MLA_ROPE)
    q_nope, q_pe = q[..., :MLA_NOPE], apply_rope(q[..., MLA_NOPE:], cos, sin)
    kv = (rms_norm(c_kv, kv_norm_g) @ w_ukv).reshape(bsz, s, MLA_HEADS, MLA_NOPE + MLA_V)
    k_nope, v = kv[..., :MLA_NOPE], kv[..., MLA_NOPE:]
    k_pe = apply_rope(k_rope[:, :, None, :], cos, sin)[:, :, 0, :]
    return causal_mla_attention(q_nope, q_pe, k_nope, k_pe, v).astype(c_q.dtype)


def hgrn2_mixer(q, f, i_in, g, lb, norm_g):
    f32 = jnp.float32
    bsz, s, _ = q.shape
    n = s // HG_CHUNK
    lb = lb.astype(f32)
    ff = f.astype(f32)
    log_f = jnp.logaddexp(jnp.log(lb), jnp.log1p(-lb) + jax.nn.log_sigmoid(ff))
    k = (1.0 - lb) * jax.nn.sigmoid(-ff)
    qf = jax.nn.silu(q.astype(f32))

    def chunks(t, d):
        return t.reshape(bsz, n, HG_CHUNK, HG_HEADS, d).transpose(0, 3, 1, 2, 4)

    qc, kc, lfc = chunks(qf, HG_K), chunks(k, HG_K), chunks(log_f, HG_K)
    vc = chunks(i_in.astype(f32), HG_V)
    bcum = jnp.cumsum(lfc, axis=3)
    tri = jnp.tril(jnp.ones((HG_CHUNK, HG_CHUNK), dtype=bool))
    diff = bcum[..., :, None, :] - bcum[..., None, :, :]
    decay = jnp.exp(jnp.where(tri[:, :, None], diff, -jnp.inf))
    attn = jnp.einsum('bhntk,bhnsk,bhntsk->bhnts', qc, kc, decay)
    o_intra = jnp.einsum('bhnts,bhnsv->bhntv', attn, vc)
    b_last = bcum[..., -1:, :]
    kv_chunk = jnp.einsum('bhnck,bhncv->bhnkv', kc * jnp.exp(b_last - bcum), vc)
    chunk_decay = jnp.exp(b_last[..., 0, :])

    def step(state, inp):
        dec, kv = inp
        return dec[..., None] * state + kv, state

    init = jnp.zeros((bsz, HG_HEADS, HG_K, HG_V), f32)
    _, s_prev = lax.scan(step, init, (jnp.moveaxis(chunk_decay, 2, 0), jnp.moveaxis(kv_chunk, 2, 0)))
    s_prev = jnp.moveaxis(s_prev, 0, 2)
    o_inter = jnp.einsum('bhnck,bhnkv->bhncv', qc * jnp.exp(bcum), s_prev)
    o = (o_intra + o_inter).transpose(0, 2, 3, 1, 4).reshape(bsz, s, HG_HEADS, HG_V)
    o = rms_norm(o, norm_g.astype(f32).reshape(HG_HEADS, HG_V)).reshape(bsz, s, HG_HEADS * HG_V)
    return (o * jax.nn.silu(g.astype(f32))).astype(q.dtype)


def retention_mixer(q, k, v, g, cos, sin, gn_g, gn_b):
    f32 = jnp.float32
    bsz, s, _ = q.shape
    n = s // RET_CHUNK
    qh = apply_rope(q.astype(f32).reshape(bsz, s, RET_HEADS, RET_K), cos, sin) * RET_K ** -0.5
    kh = apply_rope(k.astype(f32).reshape(bsz, s, RET_HEADS, RET_K), cos, sin)
    vh = v.astype(f32).reshape(bsz, s, RET_HEADS, RET_V)

    def chunks(t):
        return t.reshape(bsz, n, RET_CHUNK, RET_HEADS, t.shape[-1]).transpose(0, 3, 1, 2, 4)

    qc, kc, vc = chunks(qh), chunks(kh), chunks(vh)
    log_gamma = jnp.log(1.0 - 2.0 ** (-5.0 - jnp.arange(RET_HEADS, dtype=f32)))
    t_idx = jnp.arange(RET_CHUNK, dtype=f32)
    rel = t_idx[:, None] - t_idx[None, :]
    dmat = jnp.where(rel >= 0, jnp.exp(jnp.maximum(rel, 0.0)[None] * log_gamma[:, None, None]), 0.0)
    scores = jnp.einsum('bhntd,bhnsd->bhnts', qc, kc) * dmat[None, :, None]
    o_intra = jnp.einsum('bhnts,bhnsv->bhntv', scores, vc)
    k_decay = jnp.exp((RET_CHUNK - 1 - t_idx)[None, :] * log_gamma[:, None])
    kv_chunk = jnp.einsum('bhnck,bhncv->bhnkv', kc * k_decay[None, :, None, :, None], vc)
    chunk_decay = jnp.exp(RET_CHUNK * log_gamma)[None, :, None, None]

    def step(state, kv):
        return chunk_decay * state + kv, state

    init = jnp.zeros((bsz, RET_HEADS, RET_K, RET_V), f32)
    _, s_prev = lax.scan(step, init, jnp.moveaxis(kv_chunk, 2, 0))
    s_prev = jnp.moveaxis(s_prev, 0, 2)
    q_decay = jnp.exp((t_idx + 1.0)[None, :] * log_gamma[:, None])
    o_inter = jnp.einsum('bhnck,bhnkv->bhncv', qc * q_decay[None, :, None, :, None], s_prev)
    o = (o_intra + o_inter).transpose(0, 2, 3, 1, 4).reshape(bsz, s, RET_HEADS, RET_V)
    o = layer_norm(o, gn_g.astype(f32).reshape(RET_HEADS, RET_V), gn_b.astype(f32).reshape(RET_HEADS, RET_V))
    return (jax.nn.silu(g.astype(f32)) * o.reshape(bsz, s, RET_HEADS * RET_V)).astype(q.dtype)


def swiglu(x, w_gate, w_up, w_down):
    return (jax.nn.silu(x @ w_gate) * (x @ w_up)) @ w_down


def moe_swiglu(x, w_router, w_gate, w_up, w_down):
    bsz, s, d = x.shape
    t = bsz * s
    xf = x.reshape(t, d)
    logits = (xf @ w_router).astype(jnp.float32)
    top_val, top_idx = lax.top_k(logits, TOP_K)
    top_w = jax.nn.softmax(top_val, axis=-1)
    m = t * TOP_K
    e_flat = top_idx.reshape(m)
    tok_flat = jnp.repeat(jnp.arange(t, dtype=jnp.int32), TOP_K)
    w_flat = top_w.reshape(m)
    order = jnp.argsort(e_flat)
    e_sorted = e_flat[order]
    counts = jnp.bincount(e_flat, length=N_EXPERTS)
    padded = (counts + MOE_BLOCK - 1) // MOE_BLOCK * MOE_BLOCK
    pad_end = jnp.cumsum(padded)
    pad_start = pad_end - padded
    start = jnp.cumsum(counts) - counts
    dest = pad_start[e_sorted] + jnp.arange(m) - start[e_sorted]
    n_blk = (m + MOE_BLOCK - 1) // MOE_BLOCK + N_EXPERTS
    p_len = n_blk * MOE_BLOCK
    slot_tok = jnp.full((p_len,), t, jnp.int32).at[dest].set(tok_flat[order])
    slot_w = jnp.zeros((p_len,), jnp.float32).at[dest].set(w_flat[order])
    blk_start = jnp.arange(n_blk) * MOE_BLOCK
    blk_expert = jnp.minimum(jnp.sum(pad_end[None, :] <= blk_start[:, None], axis=1), N_EXPERTS - 1)
    x_pad = jnp.concatenate([xf, jnp.zeros((1, d), xf.dtype)], axis=0)
    xb = x_pad[slot_tok].reshape(n_blk, MOE_BLOCK, d)

    def expert_block(args):
        xe, e = args
        return (jax.nn.silu(xe @ w_gate[e]) * (xe @ w_up[e])) @ w_down[e]

    yb = lax.map(expert_block, (xb, blk_expert)).reshape(p_len, d)
    y = jnp.zeros((t + 1, d), yb.dtype).at[slot_tok].add(yb * slot_w[:, None].astype(yb.dtype))
    return y[:t].reshape(bsz, s, d).astype(x.dtype)


def setup_inputs(seed: int = 0) -> dict:
    key = jax.random.key(seed)
    ks = iter(jax.random.split(key, 40))
    f32 = jnp.float32

    def nrm(shape, scale):
        return jax.random.normal(next(ks), shape, f32) * scale

    def gain(shape):
        return 1.0 + nrm(shape, 0.01)

    x = nrm((BATCH, SEQ, D_MODEL), 1.0)
    p = nrm((DEPTH, BATCH, SEQ, PLE_DIM), 1.0)
    positions = (jax.random.randint(next(ks), (BATCH, 1), 0, 1024, dtype=jnp.int32)
                 + jnp.arange(SEQ, dtype=jnp.int32)[None, :])
    w_in = nrm((DEPTH, D_MODEL, D_IN), D_MODEL ** -0.5)
    s5_a_re = -0.5 + nrm((DEPTH, S5_GROUPS, S5_STATE), 0.01)
    s5_a_im = math.pi * jnp.arange(S5_STATE, dtype=f32) + nrm((DEPTH, S5_GROUPS, S5_STATE), 0.01)
    s5_log_dt = jax.random.uniform(next(ks), (DEPTH, S5_GROUPS), f32, math.log(0.001), math.log(0.1))
    s5_b_re = nrm((DEPTH, S5_GROUPS, S5_STATE, S5_GROUP), (2 * S5_GROUP) ** -0.5)
    s5_b_im = nrm((DEPTH, S5_GROUPS, S5_STATE, S5_GROUP), (2 * S5_GROUP) ** -0.5)
    s5_c_re = nrm((DEPTH, S5_GROUPS, S5_GROUP, S5_STATE), 0.5)
    s5_c_im = nrm((DEPTH, S5_GROUPS, S5_GROUP, S5_STATE), 0.5)
    s5_d = nrm((DEPTH, S5_WIDTH), 1.0)
    s5_w_glu = nrm((DEPTH, S5_WIDTH, S5_WIDTH), S5_WIDTH ** -0.5)
    mla_q_norm = gain((DEPTH, MLA_Q_LORA))
    mla_kv_norm = gain((DEPTH, MLA_KV_LORA))
    mla_w_uq = nrm((DEPTH, MLA_Q_LORA, MLA_HEADS * (MLA_NOPE + MLA_ROPE)), MLA_Q_LORA ** -0.5)
    mla_w_ukv = nrm((DEPTH, MLA_KV_LORA, MLA_HEADS * (MLA_NOPE + MLA_V)), MLA_KV_LORA ** -0.5)
    hg_lb_raw = nrm((DEPTH, HG_HEADS * HG_K), 0.1)
    hg_norm = gain((DEPTH, HG_HEADS * HG_V))
    ret_gn_g = gain((DEPTH, RET_HEADS * RET_V))
    ret_gn_b = nrm((DEPTH, RET_HEADS * RET_V), 0.01)
    w_branch = nrm((DEPTH, N_BRANCH, MIX_WIDTH, D_MODEL), BETA * MIX_WIDTH ** -0.5)
    w_o = nrm((DEPTH, D_MODEL, D_MODEL), BETA * D_MODEL ** -0.5)
    ln1_g = gain((DEPTH, D_MODEL))
    ln1_b = nrm((DEPTH, D_MODEL), 0.01)
    ff_w_gate = nrm((N_DENSE, D_MODEL, D_FF), BETA * D_MODEL ** -0.5)
    ff_w_up = nrm((N_DENSE, D_MODEL, D_FF), BETA * D_MODEL ** -0.5)
    ff_w_down = nrm((N_DENSE, D_FF, D_MODEL), BETA * D_FF ** -0.5)
    moe_router = nrm((N_MOE, D_MODEL, N_EXPERTS), D_MODEL ** -0.5)
    moe_w_gate = nrm((N_MOE, N_EXPERTS, D_MODEL, D_FF_EXPERT), BETA * D_MODEL ** -0.5)
    moe_w_up = nrm((N_MOE, N_EXPERTS, D_MODEL, D_FF_EXPERT), BETA * D_MODEL ** -0.5)
    moe_w_down = nrm((N_MOE, N_EXPERTS, D_FF_EXPERT, D_MODEL), BETA * D_FF_EXPERT ** -0.5)
    ple_w_gate = nrm((DEPTH, D_MODEL, D_MODEL), D_MODEL ** -0.5)
    ple_w_proj = nrm((DEPTH, PLE_DIM, D_MODEL), PLE_DIM ** -0.5)
    ln2_g = gain((DEPTH, D_MODEL))
    ln2_b = nrm((DEPTH, D_MODEL), 0.01)
    return {'x': x, 'p': p, 'positions': positions, 'w_in': w_in,
            's5_a_re': s5_a_re, 's5_a_im': s5_a_im, 's5_log_dt': s5_log_dt,
            's5_b_re': s5_b_re, 's5_b_im': s5_b_im, 's5_c_re': s5_c_re, 's5_c_im': s5_c_im,
            's5_d': s5_d, 's5_w_glu': s5_w_glu,
            'mla_q_norm': mla_q_norm, 'mla_kv_norm': mla_kv_norm, 'mla_w_uq': mla_w_uq, 'mla_w_ukv': mla_w_ukv,
            'hg_lb_raw': hg_lb_raw, 'hg_norm': hg_norm, 'ret_gn_g': ret_gn_g, 'ret_gn_b': ret_gn_b,
            'w_branch': w_branch, 'w_o': w_o, 'ln1_g': ln1_g, 'ln1_b': ln1_b,
            'ff_w_gate': ff_w_gate, 'ff_w_up': ff_w_up, 'ff_w_down': ff_w_down,
            'moe_router': moe_router, 'moe_w_gate': moe_w_gate, 'moe_w_up': moe_w_up, 'moe_w_down': moe_w_down,
            'ple_w_gate': ple_w_gate, 'ple_w_proj': ple_w_proj, 'ln2_g': ln2_g, 'ln2_b': ln2_b}


def reference(x, p, positions, w_in, s5_a_re, s5_a_im, s5_log_dt, s5_b_re, s5_b_im, s5_c_re, s5_c_im,
              s5_d, s5_w_glu, mla_q_norm, mla_kv_norm, mla_w_uq, mla_w_ukv, hg_lb_raw, hg_norm,
              ret_gn_g, ret_gn_b, w_branch, w_o, ln1_g, ln1_b, ff_w_gate, ff_w_up, ff_w_down,
              moe_router, moe_w_gate, moe_w_up, moe_w_down, ple_w_gate, ple_w_proj, ln2_g, ln2_b):
    bsz, s, _ = x.shape
    cos_m, sin_m = rope_tables(positions, MLA_ROPE)
    cos_r, sin_r = rope_tables(positions, RET_K)
    lb_all = jnp.cumsum(jax.nn.softmax(hg_lb_raw.astype(jnp.float32), axis=0), axis=0)
    lb_all = lb_all - lb_all[0]
    for i in range(DEPTH):
        h = x @ w_in[i]
        (u_s5, c_q, c_kv, k_rope, hq, hf, hi, hg, rq, rk, rv, rg, gate_logits) = split_columns(h, IN_SPLITS)
        y_a = s5_mixer(u_s5, s5_a_re[i], s5_a_im[i], s5_log_dt[i], s5_b_re[i], s5_b_im[i],
                       s5_c_re[i], s5_c_im[i], s5_d[i], s5_w_glu[i])
        y_b = mla_mixer(c_q, c_kv, k_rope, cos_m, sin_m, mla_q_norm[i], mla_kv_norm[i], mla_w_uq[i], mla_w_ukv[i])
        y_c = hgrn2_mixer(hq, hf, hi, hg, lb_all[i], hg_norm[i])
        y_d = retention_mixer(rq, rk, rv, rg, cos_r, sin_r, ret_gn_g[i], ret_gn_b[i])
        branches = jnp.einsum('bsnc,ncd->bsnd', jnp.stack([y_a, y_b, y_c, y_d], axis=2), w_branch[i])
        gates = jax.nn.sigmoid(gate_logits.reshape(bsz, s, N_BRANCH, D_MODEL))
        mixed = jnp.sum(gates * branches, axis=2) @ w_o[i]
        x = layer_norm(ALPHA * x + mixed, ln1_g[i], ln1_b[i])
        if i % 2 == 0:
            f = swiglu(x, ff_w_gate[i // 2], ff_w_up[i // 2], ff_w_down[i // 2])
        else:
            f = moe_swiglu(x, moe_router[i // 2], moe_w_gate[i // 2], moe_w_up[i // 2], moe_w_down[i // 2])
        ple = jax.nn.sigmoid(x @ ple_w_gate[i]) * (p[i] @ ple_w_proj[i])
        x = layer_norm(ALPHA * x + f + ple, ln2_g[i], ln2_b[i])
    return x
```

```python
import math
from contextlib import ExitStack
import numpy as np
import concourse.bass as bass
import concourse.mybir as mybir
from concourse.bass_utils import run_bass_kernel_spmd

F32 = mybir.dt.float32
BF16 = mybir.dt.bfloat16
I32 = mybir.dt.int32
AF = mybir.ActivationFunctionType
ALU = mybir.AluOpType

T = 2048
D = 1024
NL = 4
DFF = 3584
NE = 8
ALPHA = (2 * NL) ** 0.25
EPS = 1e-5
MAGIC = 12582912.0
TWO_PI = 2.0 * math.pi
SB_BASE = 16512
SB_TOP = 229344
NXC = 3328
C_U, C_CQ, C_CKV, C_KPE, C_KPER = 0, 256, 448, 576, 672
C_HQ, C_HF, C_HI, C_HG = 768, 1024, 1280, 1536
C_RQ, C_RK, C_RV, C_RG, C_RQP, C_RKP = 1792, 2048, 2304, 2560, 2816, 3072


class Reg:
    __slots__ = ("name", "last_w", "readers", "excl")

    def __init__(self, name=""):
        self.name = name
        self.last_w = None
        self.readers = []
        self.excl = False


class DSem:
    __slots__ = ("sem", "count")

    def __init__(self, sem):
        self.sem = sem
        self.count = 0


class Buf:
    def __init__(self, h, name):
        self.h = h
        self.reg = Reg(name)
        self.dsem = None

    def __getitem__(self, idx):
        return self.h[idx]


class Op:
    __slots__ = ("eng", "fn", "reads", "writes", "acc", "dsem", "idx", "signal", "waits", "mark", "phase", "ninst")

    def __init__(self, eng, fn, reads, writes, acc, dsem):
        self.eng = eng
        self.fn = fn
        self.reads = reads
        self.writes = writes
        self.acc = acc
        self.dsem = dsem
        self.signal = None
        self.waits = []
        self.mark = False


def _regs(lst):
    out = []
    for x in lst:
        if x is None:
            continue
        if isinstance(x, (list, tuple)):
            out.extend(_regs(x))
        elif isinstance(x, Reg):
            out.append(x)
        else:
            out.append(x.reg)
    return out


class Prog:
    ENGS = ("pe", "act", "dve", "pool", "sync")

    def __init__(self, nc):
        self.nc = nc
        self.ops = []
        self.es = ExitStack()
        self.n = 0
        self.off = SB_BASE
        self.extra_reads = []
        self.regions = None

    def sb(self, shape, dtype, name=None):
        self.n += 1
        name = name or f"t{self.n}"
        nbytes = int(np.prod(shape[1:])) * (2 if dtype == BF16 else 4)
        nbytes = (nbytes + 31) // 32 * 32
        if self.regions is None:
            assert self.off + nbytes <= SB_TOP, f"SBUF overflow allocating {name}"
            off = self.off
            self.off += nbytes
        else:
            for rg in self.regions:
                if rg[0] + nbytes <= rg[1]:
                    off = rg[0]
                    rg[0] += nbytes
                    break
            else:
                raise AssertionError(f"SBUF arena overflow allocating {name} ({nbytes} B): {self.regions}")
        h = self.nc.alloc_sbuf_tensor_at(f"{name}_{self.n}", list(shape), dtype, offset=off)
        return Buf(h, name)

    def set_regions(self, regs):
        self.regions = [list(r) for r in regs] if regs is not None else None

    def ps(self, name):
        h = self.es.enter_context(self.nc.psum_tensor(name, [128, 512], F32))
        b = Buf(h, name)
        b.reg.excl = True
        return b

    def new_dsem(self):
        self.n += 1
        return DSem(self.es.enter_context(self.nc.semaphore(f"ds{self.n}")))

    def op(self, eng, fn, reads=(), writes=(), acc=(), dsem=None, arena=True):
        rd = _regs(reads)
        if arena:
            rd = rd + self.extra_reads
        o = Op(eng, fn, rd, _regs(writes), _regs(acc), dsem)
        o.phase = getattr(self, "phase", "")
        o.ninst = 0
        o.idx = len(self.ops)
        self.ops.append(o)
        return o

    def dma(self, out, in_, reads=(), writes=(), dsem=None, eng="sync", arena=True, **kw):
        assert dsem is not None
        return self.op(eng, lambda e: e.dma_start(out=out, in_=in_, **kw), reads, writes, (), dsem, arena)

    def fence(self, reg, extra_writes=()):
        self.op("pool", lambda e: e.memset(self.fence_scratch[0:1, 0:1], 0.0), [], [reg, self.fence_scratch] + list(extra_writes),
                arena=False)

    def finalize(self, final_dsems=()):
        nc = self.nc
        ops = self.ops
        deps = [None] * len(ops)
        for o in ops:
            d = set()
            for r in o.reads:
                if r.last_w is not None:
                    d.add(r.last_w)
                if r.excl:
                    d.update(i for i in r.readers if ops[i].eng != o.eng)
            for w in o.writes:
                if w.last_w is not None:
                    d.add(w.last_w)
                d.update(w.readers)
            for w in o.acc:
                if w.last_w is not None and ops[w.last_w].eng != o.eng:
                    d.add(w.last_w)
                d.update(w.readers)
            d.discard(o.idx)
            deps[o.idx] = d
            for r in o.reads:
                r.readers.append(o.idx)
            for w in o.writes:
                w.last_w = o.idx
                w.readers = []
            for w in o.acc:
                w.last_w = o.idx
                w.readers = []
            for i in d:
                ops[i].mark = True
        esem = {en: self.es.enter_context(nc.semaphore("sem_" + en)) for en in ("pe", "act", "dve", "pool")}
        ecount = {en: 0 for en in esem}
        seen = {en: {} for en in self.ENGS}
        for o in ops:
            need = {}
            for i in deps[o.idx]:
                p = ops[i]
                if p.dsem is not None:
                    key, sem, val = id(p.dsem), p.dsem.sem, 16 * p.dsem.count
                else:
                    key, sem, val = p.eng, esem[p.eng], p.signal
                if key not in need or need[key][1] < val:
                    need[key] = (sem, val)
            sn = seen[o.eng]
            for key, (sem, val) in need.items():
                if sn.get(key, 0) >= val:
                    continue
                sn[key] = val
                o.waits.append((sem, val))
            if o.dsem is not None:
                o.dsem.count += 1
            elif o.mark:
                ecount[o.eng] += 1
                o.signal = ecount[o.eng]
        finals = [(d.sem, 16 * d.count) for d in final_dsems if d.count > 0]
        engmap = {"pe": "tensor", "act": "scalar", "dve": "vector", "pool": "gpsimd", "sync": "sync"}
        with nc.Block() as block:
            for en in self.ENGS:
                myops = [o for o in ops if o.eng == en]

                def body(e, myops=myops, en=en):
                    for o in myops:
                        for sem, val in o.waits:
                            e.wait_ge(sem, val)
                        n0 = nc.n_instructions()
                        ins = o.fn(e)
                        o.ninst = nc.n_instructions() - n0
                        if o.dsem is not None:
                            ins.then_inc(o.dsem.sem, 16)
                        elif o.mark:
                            ins.then_inc(esem[en], 1)
                    if en == "sync":
                        for sem, val in finals:
                            e.wait_ge(sem, val)
                getattr(block, engmap[en])(body)
        self.stats = {en: sum(1 for o in ops if o.eng == en) for en in self.ENGS}
        self.stats["marks"] = dict(ecount)
        self.es.close()

    def mm(self, out, pairs, reads, writes=(), acc=(), start=True, stop=True):
        pairs = list(pairs)

        def fn(e):
            n = len(pairs)
            ins = None
            for i, (l, r) in enumerate(pairs):
                ins = e.matmul(out, l, r, start=(start and i == 0), stop=(stop and i == n - 1))
            return ins
        return self.op("pe", fn, reads, writes, acc)

    def tr(self, out, in_, ident, reads, writes):
        return self.op("pe", lambda e: e.transpose(out, in_, ident), reads, writes)

    def act(self, out, in_, func, reads, writes, scale=None, bias=None, eng="act"):
        kw = {}
        if scale is not None:
            kw["scale"] = scale
        if bias is not None:
            kw["bias"] = bias
        return self.op(eng, lambda e: e.activation(out=out, in_=in_, func=func, **kw), reads, writes)

    def tt(self, out, in0, in1, op, reads, writes, eng="dve"):
        return self.op(eng, lambda e: e.tensor_tensor(out=out, in0=in0, in1=in1, op=op), reads, writes)

    def ts(self, out, in0, s1, s2, op0, op1, reads, writes, eng="dve"):
        if op1 is None:
            return self.op(eng, lambda e: e.tensor_scalar(out=out, in0=in0, scalar1=s1, scalar2=None, op0=op0), reads, writes)
        return self.op(eng, lambda e: e.tensor_scalar(out=out, in0=in0, scalar1=s1, scalar2=s2, op0=op0, op1=op1), reads, writes)

    def stt(self, out, in0, scalar, in1, op0, op1, reads, writes):
        return self.op("dve", lambda e: e.scalar_tensor_tensor(out=out, in0=in0, scalar=scalar, in1=in1, op0=op0, op1=op1), reads, writes)

    def cp(self, out, in_, reads, writes, eng="dve"):
        if eng == "act":
            return self.op("act", lambda e: e.activation(out=out, in_=in_, func=AF.Copy), reads, writes)
        return self.op(eng, lambda e: e.tensor_copy(out=out, in_=in_), reads, writes)

    def scan(self, out, d0, d1, init, reads, writes):
        return self.op("dve", lambda e: e.tensor_tensor_scan(out=out, data0=d0, data1=d1, initial=init, op0=ALU.mult, op1=ALU.add), reads, writes)


def _gammas():
    return [1.0 - 2.0 ** (-5.0 - h) for h in range(4)]


def host_consts():
    c = {}
    c["ident"] = np.eye(128, dtype=np.float32)
    s = np.arange(128)[:, None]
    t = np.arange(512)[None, :]
    c["cmask"] = np.stack([(t >= s + 128 * v).astype(np.float32) for v in range(4)], axis=1)
    ss = np.arange(128)[:, None]
    tt = np.arange(128)[None, :]
    c["bdmask"] = ((ss // 16 == tt // 16) & (ss <= tt)).astype(np.float32)
    n = np.arange(8)[None, :, None]
    c["cexp"] = np.broadcast_to((ss[:, :, None] // 16 == n), (128, 8, 64)).astype(np.float32).copy()
    c["iota512"] = np.broadcast_to(np.arange(512, dtype=np.float32)[None, :], (128, 512)).copy()
    c["rst16"] = np.broadcast_to((np.arange(512) % 16 != 0).astype(np.float32)[None, :], (128, 512)).copy()
    rc = np.zeros((128, 2), np.float32)
    for r in range(64):
        j = r % 32
        rc[r, 0] = (10000.0 ** (-j / 32.0)) / TWO_PI
        rc[r, 1] = -TWO_PI if r < 32 else TWO_PI
    for r in range(64, 96):
        j = (r - 64) % 16
        rc[r, 0] = (10000.0 ** (-j / 16.0)) / TWO_PI
        rc[r, 1] = -TWO_PI if r < 80 else TWO_PI
    c["ropec"] = rc
    g = _gammas()
    gq = np.zeros((64, 4, 512), np.float32)
    gk = np.zeros((64, 4, 128), np.float32)
    for h in range(4):
        gq[:, h, :] = (g[h] ** np.arange(512, dtype=np.float64))[None, :]
        gk[:, h, :] = (g[h] ** (-np.arange(128, dtype=np.float64)))[None, :]
    c["gq"] = gq
    c["gk"] = gk
    gm = np.zeros((128, 8), np.float32)
    for r in range(128):
        gm[r, r // 16] = 1.0
    c["grpmask"] = gm
    sg = np.ones((128, 2), np.float32)
    sg[64:, 0] = -1.0
    sg[:, 1] = -1.0
    c["sgn"] = sg
    sel = np.zeros((64, 8, 128), np.float32)
    for e in range(8):
        sel[e, e, :] = 1.0
        sel[32 + e, e, :] = 1.0
    c["sel"] = sel
    sw = np.zeros((128, 128), np.float32)
    for r in range(128):
        sw[r, (r + 64) % 128] = 1.0
    c["swapid"] = sw
    return c


def host_weights(inp):
    w = {}
    w_in = inp["w_in"]
    wx = np.zeros((NL, D, NXC), np.float32)
    wx[:, :, 0:576] = w_in[:, :, 0:576]
    kr = w_in[:, :, 576:608]
    wx[:, :, C_KPE + 64:C_KPE + 96] = kr
    wx[:, :, C_KPER + 64:C_KPER + 80] = kr[:, :, 16:32]
    wx[:, :, C_KPER + 80:C_KPER + 96] = kr[:, :, 0:16]
    wx[:, :, C_HQ:C_HQ + 1024] = w_in[:, :, 608:1632]
    wx[:, :, C_RQ:C_RQ + 1024] = w_in[:, :, 1632:2656]
    for (src, dst) in ((1632, C_RQP), (1888, C_RKP)):
        blk = w_in[:, :, src:src + 256].reshape(NL, D, 4, 2, 32)
        wx[:, :, dst:dst + 256] = blk[:, :, :, ::-1, :].reshape(NL, D, 256)
    w["wx"] = wx
    w["wg"] = np.ascontiguousarray(w_in[:, :, 2656:])
    uq = inp["mla_w_uq"]
    uqr = np.zeros_like(uq)
    for h in range(4):
        uqr[:, :, h * 96 + 64:h * 96 + 80] = uq[:, :, h * 96 + 80:h * 96 + 96]
        uqr[:, :, h * 96 + 80:h * 96 + 96] = uq[:, :, h * 96 + 64:h * 96 + 80]
    def k2(a):
        o = np.zeros((NL, 128, 2, a.shape[2]), np.float32)
        o[:, :, 0, :] = a[:, 0:128, :]
        o[:, 0:64, 1, :] = a[:, 128:192, :]
        return o
    w["uq"] = k2(uq)
    w["uqr"] = k2(uqr)
    w["ukv"] = inp["mla_w_ukv"]
    are, aim, ldt = inp["s5_a_re"], inp["s5_a_im"], inp["s5_log_dt"]
    bre, bim = inp["s5_b_re"], inp["s5_b_im"]
    cre, cim = inp["s5_c_re"], inp["s5_c_im"]
    aB = np.zeros((NL, 128, 2, 2, 64), np.float32)
    bT = np.zeros((NL, 128, 2, 2, 64), np.float32)
    ldB = np.zeros((NL, 128, 2), np.float32)
    for g in range(16):
        cc, gi = g // 8, g % 8
        rows = slice(gi * 16, gi * 16 + 16)
        aB[:, rows, cc, 0, :] = are[:, g, None, :]
        aB[:, rows, cc, 1, :] = aim[:, g, None, :]
        ldB[:, rows, cc] = ldt[:, g, None]
        bT[:, rows, cc, 0, :] = bre[:, g].transpose(0, 2, 1)
        bT[:, rows, cc, 1, :] = bim[:, g].transpose(0, 2, 1)
    w["s5aB"], w["s5bT"], w["s5ldB"] = aB, bT, ldB
    aC = np.zeros((NL, 128, 2, 16), np.float32)
    aC[:, 0:64, 0, :] = are.transpose(0, 2, 1)
    aC[:, 64:128, 0, :] = are.transpose(0, 2, 1)
    aC[:, 0:64, 1, :] = aim.transpose(0, 2, 1)
    aC[:, 64:128, 1, :] = aim.transpose(0, 2, 1)
    w["s5aC"] = aC
    w["s5ldC"] = np.ascontiguousarray(np.broadcast_to(ldt[:, None, :], (NL, 128, 16)))
    C1 = np.zeros((NL, 128, 16, 128), np.float32)
    C2 = np.zeros((NL, 128, 16, 128), np.float32)
    for g in range(16):
        gi = g % 8
        cols = slice(gi * 16, gi * 16 + 16)
        C1[:, 0:64, g, cols] = cre[:, g].transpose(0, 2, 1)
        C1[:, 64:128, g, cols] = cim[:, g].transpose(0, 2, 1)
        C2[:, 0:64, g, cols] = cim[:, g].transpose(0, 2, 1)
        C2[:, 64:128, g, cols] = cre[:, g].transpose(0, 2, 1)
    w["s5C1"], w["s5C2"] = C1, C2
    w["s5glu"] = inp["s5_w_glu"]
    w["wbr"] = inp["w_branch"]
    w["wo"] = inp["w_o"]
    w["ffg"], w["ffu"], w["ffd"] = inp["ff_w_gate"], inp["ff_w_up"], inp["ff_w_down"]
    w["mog"], w["mou"], w["mod"], w["mor"] = inp["moe_w_gate"], inp["moe_w_up"], inp["moe_w_down"], inp["moe_router"]
    w["plg"], w["plp"] = inp["ple_w_gate"], inp["ple_w_proj"]
    return w


class K:
    pass


def build(cfg):
    nc = bass.Bass("TRN2", target_bir_lowering=False)
    P = Prog(nc)
    k = K()
    k.P, k.nc, k.cfg = P, nc, cfg
    k.dram = {}

    def din(name, shape, dtype=F32):
        k.dram[name] = nc.dram_tensor(name, list(shape), dtype, kind="ExternalInput").ap()
        return k.dram[name]

    def dout(name, shape, dtype=F32):
        k.dram[name] = nc.dram_tensor(name, list(shape), dtype, kind="ExternalOutput").ap()
        return k.dram[name]

    din("x", [T, D])
    din("pos", [1, T], I32)
    din("wx", [NL, D, NXC])
    din("uq", [NL, 128, 2, 384])
    din("uqr", [NL, 128, 2, 384])
    din("ukv", [NL, 128, 512])
    din("s5aB", [NL, 128, 2, 2, 64])
    din("s5bT", [NL, 128, 2, 2, 64])
    din("s5ldB", [NL, 128, 2])
    din("s5aC", [NL, 128, 2, 16])
    din("s5ldC", [NL, 128, 16])
    din("s5C1", [NL, 128, 16, 128])
    din("s5C2", [NL, 128, 16, 128])
    din("s5glu", [NL, 256, 256])
    din("wg", [NL, D, 4096])
    din("wbr", [NL, 4, 256, D])
    din("wo", [NL, D, D])
    din("ffg", [2, D, DFF]); din("ffu", [2, D, DFF]); din("ffd", [2, DFF, D])
    din("mog", [2, NE, D, DFF]); din("mou", [2, NE, D, DFF]); din("mod", [2, NE, DFF, D]); din("mor", [2, D, NE])
    din("plg", [NL, D, D]); din("plp", [NL, 256, D])
    din("p", [NL, T, 256])
    hc = host_consts()
    for nm, v in hc.items():
        din("c_" + nm, v.shape)
    dout("y", [T, D])
    k.taps = {}
    k.out_dsems = []

    k.XTb = P.sb([128, 8, T], BF16, "XTb")
    k.rXT = [[Reg(f"XT{c}_{j}") for j in range(4)] for c in range(8)]
    k.rXTb = [[Reg(f"XTb{c}_{j}") for j in range(4)] for c in range(8)]
    P.fence_scratch = P.sb([128, 8], F32, "fsc")
    k.ARENA = Reg("arena")
    P.extra_reads = [k.ARENA]
    k.PS = [P.ps(f"ps{i}") for i in range(8)]
    k.c = {}
    ds_c = P.new_dsem()
    ds_c2 = P.new_dsem()
    for nm, v in hc.items():
        if nm in ("cmask", "bdmask", "cexp", "sel"):
            b16 = P.sb(list(v.shape), BF16, nm + "16")
            P.dma(b16.h[:], k.dram["c_" + nm], writes=[b16], dsem=ds_c2, eng="pool", arena=False)
            k.c[nm + "16"] = b16
        else:
            k.c[nm] = P.sb(list(v.shape), F32, "c_" + nm)
            P.dma(k.c[nm].h[:], k.dram["c_" + nm], writes=[k.c[nm]], dsem=ds_c, arena=False)
    k.ones64 = P.sb([64, 64], F32, "ones64")
    P.op("dve", lambda e: e.memset(k.ones64.h[:], 1.0 / 64.0), [], [k.ones64])
    k.ones128 = P.sb([128, 128], F32, "ones128")
    P.op("dve", lambda e: e.memset(k.ones128.h[:], 1.0), [], [k.ones128])
    k.ones16 = P.sb([128, 128], BF16, "ones16")
    P.op("dve", lambda e: e.memset(k.ones16.h[:], 1.0), [], [k.ones16])
    k.prm = {}
    for nm, shp in PRM_SHAPES.items():
        din("p_" + nm, shp)
        k.prm[nm] = P.sb(list(shp), F32, "p_" + nm)
        P.dma(k.prm[nm].h[:], k.dram["p_" + nm], writes=[k.prm[nm]], dsem=ds_c, arena=False)
    hgrn_setup(k)
    k.xt_off = P.off
    k.XT = P.sb([128, 8, T], F32, "XT")
    k.up_off = P.off
    k.spill = nc.dram_tensor("xt_spill", [128, 8, T], F32, kind="Internal").ap()
    k.rSP = [Reg(f"sp{c}") for c in range(8)]
    k.ds_sp = P.new_dsem()
    k.ds_tap = P.new_dsem()
    k.out_dsems.append(k.ds_tap)

    if not cfg.get("skip_x"):
        load_x(k)
    nl = cfg.get("nl", 0)
    for l in range(nl):
        layer(k, l)
    if not cfg.get("skip_x") and not cfg.get("skip_store"):
        store_out(k)
    P.finalize(final_dsems=k.out_dsems)
    return nc, k


def tap(k, name, buf_ap, shape, reads, dtype=F32):
    P = k.P
    if not k.cfg.get("taps"):
        return
    d = k.nc.dram_tensor("tap_" + name, list(shape), dtype, kind="ExternalOutput").ap()
    P.dma(d, buf_ap, reads=reads, dsem=k.ds_tap)
    k.taps[name] = d


def load_x(k):
    P = k.P
    mark = P.off
    P.off = k.up_off
    xin = [P.sb([128, D], F32, f"xin{i}") for i in range(2)]
    for b in xin:
        b.dsem = P.new_dsem()
    ident = k.c["ident"]
    for i in range(16):
        xb = xin[i % 2]
        P.dma(xb.h[:], k.dram["x"][i * 128:(i + 1) * 128, :], writes=[xb], dsem=xb.dsem)
        j = i // 4
        for half in range(2):
            ps = k.PS[(2 * i + half) % 4]
            def fn(e, ps=ps, xb=xb, half=half):
                ins = None
                for q in range(4):
                    c = half * 4 + q
                    ins = e.transpose(ps.h[:, q * 128:(q + 1) * 128], xb.h[:, c * 128:(c + 1) * 128], ident.h[:])
                return ins
            P.op("pe", fn, [xb, ident], [ps])
            src = ps.h[:].rearrange("p (c t) -> p c t", c=4)
            cs = slice(half * 4, half * 4 + 4)
            regs32 = [k.rXT[c][j] for c in range(half * 4, half * 4 + 4)]
            regs16 = [k.rXTb[c][j] for c in range(half * 4, half * 4 + 4)]
            P.cp(k.XT.h[:, cs, i * 128:(i + 1) * 128], src, [ps], regs32, eng="dve")
            P.cp(k.XTb.h[:, cs, i * 128:(i + 1) * 128], src, [ps], regs16, eng="act")
    P.off = mark


def store_out(k):
    P = k.P
    mark = P.off
    P.off = k.up_off
    yo = [P.sb([128, D], F32, f"yo{i}") for i in range(2)]
    ident = k.c["ident"]
    for b in yo:
        b.dsem = P.new_dsem()
        k.out_dsems.append(b.dsem)
    for i in range(16):
        ob = yo[i % 2]
        j = i // 4
        for half in range(2):
            ps = k.PS[(2 * i + half) % 4]

            def fn(e, ps=ps, half=half, i=i):
                ins = None
                for q in range(4):
                    c = half * 4 + q
                    ins = e.transpose(ps.h[:, q * 128:(q + 1) * 128], k.XT.h[:, c, i * 128:(i + 1) * 128], ident.h[:])
                return ins
            P.op("pe", fn, [k.rXT[c][j] for c in range(half * 4, half * 4 + 4)] + [ident], [ps])
            P.cp(ob.h[:, half * 512:(half + 1) * 512], ps.h[:], [ps], [ob], eng=("dve" if half == 0 else "act"))
        P.dma(k.dram["y"][i * 128:(i + 1) * 128, :], ob.h[:], reads=[ob], dsem=ob.dsem)
    P.off = mark


PRM_SHAPES = {
    "retg": [64, 4, NL], "retb": [64, 4, NL],
    "qn": [128, 2, NL], "kvn": [128, NL],
    "s5d": [128, 2, NL],
    "lbraw": [64, 4, NL], "hgn": [64, 4, NL],
    "ln1g": [128, 8, NL], "ln1b": [128, 8, NL], "ln2g": [128, 8, NL], "ln2b": [128, 8, NL],
}


def host_params(inp):
    p = {}
    def hl(a):
        return np.ascontiguousarray(a.reshape(NL, 4, 64).transpose(2, 1, 0))
    p["retg"] = hl(inp["ret_gn_g"])
    p["retb"] = hl(inp["ret_gn_b"])
    qn = np.zeros((128, 2, NL), np.float32)
    qn[:, 0, :] = inp["mla_q_norm"][:, 0:128].T
    qn[0:64, 1, :] = inp["mla_q_norm"][:, 128:192].T
    p["qn"] = qn
    p["kvn"] = np.ascontiguousarray(inp["mla_kv_norm"].T)
    for nm in ("ln1_g", "ln1_b", "ln2_g", "ln2_b"):
        p[nm.replace("_", "")] = np.ascontiguousarray(inp[nm].reshape(NL, 8, 128).transpose(2, 1, 0))
    p["lbraw"] = hl(inp["hg_lb_raw"])
    p["hgn"] = hl(inp["hg_norm"])
    p["s5d"] = np.ascontiguousarray(inp["s5_d"].reshape(NL, 2, 128).transpose(2, 1, 0))
    return p


def interleave(gens):
    gens = list(gens)
    while gens:
        for g in list(gens):
            try:
                next(g)
            except StopIteration:
                gens.remove(g)


def wb_load(k, wb, l, c0, W, src="wx"):
    P = k.P
    ap = k.dram[src][l, :, c0:c0 + W].rearrange("(kk p) w -> p kk w", p=128)
    P.dma(wb.h[:, :, 0:W], ap, writes=[wb], dsem=wb.dsem, eng="pool", arena=False)


def xtb_regs(k, j):
    return [k.rXTb[kk][j] for kk in range(8)]


def inproj_fm(k, ps, M, wb, col, j):
    pairs = [(wb.h[:, kk, col:col + M], k.XTb.h[:, kk, j * 512:(j + 1) * 512]) for kk in range(8)]
    k.P.mm(ps.h[0:M, :], pairs, [wb] + xtb_regs(k, j), [ps])


def inproj_tm(k, ps, N, wb, col, i):
    pairs = [(k.XTb.h[:, kk, i * 128:(i + 1) * 128], wb.h[:, kk, col:col + N]) for kk in range(8)]
    k.P.mm(ps.h[:, 0:N], pairs, [wb] + xtb_regs(k, i // 4), [ps])


MIX_LOADS = {
    "ret": [(0, C_RQ, 256), (1, C_RK, 256), (2, C_RQP, 256), (3, C_RKP, 256)],
    "mla": [(0, C_CQ, 192), (1, C_CKV, 128), (2, C_KPE, 192)],
    "s5": [(0, C_U, 256)],
    "hg": [(0, C_HQ, 256), (1, C_HF, 256), (2, C_HI, 256), (3, C_HG, 256)],
}


def mixer_loads(k, l, name):
    if not k.cfg.get(name, 1):
        return
    for (i, c0, W) in MIX_LOADS[name]:
        wb_load(k, k.WB[i], l, c0, W)


def layer(k, l):
    P, cfg = k.P, k.cfg
    PS = k.PS
    if l == 0:
        for c in range(8):
            P.dma(k.spill[:, c, :], k.XT.h[:, c, :], reads=k.rXT[c], writes=[k.rSP[c]], dsem=k.ds_sp, arena=False)
    P.fence(k.ARENA, [r for row in k.rXT for r in row])
    P.off = k.up_off
    k.YT = [P.sb([128, 2, T], BF16, f"YT{m}") for m in range(4)]
    k.rYT = [[[Reg(f"YT{m}_{c}_{j}") for j in range(4)] for c in range(2)] for m in range(4)]
    if not hasattr(k, "WB"):
        k.WB = [P.sb([128, 8, 256], BF16, f"WB{i}") for i in range(4)]
        for b in k.WB:
            b.dsem = P.new_dsem()
        k.up2_off = P.off
        mixer_loads(k, l, "ret")
    elif not cfg.get("xpref", 0):
        mixer_loads(k, l, "ret")
    P.set_regions([(k.xt_off, k.up_off)])
    rope_tables(k)
    k.mix_regs = [(P.regions[0][0], k.up_off), (k.up2_off, SB_TOP)]
    if cfg.get("ret", 1):
        P.set_regions(k.mix_regs)
        P.phase = f"L{l}.ret"
        ret_mixer(k, l)
        tap(k, f"y_d{l}", k.YT[3].h[:], [128, 2, T], [r for cc in k.rYT[3] for r in cc], BF16)
    mixer_loads(k, l, "mla")
    if cfg.get("mla", 1):
        P.fence(k.ARENA)
        P.set_regions(k.mix_regs)
        P.phase = f"L{l}.mla"
        mla_mixer(k, l)
        tap(k, f"y_b{l}", k.YT[1].h[:], [128, 2, T], [r for cc in k.rYT[1] for r in cc], BF16)
    k.mix_regs2 = [(k.xt_off, k.up_off), (k.up2_off, SB_TOP)]
    mixer_loads(k, l, "s5")
    if cfg.get("s5", 1):
        P.fence(k.ARENA)
        P.set_regions(k.mix_regs2)
        P.phase = f"L{l}.s5"
        s5_mixer(k, l)
        tap(k, f"y_a{l}", k.YT[0].h[:], [128, 2, T], [r for cc in k.rYT[0] for r in cc], BF16)
    mixer_loads(k, l, "hg")
    if cfg.get("hg", 1):
        P.fence(k.ARENA)
        P.set_regions(k.mix_regs2)
        P.phase = f"L{l}.hg"
        hgrn_mixer(k, l)
        tap(k, f"y_c{l}", k.YT[2].h[:], [128, 2, T], [r for cc in k.rYT[2] for r in cc], BF16)
    P.set_regions(None)
    if cfg.get("gate", 1):
        P.fence(k.ARENA)
        P.phase = f"L{l}.gate"
        gate_phase(k, l)
        tap(k, f"x1{l}", k.XT.h[:], [128, 8, T], [r for row in k.rXT for r in row])
    if cfg.get("ffn", 1):
        P.fence(k.ARENA)
        P.phase = f"L{l}.ffn"
        ffn_phase(k, l)
        tap(k, f"x2{l}", k.XT.h[:], [128, 8, T], [r for row in k.rXT for r in row])
    if l + 1 < cfg.get("nl", 0) and cfg.get("xpref", 0):
        mixer_loads(k, l + 1, "ret")


def rope_tables(k):
    P, c = k.P, k.c
    k.COS = P.sb([128, T], F32, "COS")
    k.SIN = P.sb([128, T], F32, "SIN")
    save = [list(r) for r in P.regions]
    posi = P.sb([128, T], I32, "posi")
    if not hasattr(k, "ds_pos"):
        k.ds_pos = P.new_dsem()
    x0 = P.sb([128, T], F32, "rx0")
    x1 = P.sb([128, T], F32, "rx1")
    P.dma(posi.h[:], k.dram["pos"].broadcast_to([128, T]), writes=[posi], dsem=k.ds_pos)
    P.cp(x0.h[:], posi.h[:], [posi], [x0])
    rc = c["ropec"]
    P.ts(x0.h[:], x0.h[:], rc.h[:, 0:1], None, ALU.mult, None, [x0, rc], [x0])
    P.ts(x1.h[:], x0.h[:], MAGIC, MAGIC, ALU.add, ALU.subtract, [x0], [x1])
    P.tt(x1.h[:], x0.h[:], x1.h[:], ALU.subtract, [x0, x1], [x1])
    P.act(k.SIN.h[:], x1.h[:], AF.Sin, [x1, rc], [k.SIN], scale=rc.h[:, 1:2])
    P.ts(x0.h[:], x0.h[:], 0.25, None, ALU.add, None, [x0], [x0])
    P.ts(x1.h[:], x0.h[:], MAGIC, MAGIC, ALU.add, ALU.subtract, [x0, k.SIN], [x1])
    P.tt(x1.h[:], x0.h[:], x1.h[:], ALU.subtract, [x0, x1], [x1])
    P.act(k.COS.h[:], x1.h[:], AF.Sin, [x1], [k.COS], scale=TWO_PI)
    P.regions = save


def ln_feat(k, o_sb, npart, ones, ps_a, ps_b, tmp, out_fn):
    P = k.P
    sq, mean, rstd = tmp[2], tmp[0], tmp[1]
    P.tt(sq.h[0:npart, :], o_sb.h[0:npart, :], o_sb.h[0:npart, :], ALU.mult, [o_sb], [sq])
    P.mm(ps_a.h[0:npart, :], [(ones.h[0:npart, 0:npart], o_sb.h[0:npart, :])], [ones, o_sb], [ps_a])
    P.mm(ps_b.h[0:npart, :], [(ones.h[0:npart, 0:npart], sq.h[0:npart, :])], [ones, sq], [ps_b])
    P.cp(mean.h[0:npart, :], ps_a.h[0:npart, :], [ps_a], [mean], eng="act")
    P.tt(sq.h[0:npart, :], mean.h[0:npart, :], mean.h[0:npart, :], ALU.mult, [mean], [sq])
    P.tt(rstd.h[0:npart, :], ps_b.h[0:npart, :], sq.h[0:npart, :], ALU.subtract, [ps_b, sq], [rstd])
    P.act(rstd.h[0:npart, :], rstd.h[0:npart, :], AF.Ln, [rstd], [rstd], bias=EPS)
    P.act(rstd.h[0:npart, :], rstd.h[0:npart, :], AF.Exp, [rstd], [rstd], scale=-0.5)


def rms_bc(k, srcs, nfeat, ps, out_rstd):
    P = k.P
    pairs = []
    rd = [k.ones128]
    for (sq, nr) in srcs:
        pairs.append((k.ones128.h[0:nr, :], sq.h[0:nr, :]))
        rd.append(sq)
    P.mm(ps.h[:, :], pairs, rd, [ps])
    P.act(out_rstd.h[:], ps.h[:], AF.Ln, [ps], [out_rstd], scale=1.0 / nfeat, bias=EPS)
    P.act(out_rstd.h[:], out_rstd.h[:], AF.Exp, [out_rstd], [out_rstd], scale=-0.5)


def mla_mixer(k, l):
    P, PS, c = k.P, k.PS, k.c
    wcq, wckv, wkpe, _ = k.WB
    if not hasattr(k, "ds_mlaw"):
        k.ds_mlaw = P.new_dsem()
    uq32 = P.sb([128, 2, 384], F32, "uq32")
    uqr32 = P.sb([128, 2, 384], F32, "uqr32")
    ukv32 = P.sb([128, 512], F32, "ukv32")
    uq16 = P.sb([128, 2, 384], BF16, "uq16")
    uqr16 = P.sb([128, 2, 384], BF16, "uqr16")
    ukvK = P.sb([128, 256], BF16, "ukvK")
    ukvV = P.sb([128, 256], BF16, "ukvV")
    P.dma(uq32.h[:], k.dram["uq"][l], writes=[uq32], dsem=k.ds_mlaw)
    P.dma(uqr32.h[:], k.dram["uqr"][l], writes=[uqr32], dsem=k.ds_mlaw)
    P.dma(ukv32.h[:], k.dram["ukv"][l], writes=[ukv32], dsem=k.ds_mlaw)
    qn, kvn = k.prm["qn"], k.prm["kvn"]
    sc = 96.0 ** -0.5
    for cc in range(2):
        P.ts(uq16.h[:, cc, :], uq32.h[:, cc, :], qn.h[:, cc, l:l + 1], sc, ALU.mult, ALU.mult, [uq32, qn], [uq16])
        P.ts(uqr16.h[:, cc, :], uqr32.h[:, cc, :], qn.h[:, cc, l:l + 1], sc, ALU.mult, ALU.mult, [uqr32, qn], [uqr16])
    kv4 = ukv32.h[:].rearrange("p (h two d) -> p h two d", h=4, two=2)
    P.ts(ukvK.h[:].rearrange("p (h d) -> p h d", h=4), kv4[:, :, 0, :], kvn.h[:, l:l + 1], None, ALU.mult, None, [ukv32, kvn], [ukvK])
    P.ts(ukvV.h[:].rearrange("p (h d) -> p h d", h=4), kv4[:, :, 1, :], kvn.h[:, l:l + 1], None, ALU.mult, None, [ukv32, kvn], [ukvV])
    cqn = P.sb([128, 2, T], BF16, "cqn")
    ckvn = P.sb([128, T], BF16, "ckvn")
    KPE = P.sb([96, T], BF16, "KPE")
    Vtok = P.sb([128, 16, 256], BF16, "Vtok")
    QT = [P.sb([96, T], BF16, f"mQT{i}") for i in range(2)]
    KT = [P.sb([96, T], BF16, f"mKT{i}") for i in range(2)]
    A = [P.sb([128, 512], BF16, f"mA{i}") for i in range(3)]
    f0 = P.sb([128, 512], F32, "mf0")
    f1 = P.sb([128, 512], F32, "mf1")
    f2 = P.sb([128, 512], F32, "mf2")
    s0 = P.sb([128, 512], F32, "ms0")
    s1 = P.sb([128, 512], F32, "ms1")
    rs = P.sb([128, 512], F32, "mrs")
    ta = [P.sb([96, 512], F32, f"mta{i}") for i in range(2)]
    tb = [P.sb([96, 512], F32, f"mtb{i}") for i in range(2)]
    rec = [P.sb([64, 512], F32, f"mrec{i}") for i in range(2)]
    pend = [None]
    for j in range(4):
        js = slice(j * 512, (j + 1) * 512)
        inproj_fm(k, PS[4], 128, wcq, 0, j)
        inproj_fm(k, PS[5], 64, wcq, 128, j)
        P.cp(f0.h[:], PS[4].h[:], [PS[4]], [f0], eng="act")
        P.cp(f1.h[0:64, :], PS[5].h[0:64, :], [PS[5]], [f1], eng="act")
        P.tt(s0.h[:], f0.h[:], f0.h[:], ALU.mult, [f0], [s0])
        P.tt(s1.h[0:64, :], f1.h[0:64, :], f1.h[0:64, :], ALU.mult, [f1], [s1])
        rms_bc(k, [(s0, 128), (s1, 64)], 192.0, PS[6], rs)
        P.tt(cqn.h[:, 0, js], f0.h[:], rs.h[:], ALU.mult, [f0, rs], [cqn])
        P.tt(cqn.h[0:64, 1, js], f1.h[0:64, :], rs.h[0:64, :], ALU.mult, [f1, rs], [cqn])
        inproj_fm(k, PS[7], 128, wckv, 0, j)
        P.cp(f2.h[:], PS[7].h[:], [PS[7]], [f2], eng="act")
        P.tt(s0.h[:], f2.h[:], f2.h[:], ALU.mult, [f2], [s0])
        rms_bc(k, [(s0, 128)], 128.0, PS[6], rs)
        P.tt(ckvn.h[:, js], f2.h[:], rs.h[:], ALU.mult, [f2, rs], [ckvn])
        inproj_fm(k, PS[4], 96, wkpe, 0, j)
        inproj_fm(k, PS[5], 96, wkpe, 96, j)
        a1, a2 = ta[0], tb[0]
        P.tt(a1.h[64:96, :], PS[4].h[64:96, :], k.COS.h[64:96, js], ALU.mult, [PS[4], k.COS], [a1])
        P.tt(a2.h[64:96, :], PS[5].h[64:96, :], k.SIN.h[64:96, js], ALU.mult, [PS[5], k.SIN], [a2])
        P.tt(KPE.h[64:96, js], a1.h[64:96, :], a2.h[64:96, :], ALU.add, [a1, a2], [KPE])
    for i in range(16):
        ps = PS[4 + i % 2]
        P.mm(ps.h[:, 0:256], [(ckvn.h[:, i * 128:(i + 1) * 128], ukvV.h[:, :])], [ckvn, ukvV], [ps])
        P.cp(Vtok.h[:, i, :], ps.h[:, 0:256], [ps], [Vtok], eng="act")
    def gen(h):
        qt, kt = QT[h % 2], KT[h % 2]
        P.cp(kt.h[64:96, :], KPE.h[64:96, :], [KPE], [kt], eng="dve")
        for j in range(4):
            js = slice(j * 512, (j + 1) * 512)
            P.mm(PS[4].h[0:64, :], [(ukvK.h[:, h * 64:(h + 1) * 64], ckvn.h[:, js])], [ukvK, ckvn], [PS[4]])
            P.cp(kt.h[0:64, js], PS[4].h[0:64, :], [PS[4]], [kt], eng="act")
            yield
            hs = slice(h * 96, (h + 1) * 96)
            P.mm(PS[4].h[0:96, :], [(uq16.h[:, 0, hs], cqn.h[:, 0, js]), (uq16.h[0:64, 1, hs], cqn.h[0:64, 1, js])], [uq16, cqn], [PS[4]])
            P.mm(PS[5].h[0:96, :], [(uqr16.h[:, 0, hs], cqn.h[:, 0, js]), (uqr16.h[0:64, 1, hs], cqn.h[0:64, 1, js])], [uqr16, cqn], [PS[5]])
            a1, a2 = ta[j % 2], tb[j % 2]
            P.cp(qt.h[0:64, js], PS[4].h[0:64, :], [PS[4]], [qt], eng="act")
            P.tt(a1.h[64:96, :], PS[4].h[64:96, :], k.COS.h[64:96, js], ALU.mult, [PS[4], k.COS], [a1])
            P.tt(a2.h[64:96, :], PS[5].h[64:96, :], k.SIN.h[64:96, js], ALU.mult, [PS[5], k.SIN], [a2])
            yield
            P.tt(qt.h[64:96, js], a1.h[64:96, :], a2.h[64:96, :], ALU.add, [a1, a2], [qt])
            yield

    def attn(h):
        qt, kt = QT[h % 2], KT[h % 2]
        blocks = [(qb, kb) for qb in range(4) for kb in range(4 * qb + 4)]

        def emitS(i):
            qb, kb = blocks[i]
            S, a = PS[i % 2], A[i % 3]
            qs = slice(qb * 512, (qb + 1) * 512)
            P.mm(S.h[:, :], [(kt.h[:, kb * 128:(kb + 1) * 128], qt.h[:, qs])], [kt, qt], [S])
            P.act(a.h[:], S.h[:], AF.Exp, [S], [a])
            if kb >= 4 * qb:
                v = kb - 4 * qb
                P.tt(a.h[:], a.h[:], c["cmask16"].h[:, v, :], ALU.mult, [a, c["cmask16"]], [a])

        emitS(0)
        for i, (qb, kb) in enumerate(blocks):
            O, Dn = PS[2 + qb % 2], PS[6 + qb % 2]
            qs = slice(qb * 512, (qb + 1) * 512)
            nkb = 4 * qb + 4
            if i + 1 < len(blocks):
                emitS(i + 1)
            a = A[i % 3]
            st, sp = (kb == 0), (kb == nkb - 1)
            P.mm(O.h[0:64, :], [(Vtok.h[:, kb, h * 64:(h + 1) * 64], a.h[:])], [Vtok, a], acc=[O], start=st, stop=sp)
            P.mm(Dn.h[0:64, :], [(k.ones16.h[:, 0:64], a.h[:])], [k.ones16, a], acc=[Dn], start=st, stop=sp)
            if kb % 2 == 1 and kb != nkb - 1:
                yield
            if kb != nkb - 1:
                continue

            def post(O=O, Dn=Dn, qb=qb, qs=qs, h=h):
                r_ = rec[qb % 2]
                P.op("dve", lambda e, r_=r_, Dn=Dn: e.reciprocal(out=r_.h[:], in_=Dn.h[0:64, :]), [Dn], [r_])
                p0 = (h % 2) * 64
                P.tt(k.YT[1].h[p0:p0 + 64, h // 2, qs], O.h[0:64, :], r_.h[:], ALU.mult, [O, r_], [k.rYT[1][h // 2][qb]])
            if pend[0] is not None:
                pend[0]()
            pend[0] = post
            yield

    interleave([gen(0)])
    for h in range(4):
        gs = [attn(h)]
        if h + 1 < 4:
            gs.append(gen(h + 1))
        interleave(gs)
    if pend[0] is not None:
        pend[0]()


def frac_sincos(k, x, tmp, out_sin, out_cos, np_, reads):
    P = k.P
    r, f = tmp
    sl = slice(0, np_)
    P.ts(r.h[sl], x.h[sl], MAGIC, MAGIC, ALU.add, ALU.subtract, [x] + reads, [r])
    P.tt(f.h[sl], x.h[sl], r.h[sl], ALU.subtract, [x, r], [f])
    P.act(out_sin.h[sl], f.h[sl], AF.Sin, [f], [out_sin], scale=TWO_PI)
    P.ts(f.h[sl], x.h[sl], 0.25, None, ALU.add, None, [x, out_sin], [f])
    P.ts(r.h[sl], f.h[sl], MAGIC, MAGIC, ALU.add, ALU.subtract, [f], [r])
    P.tt(f.h[sl], f.h[sl], r.h[sl], ALU.subtract, [f, r], [f])
    P.act(out_cos.h[sl], f.h[sl], AF.Sin, [f], [out_cos], scale=TWO_PI)


def s5_mixer(k, l):
    P, PS, c = k.P, k.PS, k.c
    wu, wglu, wc1, wc2 = k.WB
    if not hasattr(k, "ds_s5"):
        k.ds_s5 = P.new_dsem()
    ds = k.ds_s5
    aB = P.sb([128, 2, 2, 64], F32, "s5aB")
    bT = P.sb([128, 2, 2, 64], F32, "s5bT")
    ldB = P.sb([128, 2], F32, "s5ldB")
    aC = P.sb([128, 2, 16], F32, "s5aC")
    ldC = P.sb([128, 16], F32, "s5ldC")
    C1 = P.sb([128, 16, 128], BF16, "s5C1")
    C2 = P.sb([128, 16, 128], BF16, "s5C2")
    G16 = P.sb([128, 2, 256], BF16, "s5glu")
    for buf, nm in ((aB, "s5aB"), (bT, "s5bT"), (ldB, "s5ldB"), (aC, "s5aC"), (ldC, "s5ldC")):
        P.dma(buf.h[:], k.dram[nm][l], writes=[buf], dsem=ds)
    P.dma(C1.h[:], k.dram["s5C1"][l], writes=[C1], dsem=wc1.dsem, eng="pool")
    P.dma(C2.h[:], k.dram["s5C2"][l], writes=[C2], dsem=wc2.dsem, eng="pool")
    P.dma(G16.h[:], k.dram["s5glu"][l].rearrange("(kk p) w -> p kk w", p=128), writes=[G16], dsem=wglu.dsem, eng="pool")
    sg = c["sgn"]
    P.ts(C1.h[:], C1.h[:], sg.h[:, 0:1], None, ALU.mult, None, [C1, sg], [C1])
    P.ts(C2.h[:], C2.h[:], -1.0, None, ALU.mult, None, [C2], [C2])
    def sm(name, shape=(128, 128)):
        return P.sb(list(shape), F32, name)
    dtB = sm("dtB", (128, 2))
    P.act(dtB.h[:], ldB.h[:], AF.Exp, [ldB], [dtB])
    bbr = P.sb([128, 2, 64], F32, "bbr")
    bbi = P.sb([128, 2, 64], F32, "bbi")
    t_ = [sm(f"s5t{i}", (128, 64)) for i in range(10)]
    for cc in range(2):
        lr, li = aB.h[:, cc, 0, :], aB.h[:, cc, 1, :]
        br, bi = bT.h[:, cc, 0, :], bT.h[:, cc, 1, :]
        dcol = dtB.h[:, cc:cc + 1]
        mag, tr, sn, cs, x0, x1, nre, nim, den, cre = t_
        P.act(mag.h[:], lr, AF.Exp, [aB, dtB], [mag], scale=dcol)
        P.ts(tr.h[:], li, dcol, 1.0 / TWO_PI, ALU.mult, ALU.mult, [aB, dtB], [tr])
        frac_sincos(k, tr, (x0, x1), sn, cs, 128, [])
        P.tt(nre.h[:], mag.h[:], cs.h[:], ALU.mult, [mag, cs], [nre])
        P.ts(nre.h[:], nre.h[:], -1.0, None, ALU.add, None, [nre], [nre])
        P.tt(nim.h[:], mag.h[:], sn.h[:], ALU.mult, [mag, sn], [nim])
        P.tt(den.h[:], lr, lr, ALU.mult, [aB], [den])
        P.tt(x0.h[:], li, li, ALU.mult, [aB, sn, cs], [x0])
        P.tt(den.h[:], den.h[:], x0.h[:], ALU.add, [den, x0], [den])
        P.op("dve", lambda e, den=den: e.reciprocal(out=den.h[:], in_=den.h[:]), [den], [den])
        P.tt(x0.h[:], nre.h[:], lr, ALU.mult, [nre, aB], [x0])
        P.tt(x1.h[:], nim.h[:], li, ALU.mult, [nim, aB], [x1])
        P.tt(cre.h[:], x0.h[:], x1.h[:], ALU.add, [x0, x1], [cre])
        P.tt(cre.h[:], cre.h[:], den.h[:], ALU.mult, [cre, den], [cre])
        P.tt(x0.h[:], nim.h[:], lr, ALU.mult, [nim, aB], [x0])
        P.tt(x1.h[:], nre.h[:], li, ALU.mult, [nre, aB], [x1])
        P.tt(x0.h[:], x0.h[:], x1.h[:], ALU.subtract, [x0, x1], [x0])
        P.tt(x0.h[:], x0.h[:], den.h[:], ALU.mult, [x0, den], [x0])
        P.tt(x1.h[:], cre.h[:], br, ALU.mult, [cre, bT], [x1])
        P.tt(mag.h[:], x0.h[:], bi, ALU.mult, [x0, bT], [mag])
        P.tt(bbr.h[:, cc, :], x1.h[:], mag.h[:], ALU.subtract, [x1, mag], [bbr])
        P.tt(x1.h[:], cre.h[:], bi, ALU.mult, [cre, bT], [x1])
        P.tt(mag.h[:], x0.h[:], br, ALU.mult, [x0, bT], [mag])
        P.tt(bbi.h[:, cc, :], x1.h[:], mag.h[:], ALU.add, [x1, mag], [bbi])
    LB = P.sb([128, 16, 128], BF16, "s5LB")
    LBs = P.sb([128, 16, 128], BF16, "s5LBs")
    gm = c["grpmask"]
    for g in range(16):
        cc, gi = g // 8, g % 8
        mcol = gm.h[:, gi:gi + 1]
        P.ts(LB.h[:, g, 0:64], bbr.h[:, cc, :], mcol, None, ALU.mult, None, [bbr, gm], [LB])
        P.ts(LB.h[:, g, 64:128], bbi.h[:, cc, :], mcol, None, ALU.mult, None, [bbi, gm], [LB])
        P.ts(LBs.h[:, g, 0:64], bbi.h[:, cc, :], mcol, None, ALU.mult, None, [bbi, gm], [LBs])
        P.ts(LBs.h[:, g, 64:128], bbr.h[:, cc, :], mcol, -1.0, ALU.mult, ALU.mult, [bbr, gm], [LBs])
    dtC = sm("dtC", (128, 16))
    rC = sm("rC", (128, 16))
    fC = sm("fC", (128, 16))
    fx = sm("fCx", (128, 16))
    P.act(dtC.h[:], ldC.h[:], AF.Exp, [ldC], [dtC])
    P.tt(rC.h[:], aC.h[:, 0, :], dtC.h[:], ALU.mult, [aC, dtC], [rC])
    P.act(rC.h[:], rC.h[:], AF.Exp, [rC], [rC])
    P.tt(fC.h[:], aC.h[:, 1, :], dtC.h[:], ALU.mult, [aC, dtC], [fC])
    P.ts(fC.h[:], fC.h[:], 1.0 / TWO_PI, None, ALU.mult, None, [fC], [fC])
    P.ts(fx.h[:], fC.h[:], MAGIC, MAGIC, ALU.add, ALU.subtract, [fC], [fx])
    P.tt(fC.h[:], fC.h[:], fx.h[:], ALU.subtract, [fC, fx], [fC])
    Dg = P.sb([128, 2, 128], BF16, "s5Dg")
    pd = k.prm["s5d"]
    for cc in range(2):
        P.ts(Dg.h[:, cc, :], c["ident"].h[:], pd.h[:, cc, l:l + 1], None, ALU.mult, None, [c["ident"], pd], [Dg])
    uT = P.sb([128, 2, T], BF16, "s5uT")
    gT = P.sb([128, 2, T], BF16, "s5gT")
    for cc in range(2):
        for j in range(4):
            ps = PS[4 + j % 2]
            inproj_fm(k, ps, 128, wu, cc * 128, j)
            P.cp(uT.h[:, cc, j * 512:(j + 1) * 512], ps.h[:], [ps], [uT], eng="act")
    onesf = sm("s5ones", (128, 512))
    P.op("dve", lambda e: e.memset(onesf.h[:], 1.0), [], [onesf])
    Rt = [sm(f"s5Rt{i}", (128, 512)) for i in range(2)]
    X = [sm(f"s5x{i}", (128, 512)) for i in range(2)]
    Xr = [sm(f"s5xr{i}", (128, 512)) for i in range(2)]
    Xf = [sm(f"s5xf{i}", (128, 512)) for i in range(2)]
    CO = [sm(f"s5co{i}", (128, 512)) for i in range(2)]
    SI = [sm(f"s5si{i}", (128, 512)) for i in range(2)]
    BT = [sm(f"s5bt{i}", (128, 512)) for i in range(2)]
    B2 = [sm(f"s5b2{i}", (128, 512)) for i in range(2)]
    ST = [sm(f"s5st{i}", (128, 512)) for i in range(2)]
    STS = [[ST[0], sm("s5st2", (128, 512))], [ST[1], sm("s5st3", (128, 512))]]
    Z1 = [P.sb([128, 512], BF16, f"s5z1{i}") for i in range(2)]
    Z2 = [P.sb([128, 512], BF16, f"s5z2{i}") for i in range(2)]
    def fsc(x, r, f, si, co, n=512):
        sl = slice(0, n)
        P.ts(r.h[:, sl], x.h[:, sl], MAGIC, MAGIC, ALU.add, ALU.subtract, [x], [r])
        P.tt(f.h[:, sl], x.h[:, sl], r.h[:, sl], ALU.subtract, [x, r], [f])
        P.act(si.h[:, sl], f.h[:, sl], AF.Sin, [f], [si], scale=TWO_PI)
        P.act(f.h[:, sl], x.h[:, sl], AF.Identity, [x, si], [f], bias=0.25)
        P.ts(r.h[:, sl], f.h[:, sl], MAGIC, MAGIC, ALU.add, ALU.subtract, [f], [r])
        P.tt(f.h[:, sl], f.h[:, sl], r.h[:, sl], ALU.subtract, [f, r], [f])
        P.act(co.h[:, sl], f.h[:, sl], AF.Sin, [f], [co], scale=TWO_PI)

    p512, c512, s512 = sm("s5p512", (128, 16)), sm("s5c512", (128, 16)), sm("s5s512", (128, 16))
    P.ts(p512.h[:], fC.h[:], 512.0, None, ALU.mult, None, [fC], [p512])
    fsc(p512, X[0], Xf[0], s512, c512, n=16)
    P.ts(s512.h[:], s512.h[:], c["sgn"].h[:, 0:1], None, ALU.mult, None, [s512, c["sgn"]], [s512])
    MROT = [sm(f"s5mrot{i}", (128, 128)) for i in range(2)]
    INI = [[sm(f"s5ini{b}{i}", (128, 1)) for i in range(2)] for b in range(2)]
    BTS = [[BT[0], CO[0]], [BT[1], CO[1]]]
    B2S = [[B2[0], SI[0]], [B2[1], SI[1]]]
    Z1S = [[Z1[0], P.sb([128, 512], BF16, "s5z1b")], [Z1[1], P.sb([128, 512], BF16, "s5z1c")]]
    Z2S = [[Z2[0], P.sb([128, 512], BF16, "s5z2b")], [Z2[1], P.sb([128, 512], BF16, "s5z2c")]]
    TABS = [sm(f"s5tabs{i}", (128, 512)) for i in range(2)]
    TABC = [sm(f"s5tabc{i}", (128, 512)) for i in range(2)]

    def lane(cc, gi, b):
        g = cc * 8 + gi
        rt, x, si, co, mrot = Rt[b], X[b], TABS[b], TABC[b], MROT[b]
        pa, pb = PS[4 + 2 * b], PS[5 + 2 * b]
        P.act(rt.h[:], onesf.h[:], AF.Copy, [onesf, rC], [rt], scale=rC.h[:, g:g + 1])
        P.act(x.h[:], c["iota512"].h[:], AF.Copy, [c["iota512"], fC], [x], scale=fC.h[:, g:g + 1])
        yield
        fsc(x, Xr[b], Xf[b], si, co)
        P.ts(mrot.h[:], c["ident"].h[:], c512.h[:, g:g + 1], None, ALU.mult, None, [c["ident"], c512], [mrot])
        yield
        P.stt(mrot.h[:], c["swapid"].h[:], s512.h[:, g:g + 1], mrot.h[:], ALU.mult, ALU.add, [c["swapid"], s512, mrot], [mrot])
        yield
        sts = {}

        def pre(j):
            js = slice(j * 512, (j + 1) * 512)
            bt, b2 = BTS[b][j % 2], B2S[b][j % 2]
            P.mm(pa.h[:, :], [(LB.h[:, g, :], uT.h[:, cc, js])], [LB, uT], [pa])
            P.mm(pb.h[:, :], [(LBs.h[:, g, :], uT.h[:, cc, js])], [LBs, uT], [pb])
            yield
            P.cp(bt.h[:], pa.h[:], [pa], [bt], eng="act")
            P.cp(b2.h[:], pb.h[:], [pb], [b2], eng="act")
            yield
            P.tt(bt.h[:], bt.h[:], co.h[:], ALU.mult, [bt, co], [bt])
            yield
            P.tt(b2.h[:], b2.h[:], si.h[:], ALU.mult, [b2, si], [b2])
            yield
            P.tt(bt.h[:], bt.h[:], b2.h[:], ALU.add, [bt, b2], [bt])
            yield

        def post(j):
            bt = BTS[b][j % 2]
            st = STS[b][j % 2]
            z1, z2 = Z1S[b][j % 2], Z2S[b][j % 2]
            if j == 0:
                P.scan(st.h[:], rt.h[:], bt.h[:], 0.0, [rt, bt], [st])
            else:
                prev = sts[j - 1]
                ini = INI[b][j % 2]
                P.mm(pa.h[:, 0:1], [(mrot.h[:, :], prev.h[:, 511:512])], [mrot, prev], [pa])
                P.cp(ini.h[:], pa.h[:, 0:1], [pa], [ini], eng="act")
                yield
                P.scan(st.h[:], rt.h[:], bt.h[:], ini.h[:, 0:1], [rt, bt, ini], [st])
            sts[j] = st
            yield
            P.tt(z1.h[:], st.h[:], co.h[:], ALU.mult, [st, co], [z1])
            P.tt(z2.h[:], st.h[:], si.h[:], ALU.mult, [st, si], [z2], eng="pool")
            yield
            P.mm(PS[j].h[:, :], [(C1.h[:, g, :], z1.h[:])], [C1, z1], acc=[PS[j]], start=(gi == 0), stop=False)
            P.mm(PS[j].h[:, :], [(C2.h[:, g, :], z2.h[:])], [C2, z2], acc=[PS[j]], start=False, stop=False)
            yield

        yield from pre(0)
        for j in range(4):
            if j + 1 < 4:
                yield from pre(j + 1)
            yield from post(j)

    for cc in range(2):
        for gp in range(4):
            interleave([lane(cc, 2 * gp, 0), lane(cc, 2 * gp + 1, 1)])
        for j in range(4):
            js = slice(j * 512, (j + 1) * 512)
            P.mm(PS[j].h[:, :], [(Dg.h[:, cc, :], uT.h[:, cc, js])], [Dg, uT], acc=[PS[j]], start=False, stop=True)
            y, t = BT[j % 2], B2[j % 2]
            P.cp(y.h[:], PS[j].h[:], [PS[j]], [y], eng="act")
            cg = math.sqrt(2.0 / math.pi)
            P.tt(t.h[:], y.h[:], y.h[:], ALU.mult, [y], [t])
            P.ts(t.h[:], t.h[:], 2.0 * cg * 0.044715, 2.0 * cg, ALU.mult, ALU.add, [t], [t])
            P.tt(t.h[:], t.h[:], y.h[:], ALU.mult, [t, y], [t])
            P.act(t.h[:], t.h[:], AF.Sigmoid, [t], [t])
            P.tt(gT.h[:, cc, js], t.h[:], y.h[:], ALU.mult, [t, y], [gT])
    for oc in range(2):
        for j in range(4):
            js = slice(j * 512, (j + 1) * 512)
            ps = PS[4 + j % 2]
            P.mm(ps.h[:, :], [(G16.h[:, kc, oc * 128:(oc + 1) * 128], gT.h[:, kc, js]) for kc in range(2)], [G16, gT], [ps])
            sgm = ST[j % 2]
            P.act(sgm.h[:], ps.h[:], AF.Sigmoid, [ps], [sgm])
            P.tt(k.YT[0].h[:, oc, js], gT.h[:, oc, js], sgm.h[:], ALU.mult, [gT, sgm], [k.rYT[0][oc][j]])


def hgrn_setup(k):
    P = k.P
    raw = k.prm["lbraw"]
    e = P.sb([64, 4, NL], F32, "lb_e")
    tot = P.sb([64, 4, 1], F32, "lb_tot")
    k.LBt = P.sb([64, 4, NL], F32, "LBt")
    k.OML = P.sb([64, 4, NL], F32, "OML")
    k.NOML = P.sb([64, 4, NL], F32, "NOML")
    P.act(e.h[:], raw.h[:], AF.Exp, [raw], [e])
    P.tt(tot.h[:, :, 0], e.h[:, :, 0], e.h[:, :, 1], ALU.add, [e], [tot])
    for l in range(2, NL):
        P.tt(tot.h[:, :, 0], tot.h[:, :, 0], e.h[:, :, l], ALU.add, [e, tot], [tot])
    P.op("dve", lambda ee: ee.reciprocal(out=tot.h[:, :, 0], in_=tot.h[:, :, 0]), [tot], [tot])
    P.op("dve", lambda ee: ee.memset(k.LBt.h[:, :, 0], 0.0), [], [k.LBt])
    for l in range(1, NL):
        if l == 1:
            P.cp(k.LBt.h[:, :, 1], e.h[:, :, 1], [e], [k.LBt])
        else:
            P.tt(k.LBt.h[:, :, l], k.LBt.h[:, :, l - 1], e.h[:, :, l], ALU.add, [e, k.LBt], [k.LBt])
    for l in range(1, NL):
        P.tt(k.LBt.h[:, :, l], k.LBt.h[:, :, l], tot.h[:, :, 0], ALU.mult, [k.LBt, tot], [k.LBt])
    P.ts(k.OML.h[:], k.LBt.h[:], -1.0, 1.0, ALU.mult, ALU.add, [k.LBt], [k.OML])
    P.ts(k.NOML.h[:], k.OML.h[:], -1.0, None, ALU.mult, None, [k.OML], [k.NOML])


def hgrn_mixer(k, l):
    P, PS, c = k.P, k.PS, k.c
    wq, wf, wi, wg = k.WB
    Vtok = P.sb([128, 16, 256], BF16, "hVtok")
    QT = P.sb([64, T], BF16, "hQT")
    KT = P.sb([64, T], BF16, "hKT")
    Khtok = P.sb([128, 16, 64], BF16, "hKhtok")
    US = P.sb([64, 64, 128], BF16, "hUS")
    dco = P.sb([64, 128], F32, "hdco")
    dco8 = P.sb([64, 8, 128], F32, "hdco8")

    def f(name):
        return P.sb([64, 512], F32, name)
    S1 = [[f(f"h{n}{i}") for n in ("A", "B", "C", "D", "E")] for i in range(4)]
    S4 = [[f(f"h{n}{i}") for n in ("osb", "sqo", "rstd", "sgs")] for i in range(2)]
    Vexp = [P.sb([128, 8, 64], BF16, f"hVexp{i}") for i in range(2)]
    A = [P.sb([128, 128], BF16, f"hA{i}") for i in range(2)]
    hgn = k.prm["hgn"]
    idn = c["ident"]
    for i in range(16):
        ps = PS[4 + i % 2]
        inproj_tm(k, ps, 256, wi, 0, i)
        P.cp(Vtok.h[:, i, :], ps.h[:, 0:256], [ps], [Vtok], eng="act")

    def ph1(h, j, b):
        A_, B_, C_, D_, E_ = S1[b]
        pf, pq = PS[2 * b], PS[2 * b + 1]
        pt = pf
        lbc, omlc, nomlc = k.LBt.h[:, h, l:l + 1], k.OML.h[:, h, l:l + 1], k.NOML.h[:, h, l:l + 1]
        prm_r = [k.LBt, k.OML, k.NOML]
        js = slice(j * 512, (j + 1) * 512)
        inproj_fm(k, pf, 64, wf, h * 64, j)
        inproj_fm(k, pq, 64, wq, h * 64, j)
        yield
        P.act(A_.h[:], pf.h[0:64, :], AF.Sigmoid, [pf], [A_])
        P.act(D_.h[:], pq.h[0:64, :], AF.Silu, [pq], [D_])
        yield
        P.act(B_.h[:], A_.h[:], AF.Ln, [A_] + prm_r, [B_], scale=omlc, bias=lbc)
        P.ts(C_.h[:], A_.h[:], nomlc, omlc, ALU.mult, ALU.add, [A_] + prm_r, [C_])
        yield
        P.scan(B_.h[:], c["rst16"].h[0:64, :], B_.h[:], 0.0, [c["rst16"], B_], [B_])
        yield
        P.act(A_.h[:], B_.h[:], AF.Exp, [B_, C_], [A_])
        b3 = B_.h[:].rearrange("p (n s) -> p n s", s=16)
        P.tt(E_.h[:].rearrange("p (n s) -> p n s", s=16), b3[:, :, 15:16].broadcast_to([64, 32, 16]), b3, ALU.subtract, [B_], [E_])
        yield
        P.tt(QT.h[:, js], D_.h[:], A_.h[:], ALU.mult, [D_, A_], [QT])
        P.act(E_.h[:], E_.h[:], AF.Exp, [E_], [E_])
        yield
        P.act(A_.h[:], B_.h[:], AF.Exp, [B_, QT], [A_], scale=-1.0)
        P.tt(E_.h[:], C_.h[:], E_.h[:], ALU.mult, [C_, E_], [E_])
        yield
        P.tt(KT.h[:, js], C_.h[:], A_.h[:], ALU.mult, [C_, A_], [KT])
        P.act(dco.h[:, j * 32:(j + 1) * 32], b3[:, :, 15], AF.Exp, [B_], [dco])

        def trf(e):
            ins = None
            for q in range(4):
                ins = e.transpose(pt.h[:, q * 64:(q + 1) * 64], E_.h[:, q * 128:(q + 1) * 128], idn.h[0:64, 0:64])
            return ins
        P.op("pe", trf, [E_, idn], [pt])
        yield
        P.cp(Khtok.h[:, j * 4:(j + 1) * 4, :], pt.h[:, 0:256].rearrange("p (q d) -> p q d", q=4), [pt], [Khtok], eng="act")
        yield

    def ph4(h, j, b):
        osb, sqo, rstd, sgs = S4[b]
        S, O, I, X = (PS[0], PS[2], PS[4], PS[6]) if b == 0 else (PS[1], PS[3], PS[5], PS[7])
        a = A[b]
        js = slice(j * 512, (j + 1) * 512)
        for q in range(4):
            i = j * 4 + q
            ts_ = slice(i * 128, (i + 1) * 128)
            P.mm(S.h[:, 0:128], [(KT.h[:, ts_], QT.h[:, ts_])], [KT, QT], [S])
            yield
            P.tt(a.h[:], S.h[:, 0:128], c["bdmask16"].h[:], ALU.mult, [S, c["bdmask16"]], [a])
            yield
            P.mm(O.h[0:64, q * 128:(q + 1) * 128], [(Vtok.h[:, i, h * 64:(h + 1) * 64], a.h[:])], [Vtok, a], acc=[O])
            yield
        if j == 0:
            P.op("dve", lambda e: e.memset(I.h[0:64, 0:16], 0.0), [], [], acc=[I])

        def interf(e):
            ins = None
            for nn in range(32):
                n = j * 32 + nn
                if n == 0:
                    continue
                ins = e.matmul(I.h[0:64, nn * 16:(nn + 1) * 16], US.h[:, :, n - 1], QT.h[:, n * 16:(n + 1) * 16], start=True, stop=True)
            return ins
        P.op("pe", interf, [US, QT], [], acc=[I])
        yield
        P.cp(osb.h[:], O.h[0:64, :], [O], [osb], eng="act")
        yield
        P.tt(osb.h[:], osb.h[:], I.h[0:64, :], ALU.add, [osb, I], [osb])
        yield
        P.tt(sqo.h[:], osb.h[:], osb.h[:], ALU.mult, [osb], [sqo])
        yield
        P.mm(X.h[0:64, :], [(k.ones64.h[:, :], sqo.h[:])], [k.ones64, sqo], [X])
        yield
        P.act(rstd.h[:], X.h[0:64, :], AF.Ln, [X], [rstd], bias=EPS)
        yield
        P.act(rstd.h[:], rstd.h[:], AF.Exp, [rstd], [rstd], scale=-0.5)
        inproj_fm(k, X, 64, wg, h * 64, j)
        yield
        P.stt(osb.h[:], osb.h[:], hgn.h[:, h, l:l + 1], rstd.h[:], ALU.mult, ALU.mult, [osb, hgn, rstd], [osb])
        P.act(sgs.h[:], X.h[0:64, :], AF.Silu, [X], [sgs])
        yield
        p0 = (h % 2) * 64
        P.tt(k.YT[2].h[p0:p0 + 64, h // 2, js], osb.h[:], sgs.h[:], ALU.mult, [osb, sgs], [k.rYT[2][h // 2][j]])
        yield

    for h in range(4):
        interleave([ph1(h, j, j) for j in range(4)])
        for i in range(16):
            ve = Vexp[i % 2]
            pu = PS[6 + i % 2]
            P.tt(ve.h[:], Vtok.h[:, i:i + 1, h * 64:(h + 1) * 64].broadcast_to([128, 8, 64]), c["cexp16"].h[:], ALU.mult,
                 [Vtok, c["cexp16"]], [ve])
            P.mm(pu.h[0:64, :], [(Khtok.h[:, i, :], ve.h[:].rearrange("p n v -> p (n v)"))], [Khtok, ve], [pu])
            P.cp(US.h[:, :, i * 8:(i + 1) * 8], pu.h[0:64, :].rearrange("p (n v) -> p v n", n=8), [pu], [US], eng="act")
        P.op("dve", lambda e: e.memset(dco.h[:, 0:1], 0.0), [], [dco])
        P.cp(dco8.h[:], dco.h[:, :].unsqueeze(1).broadcast_to([64, 8, 128]), [dco], [dco8], eng="dve")
        for v8 in range(8):
            vs = slice(v8 * 8, (v8 + 1) * 8)
            P.scan(US.h[:, vs, :].rearrange("p v n -> p (v n)"), dco8.h[:].rearrange("p v n -> p (v n)"),
                   US.h[:, vs, :].rearrange("p v n -> p (v n)"), 0.0, [dco8, US], [US])
        interleave([ph4(h, 0, 0), ph4(h, 1, 1)])
        interleave([ph4(h, 2, 0), ph4(h, 3, 1)])


def ln_chunks(k, j, src, src_regs, g_prm, b_prm, l, tmps, extra_reads=()):
    P, PS = k.P, k.PS
    js = slice(j * 512, (j + 1) * 512)
    sqa, sqb, mean, rstd, nmr = tmps
    Ps, Pq = PS[6], PS[7]
    for c in range(8):
        sq = (sqa, sqb)[c % 2]
        P.tt(sq.h[:], src(c), src(c), ALU.mult, [src_regs[c]] + list(extra_reads), [sq])
        P.mm(Ps.h[:, :], [(k.ones128.h[:, :], src(c))], [k.ones128, src_regs[c]], acc=[Ps], start=(c == 0), stop=(c == 7))
        P.mm(Pq.h[:, :], [(k.ones128.h[:, :], sq.h[:])], [k.ones128, sq], acc=[Pq], start=(c == 0), stop=(c == 7))
    P.act(mean.h[:], Ps.h[:], AF.Copy, [Ps], [mean], scale=1.0 / D)
    P.tt(sqa.h[:], mean.h[:], mean.h[:], ALU.mult, [mean], [sqa])
    P.stt(rstd.h[:], Pq.h[:], 1.0 / D, sqa.h[:], ALU.mult, ALU.subtract, [Pq, sqa], [rstd])
    P.act(rstd.h[:], rstd.h[:], AF.Ln, [rstd], [rstd], bias=EPS)
    P.act(rstd.h[:], rstd.h[:], AF.Exp, [rstd], [rstd], scale=-0.5)
    P.stt(nmr.h[:], mean.h[:], -1.0, rstd.h[:], ALU.mult, ALU.mult, [mean, rstd], [nmr])
    for c in range(8):
        t = (sqa, sqb)[c % 2]
        P.tt(t.h[:], src(c), rstd.h[:], ALU.mult, [src_regs[c], rstd], [t])
        P.tt(t.h[:], t.h[:], nmr.h[:], ALU.add, [t, nmr], [t])
        P.act(k.XT.h[:, c, js], t.h[:], AF.Identity, [t, g_prm, b_prm], [k.rXT[c][j]], scale=g_prm.h[:, c, l:l + 1],
              bias=b_prm.h[:, c, l:l + 1])
        P.act(k.XTb.h[:, c, js], k.XT.h[:, c, js], AF.Copy, [k.rXT[c][j]], [k.rXTb[c][j]])


def gate_phase(k, l):
    P, PS, cfg = k.P, k.PS, k.cfg
    if not hasattr(k, "ds_g"):
        k.ds_g = [P.new_dsem() for _ in range(4)]
    P.set_regions([(k.xt_off, k.up_off)])
    WBR = P.sb([128, 4, 2, D], BF16, "WBR")
    WG = [P.sb([128, 4, 8, 128], BF16, f"WG{i}") for i in range(2)]
    WBR.dsem = k.ds_g[0]
    WG[0].dsem, WG[1].dsem = k.ds_g[1], k.ds_g[2]
    ACC = [P.sb([128, 512], F32, f"ACC{i}") for i in range(2)]
    sgt = [P.sb([128, 512], F32, f"sgt{i}") for i in range(2)]
    tmp = [P.sb([128, 512], F32, f"gtmp{i}") for i in range(2)]
    P.set_regions([(k.up2_off, SB_TOP)])
    MIXT = P.sb([128, 8, T], BF16, "MIXT")
    rMIX = [[Reg(f"mix{c}_{j}") for j in range(4)] for c in range(8)]
    P.dma(WBR.h[:], k.dram["wbr"][l].rearrange("n (kc p) d -> p n kc d", p=128), writes=[WBR], dsem=WBR.dsem, eng="pool")
    it = 0
    for c in range(8):
        wgb = WG[c % 2]
        for n in range(4):
            col0 = n * 1024 + c * 128
            P.dma(wgb.h[:, n, :, :], k.dram["wg"][l, :, col0:col0 + 128].rearrange("(kk p) w -> p kk w", p=128), writes=[wgb],
                  dsem=wgb.dsem, eng="pool")
        for j in range(4):
            js = slice(j * 512, (j + 1) * 512)
            acc = ACC[j % 2]
            for n in range(4):
                pg, pb = PS[2 * (it % 2)], PS[2 * (it % 2) + 1]
                s_, t_ = sgt[it % 2], tmp[it % 2]
                it += 1
                P.mm(pg.h[:, :], [(wgb.h[:, n, kk, :], k.XTb.h[:, kk, js]) for kk in range(8)], [wgb] + xtb_regs(k, j), [pg])
                P.mm(pb.h[:, :], [(WBR.h[:, n, kc, c * 128:(c + 1) * 128], k.YT[n].h[:, kc, js]) for kc in range(2)],
                     [WBR, k.rYT[n][0][j], k.rYT[n][1][j]], [pb])
                P.act(s_.h[:], pg.h[:], AF.Sigmoid, [pg], [s_])
                if n == 0:
                    P.tt(acc.h[:], s_.h[:], pb.h[:], ALU.mult, [s_, pb], [acc])
                else:
                    P.tt(t_.h[:], s_.h[:], pb.h[:], ALU.mult, [s_, pb], [t_])
                    if n < 3:
                        P.tt(acc.h[:], acc.h[:], t_.h[:], ALU.add, [acc, t_], [acc])
                    else:
                        P.tt(MIXT.h[:, c, js], acc.h[:], t_.h[:], ALU.add, [acc, t_], [rMIX[c][j]])
    if cfg.get("taps"):
        tap(k, f"mix{l}", MIXT.h[:], [128, 8, T], [r for row in rMIX for r in row], BF16)
    P.fence(k.ARENA, [r for row in k.rXT for r in row])
    P.phase = f"L{l}.wo"
    P.set_regions([(k.up_off, k.up_off + 32768)])
    tmps = [P.sb([128, 512], F32, f"lnA{i}") for i in range(5)]
    g1, b1 = k.prm["ln1g"], k.prm["ln1b"]
    it = 0
    for j in range(4):
        js = slice(j * 512, (j + 1) * 512)
        P.dma(k.XT.h[:, :, js], k.spill[:, :, js], reads=k.rSP, writes=[k.rXT[c][j] for c in range(8)], dsem=k.ds_g[3])
        for oh in range(4):
            wob = k.WB[it % 4]
            it += 1
            P.dma(wob.h[:], k.dram["wo"][l, :, oh * 256:(oh + 1) * 256].rearrange("(kk p) w -> p kk w", p=128), writes=[wob], dsem=wob.dsem,
                  eng="pool")
            for o2 in range(2):
                oc = oh * 2 + o2
                po = PS[4 + oc % 2]
                P.mm(po.h[:, :], [(wob.h[:, kk, o2 * 128:(o2 + 1) * 128], MIXT.h[:, kk, js]) for kk in range(8)],
                     [wob] + [rMIX[kk][j] for kk in range(8)], [po])
                P.stt(k.XT.h[:, oc, js], k.XT.h[:, oc, js], float(ALPHA), po.h[:], ALU.mult, ALU.add, [k.rXT[oc][j], po], [k.rXT[oc][j]])
        if j > 0:
            jp = j - 1
            jps = slice(jp * 512, (jp + 1) * 512)
            ln_chunks(k, jp, lambda cc, jps=jps: k.XT.h[:, cc, jps], [k.rXT[cc][jp] for cc in range(8)], g1, b1, l, tmps)
    js = slice(3 * 512, 4 * 512)
    ln_chunks(k, 3, lambda cc, js=js: k.XT.h[:, cc, js], [k.rXT[cc][3] for cc in range(8)], g1, b1, l, tmps)
    P.set_regions(None)


def ffn_phase(k, l):
    P, PS, cfg, c = k.P, k.PS, k.cfg, k.c
    moe = (l % 2 == 1)
    li = l // 2
    if not hasattr(k, "ds_f"):
        k.ds_f = [P.new_dsem() for _ in range(8)]
    P.set_regions([(k.up_off, k.up_off + 32768), (k.up2_off, SB_TOP), (k.up_off + 32768, k.up2_off)])
    Wg = [P.sb([128, 8, 512], BF16, f"Wg{i}") for i in range(2)]
    Wu = [P.sb([128, 8, 512], BF16, f"Wu{i}") for i in range(2)]
    Wd = [P.sb([128, 4, D], BF16, f"Wd{i}") for i in range(2)]
    for i in range(2):
        Wg[i].dsem, Wu[i].dsem, Wd[i].dsem = k.ds_f[3 * i], k.ds_f[3 * i + 1], k.ds_f[3 * i + 2]
    H = [P.sb([128, 4, 512], BF16, f"H{i}") for i in range(2)]
    sgb = [P.sb([128, 512], F32, f"fsg{i}") for i in range(2)]
    tb = P.sb([128, 512], F32, "ftb")
    nexp = NE if moe else 1
    if moe:
        WT = P.sb([8, T], F32, "WT")
        WT16 = P.sb([64, T], BF16, "WT16")
        WR = P.sb([128, 8, 8], F32, "WR")
        LG = P.sb([128, 16, 8], F32, "LG")
        M8 = P.sb([128, 16, 8], F32, "M8")
        MK = P.sb([128, 16, 8], F32, "MK")
        EX = P.sb([128, 16, 8], F32, "EX")
        DN = P.sb([128, 16, 1], F32, "DN")
        P.dma(WR.h[:], k.dram["mor"][li].rearrange("(kk p) e -> p kk e", p=128), writes=[WR], dsem=k.ds_f[6])
        for i in range(16):
            ps = PS[6 + i % 2]
            P.mm(ps.h[:, 0:8], [(k.XT.h[:, kk, i * 128:(i + 1) * 128], WR.h[:, kk, :]) for kk in range(8)],
                 [WR] + [k.rXT[kk][i // 4] for kk in range(8)], [ps])
            P.cp(LG.h[:, i, :], ps.h[:, 0:8], [ps], [LG], eng="act")
            P.op("dve", lambda e, i=i: e.max(out=M8.h[:, i, :], in_=LG.h[:, i, :]), [LG], [M8])
        P.tt(MK.h[:], LG.h[:], M8.h[:, :, 1:2].broadcast_to([128, 16, 8]), ALU.is_ge, [LG, M8], [MK])
        P.tt(EX.h[:], LG.h[:], M8.h[:, :, 0:1].broadcast_to([128, 16, 8]), ALU.subtract, [LG, M8], [EX])
        P.act(EX.h[:], EX.h[:], AF.Exp, [EX], [EX])
        P.tt(EX.h[:], EX.h[:], MK.h[:], ALU.mult, [EX, MK], [EX])
        P.op("dve", lambda e: e.tensor_reduce(out=DN.h[:, :, 0], in_=EX.h[:], axis=mybir.AxisListType.X, op=ALU.add), [EX], [DN])
        P.op("dve", lambda e: e.reciprocal(out=DN.h[:], in_=DN.h[:]), [DN], [DN])
        P.tt(EX.h[:], EX.h[:], DN.h[:].broadcast_to([128, 16, 8]), ALU.mult, [EX, DN], [EX])
        for i4 in range(4):
            ps = PS[6 + i4 % 2]

            def trf(e, i4=i4, ps=ps):
                ins = None
                for q in range(4):
                    ins = e.transpose(ps.h[0:8, q * 128:(q + 1) * 128], EX.h[:, i4 * 4 + q, :], c["ident"].h[:, :])
                return ins
            P.op("pe", trf, [EX, c["ident"]], [ps])
            P.cp(WT.h[:, i4 * 512:(i4 + 1) * 512], ps.h[0:8, :], [ps], [WT], eng="act")
        P.op("dve", lambda e: e.memset(WT16.h[:], 0.0), [], [WT16])
        P.cp(WT16.h[0:8, :], WT.h[:, :], [WT], [WT16])
        P.tt(WT.h[:, :], WT.h[:, :], WT16.h[0:8, :], ALU.subtract, [WT, WT16], [WT])
        P.cp(WT16.h[32:40, :], WT.h[:, :], [WT], [WT16])
    for cc in range(8):
        for j in range(4):
            js = slice(j * 512, (j + 1) * 512)
            if (cc + j) % 2 == 0:
                P.act(k.XT.h[:, cc, js], k.XT.h[:, cc, js], AF.Copy, [k.rXT[cc][j]], [k.rXT[cc][j]], scale=float(ALPHA))
            else:
                P.ts(k.XT.h[:, cc, js], k.XT.h[:, cc, js], float(ALPHA), None, ALU.mult, None, [k.rXT[cc][j]], [k.rXT[cc][j]])
    it = 0
    pending = None
    for e_ in range(nexp):
        if moe:
            gsrc, usrc, dsrc = k.dram["mog"][li, e_], k.dram["mou"][li, e_], k.dram["mod"][li, e_]
        else:
            gsrc, usrc, dsrc = k.dram["ffg"][li], k.dram["ffu"][li], k.dram["ffd"][li]
        for fb in range(7):
            b = it % 2
            it += 1
            wg_, wu_, wd_ = Wg[b], Wu[b], Wd[b]
            fs = slice(fb * 512, (fb + 1) * 512)
            P.dma(wg_.h[:], gsrc[:, fs].rearrange("(kk p) w -> p kk w", p=128), writes=[wg_], dsem=wg_.dsem, eng="pool")
            P.dma(wu_.h[:], usrc[:, fs].rearrange("(kk p) w -> p kk w", p=128), writes=[wu_], dsem=wu_.dsem, eng="pool")
            P.dma(wd_.h[:], dsrc[fs, :].rearrange("(fc p) d -> p fc d", p=128), writes=[wd_], dsem=wd_.dsem, eng="pool")
            for j in range(4):
                js = slice(j * 512, (j + 1) * 512)
                hb = H[j % 2]
                if moe:
                    pw = PS[6 + j % 2]
                    P.mm(pw.h[:, :], [(c["sel16"].h[:, e_, :], WT16.h[:, js])], [c["sel16"], WT16], [pw])
                for fc in range(4):
                    pg, pu = PS[2 * (fc % 2)], PS[2 * (fc % 2) + 1]
                    xr = xtb_regs(k, j)
                    P.mm(pg.h[:, :], [(wg_.h[:, kk, fc * 128:(fc + 1) * 128], k.XTb.h[:, kk, js]) for kk in range(8)], [wg_] + xr, [pg])
                    P.mm(pu.h[:, :], [(wu_.h[:, kk, fc * 128:(fc + 1) * 128], k.XTb.h[:, kk, js]) for kk in range(8)], [wu_] + xr, [pu])
                    s_ = sgb[fc % 2]
                    P.act(s_.h[:], pg.h[:], AF.Silu, [pg], [s_])
                    if moe:
                        P.tt(tb.h[:], s_.h[:], pw.h[:], ALU.mult, [s_, pw], [tb])
                        P.tt(hb.h[:, fc, :], tb.h[:], pu.h[:], ALU.mult, [tb, pu], [hb])
                    else:
                        P.tt(hb.h[:, fc, :], s_.h[:], pu.h[:], ALU.mult, [s_, pu], [hb])
                    if fc == 1 and pending is not None:
                        pending()
                        pending = None

                def down(wd_=wd_, hb=hb, j=j, js=js):
                    for oc in range(8):
                        pd = PS[4 + oc % 2]
                        P.mm(pd.h[:, :], [(wd_.h[:, fc, oc * 128:(oc + 1) * 128], hb.h[:, fc, :]) for fc in range(4)], [wd_, hb], [pd])
                        P.tt(k.XT.h[:, oc, js], k.XT.h[:, oc, js], pd.h[:], ALU.add, [k.rXT[oc][j], pd], [k.rXT[oc][j]])
                pending = down
    if pending is not None:
        pending()
    if cfg.get("taps"):
        tap(k, f"fpre{l}", k.XT.h[:], [128, 8, T], [r for row in k.rXT for r in row])
    P.fence(k.ARENA)
    P.phase = f"L{l}.ple"
    P.set_regions([(k.up_off, k.up_off + 32768), (k.up2_off, SB_TOP)])
    pT = P.sb([128, 2, T], BF16, "pT")
    WPP = P.sb([128, 2, D], BF16, "WPP")
    WPP.dsem = k.ds_f[0]
    ptl = [P.sb([128, 256], F32, f"ptl{i}") for i in range(2)]
    ptl[0].dsem, ptl[1].dsem = k.ds_f[6], k.ds_f[7]
    sgp = [P.sb([128, 512], F32, f"psg{i}") for i in range(2)]
    tmps = [P.sb([128, 512], F32, f"lnB{i}") for i in range(5)]
    P.dma(WPP.h[:], k.dram["plp"][l].rearrange("(kc p) d -> p kc d", p=128), writes=[WPP], dsem=WPP.dsem, eng="pool")
    for i in range(16):
        pt = ptl[i % 2]
        P.dma(pt.h[:], k.dram["p"][l, i * 128:(i + 1) * 128, :], writes=[pt], dsem=pt.dsem)
        ps = PS[6 + i % 2]

        def trf(e, pt=pt, ps=ps):
            ins = None
            for q in range(2):
                ins = e.transpose(ps.h[:, q * 128:(q + 1) * 128], pt.h[:, q * 128:(q + 1) * 128], c["ident"].h[:, :])
            return ins
        P.op("pe", trf, [pt, c["ident"]], [ps])
        P.cp(pT.h[:, :, i * 128:(i + 1) * 128], ps.h[:, 0:256].rearrange("p (q t) -> p q t", q=2), [ps], [pT], eng="act")
    g2, b2 = k.prm["ln2g"], k.prm["ln2b"]
    spill_next = (l + 1 < cfg.get("nl", 0))
    it = 0
    for j in range(4):
        js = slice(j * 512, (j + 1) * 512)
        for oh in range(4):
            wb = k.WB[it % 4]
            it += 1
            P.dma(wb.h[:], k.dram["plg"][l, :, oh * 256:(oh + 1) * 256].rearrange("(kk p) w -> p kk w", p=128), writes=[wb], dsem=wb.dsem,
                  eng="pool")
            for o2 in range(2):
                oc = oh * 2 + o2
                pa, pb = PS[2 * (oc % 2)], PS[2 * (oc % 2) + 1]
                P.mm(pa.h[:, :], [(wb.h[:, kk, o2 * 128:(o2 + 1) * 128], k.XTb.h[:, kk, js]) for kk in range(8)], [wb] + xtb_regs(k, j), [pa])
                P.mm(pb.h[:, :], [(WPP.h[:, kc, oc * 128:(oc + 1) * 128], pT.h[:, kc, js]) for kc in range(2)], [WPP, pT], [pb])
                s_ = sgp[oc % 2]
                P.act(s_.h[:], pa.h[:], AF.Sigmoid, [pa], [s_])
                P.tt(s_.h[:], s_.h[:], pb.h[:], ALU.mult, [s_, pb], [s_])
                P.tt(k.XT.h[:, oc, js], k.XT.h[:, oc, js], s_.h[:], ALU.add, [k.rXT[oc][j], s_], [k.rXT[oc][j]])
        if j > 0:
            jp = j - 1
            jps = slice(jp * 512, (jp + 1) * 512)
            ln_chunks(k, jp, lambda cc, jps=jps: k.XT.h[:, cc, jps], [k.rXT[cc][jp] for cc in range(8)], g2, b2, l, tmps)
            if spill_next:
                P.dma(k.spill[:, :, jps], k.XT.h[:, :, jps], reads=[k.rXT[cc][jp] for cc in range(8)], writes=k.rSP, dsem=k.ds_sp,
                      arena=False)
    js = slice(3 * 512, 4 * 512)
    ln_chunks(k, 3, lambda cc, js=js: k.XT.h[:, cc, js], [k.rXT[cc][3] for cc in range(8)], g2, b2, l, tmps)
    if spill_next:
        P.dma(k.spill[:, :, js], k.XT.h[:, :, js], reads=[k.rXT[cc][3] for cc in range(8)], writes=k.rSP, dsem=k.ds_sp, arena=False)
    P.set_regions(None)


def ret_mixer(k, l):
    P, PS, c = k.P, k.PS, k.c
    gam = _gammas()
    wq, wk, wqp, wkp = k.WB
    if not hasattr(k, "wbx_dsems"):
        k.wbx_dsems = [P.new_dsem() for _ in range(2)]
    wv = P.sb([128, 8, 256], BF16, "WBv")
    wg = P.sb([128, 8, 256], BF16, "WBg")
    wv.dsem, wg.dsem = k.wbx_dsems
    wb_load(k, wv, l, C_RV, 256)
    wb_load(k, wg, l, C_RG, 256)
    Vtok = P.sb([128, 16, 256], BF16, "Vtok")
    QT = [P.sb([64, T], BF16, f"QT{i}") for i in range(2)]
    KT = [P.sb([64, T], BF16, f"KT{i}") for i in range(2)]
    t1 = [P.sb([64, 512], F32, f"rt1_{i}") for i in range(2)]
    t2 = [P.sb([64, 512], F32, f"rt2_{i}") for i in range(2)]
    A = [P.sb([128, 512], BF16, f"A{i}") for i in range(3)]
    osb = [P.sb([64, 512], F32, f"osb{i}") for i in range(2)]
    tmp = [P.sb([64, 512], F32, f"lnt{i}") for i in range(3)]
    sgs = P.sb([64, 512], F32, "sgs")
    prg, prb = k.prm["retg"], k.prm["retb"]
    pend = [None]
    for i in range(16):
        ps = PS[4 + i % 2]
        inproj_tm(k, ps, 256, wv, 0, i)
        P.cp(Vtok.h[:, i, :], ps.h[:, 0:256], [ps], [Vtok], eng="act")
    def gen(h):
        qt, kt = QT[h % 2], KT[h % 2]
        for j in range(4):
            js = slice(j * 512, (j + 1) * 512)
            for which, (wa, wb_, dst) in enumerate(((wq, wqp, qt), (wk, wkp, kt))):
                pa, pb = PS[4 + 2 * which], PS[5 + 2 * which]
                inproj_fm(k, pa, 64, wa, h * 64, j)
                inproj_fm(k, pb, 64, wb_, h * 64, j)
                a1, a2 = t1[which], t2[which]
                P.tt(a1.h[:], pa.h[0:64, :], k.COS.h[0:64, js], ALU.mult, [pa, k.COS], [a1])
                P.tt(a2.h[:], pb.h[0:64, :], k.SIN.h[0:64, js], ALU.mult, [pb, k.SIN], [a2])
                yield
                P.tt(a1.h[:], a1.h[:], a2.h[:], ALU.add, [a1, a2], [a1])
                if which == 0:
                    P.tt(dst.h[:, js], a1.h[:], c["gq"].h[:, h, :], ALU.mult, [a1, c["gq"]], [dst])
                else:
                    gkb = c["gk"].h[:, h:h + 1, :].broadcast_to([64, 4, 128])
                    P.tt(dst.h[:, js].rearrange("p (a b) -> p a b", a=4), a1.h[:].rearrange("p (a b) -> p a b", a=4), gkb,
                         ALU.mult, [a1, c["gk"]], [dst])
                yield

    def attn(h):
        qt, kt = QT[h % 2], KT[h % 2]
        blocks = [(qb, kb) for qb in range(4) for kb in range(4 * qb + 4)]

        def emitS(i):
            qb, kb = blocks[i]
            S, a = PS[i % 2], A[i % 3]
            qs = slice(qb * 512, (qb + 1) * 512)
            P.mm(S.h[:, :], [(kt.h[:, kb * 128:(kb + 1) * 128], qt.h[:, qs])], [kt, qt], [S])
            bf = (gam[h] ** (512 * qb - 128 * kb)) * (64.0 ** -0.5)
            if kb < 4 * qb:
                P.act(a.h[:], S.h[:], AF.Copy, [S], [a], scale=float(bf))
            else:
                v = kb - 4 * qb
                P.stt(a.h[:], S.h[:], float(bf), c["cmask16"].h[:, v, :], ALU.mult, ALU.mult, [S, c["cmask16"]], [a])

        emitS(0)
        for i, (qb, kb) in enumerate(blocks):
            O = PS[2 + qb % 2]
            qs = slice(qb * 512, (qb + 1) * 512)
            nkb = 4 * qb + 4
            if i + 1 < len(blocks):
                emitS(i + 1)
            P.mm(O.h[0:64, :], [(Vtok.h[:, kb, h * 64:(h + 1) * 64], A[i % 3].h[:])], [Vtok, A[i % 3]], acc=[O],
                 start=(kb == 0), stop=(kb == nkb - 1))
            if kb % 2 == 1 and kb != nkb - 1:
                yield
            if kb != nkb - 1:
                continue

            def post(O=O, qb=qb, qs=qs, h=h):
                ob = osb[qb % 2]
                P.cp(ob.h[:], O.h[0:64, :], [O], [ob], eng="act")
                ln_feat(k, ob, 64, k.ones64, PS[6], PS[7], tmp, None)
                mean, rstd = tmp[0], tmp[1]
                P.tt(ob.h[:], ob.h[:], mean.h[:], ALU.subtract, [ob, mean], [ob])
                P.stt(ob.h[:], ob.h[:], prg.h[:, h, l:l + 1], rstd.h[:], ALU.mult, ALU.mult, [ob, prg, rstd], [ob])
                P.ts(ob.h[:], ob.h[:], prb.h[:, h, l:l + 1], None, ALU.add, None, [ob, prb], [ob])
                pg = PS[4]
                inproj_fm(k, pg, 64, wg, h * 64, qb)
                P.act(sgs.h[:], pg.h[0:64, :], AF.Silu, [pg], [sgs])
                p0 = (h % 2) * 64
                P.tt(k.YT[3].h[p0:p0 + 64, h // 2, qs], ob.h[:], sgs.h[:], ALU.mult, [ob, sgs], [k.rYT[3][h // 2][qb]])
            if pend[0] is not None:
                pend[0]()
            pend[0] = post
            yield

    interleave([gen(0)])
    for h in range(4):
        gs = [attn(h)]
        if h + 1 < 4:
            gs.append(gen(h + 1))
        interleave(gs)
    if pend[0] is not None:
        pend[0]()


def host_shared(inp):
    sh = {}
    for nm, v in host_consts().items():
        sh["c_" + nm] = v
    for nm, v in host_params(inp).items():
        sh["p_" + nm] = v
    sh.update(host_weights(inp))
    return sh


def core_inputs(inp, b, shared):
    m = dict(shared)
    m["x"] = np.ascontiguousarray(inp["x"][b])
    m["pos"] = np.ascontiguousarray(inp["positions"][b:b + 1]).astype(np.int32)
    m["p"] = np.ascontiguousarray(inp["p"][:, b])
    return m


def kernel(**inputs):
    inp = {kk: np.asarray(v) for kk, v in inputs.items()}
    nc, k = build({"nl": NL})
    shared = host_shared(inp)
    in_maps = []
    for b in range(8):
        m = core_inputs(inp, b, shared)
        in_maps.append({kk: v for kk, v in m.items() if kk in k.dram})
    res = run_bass_kernel_spmd(nc, in_maps, core_ids=list(range(8)))
    out = np.stack([np.asarray(r["y"]) for r in res.results], axis=0)
    return out.astype(np.float32)
```

```python
import math
from contextlib import ExitStack
import numpy as np
import concourse.bass as bass
import concourse.mybir as mybir
from concourse.bass_utils import run_bass_kernel_spmd

F32 = mybir.dt.float32
BF16 = mybir.dt.bfloat16
I32 = mybir.dt.int32
AF = mybir.ActivationFunctionType
ALU = mybir.AluOpType

T = 2048
D = 1024
NL = 4
DFF = 3584
NE = 8
ALPHA = (2 * NL) ** 0.25
EPS = 1e-5
MAGIC = 12582912.0
TWO_PI = 2.0 * math.pi
SB_BASE = 16512
SB_TOP = 229344
NXC = 3328
C_U, C_CQ, C_CKV, C_KPE, C_KPER = 0, 256, 448, 576, 672
C_HQ, C_HF, C_HI, C_HG = 768, 1024, 1280, 1536
C_RQ, C_RK, C_RV, C_RG, C_RQP, C_RKP = 1792, 2048, 2304, 2560, 2816, 3072


class Reg:
    __slots__ = ("name", "last_w", "readers", "excl")

    def __init__(self, name=""):
        self.name = name
        self.last_w = None
        self.readers = []
        self.excl = False


class DSem:
    __slots__ = ("sem", "count")

    def __init__(self, sem):
        self.sem = sem
        self.count = 0


class Buf:
    def __init__(self, h, name):
        self.h = h
        self.reg = Reg(name)
        self.dsem = None

    def __getitem__(self, idx):
        return self.h[idx]


class Op:
    __slots__ = ("eng", "fn", "reads", "writes", "acc", "dsem", "idx", "signal", "waits", "mark", "phase", "ninst")

    def __init__(self, eng, fn, reads, writes, acc, dsem):
        self.eng = eng
        self.fn = fn
        self.reads = reads
        self.writes = writes
        self.acc = acc
        self.dsem = dsem
        self.signal = None
        self.waits = []
        self.mark = False


def _regs(lst):
    out = []
    for x in lst:
        if x is None:
            continue
        if isinstance(x, (list, tuple)):
            out.extend(_regs(x))
        elif isinstance(x, Reg):
            out.append(x)
        else:
            out.append(x.reg)
    return out


class Prog:
    ENGS = ("pe", "act", "dve", "pool", "sync")

    def __init__(self, nc):
        self.nc = nc
        self.ops = []
        self.es = ExitStack()
        self.n = 0
        self.off = SB_BASE
        self.extra_reads = []
        self.regions = None

    def sb(self, shape, dtype, name=None):
        self.n += 1
        name = name or f"t{self.n}"
        nbytes = int(np.prod(shape[1:])) * (2 if dtype == BF16 else 4)
        nbytes = (nbytes + 31) // 32 * 32
        if self.regions is None:
            assert self.off + nbytes <= SB_TOP, f"SBUF overflow allocating {name}"
            off = self.off
            self.off += nbytes
        else:
            for rg in self.regions:
                if rg[0] + nbytes <= rg[1]:
                    off = rg[0]
                    rg[0] += nbytes
                    break
            else:
                raise AssertionError(f"SBUF arena overflow allocating {name} ({nbytes} B): {self.regions}")
        h = self.nc.alloc_sbuf_tensor_at(f"{name}_{self.n}", list(shape), dtype, offset=off)
        return Buf(h, name)

    def set_regions(self, regs):
        self.regions = [list(r) for r in regs] if regs is not None else None

    def ps(self, name):
        h = self.es.enter_context(self.nc.psum_tensor(name, [128, 512], F32))
        b = Buf(h, name)
        b.reg.excl = True
        return b

    def new_dsem(self):
        self.n += 1
        return DSem(self.es.enter_context(self.nc.semaphore(f"ds{self.n}")))

    def op(self, eng, fn, reads=(), writes=(), acc=(), dsem=None, arena=True):
        rd = _regs(reads)
        if arena:
            rd = rd + self.extra_reads
        o = Op(eng, fn, rd, _regs(writes), _regs(acc), dsem)
        o.phase = getattr(self, "phase", "")
        o.ninst = 0
        o.idx = len(self.ops)
        self.ops.append(o)
        return o

    def dma(self, out, in_, reads=(), writes=(), dsem=None, eng="sync", arena=True, **kw):
        assert dsem is not None
        return self.op(eng, lambda e: e.dma_start(out=out, in_=in_, **kw), reads, writes, (), dsem, arena)

    def fence(self, reg, extra_writes=()):
        self.op("pool", lambda e: e.memset(self.fence_scratch[0:1, 0:1], 0.0), [], [reg, self.fence_scratch] + list(extra_writes),
                arena=False)

    def finalize(self, final_dsems=()):
        nc = self.nc
        ops = self.ops
        deps = [None] * len(ops)
        for o in ops:
            d = set()
            for r in o.reads:
                if r.last_w is not None:
                    d.add(r.last_w)
                if r.excl:
                    d.update(i for i in r.readers if ops[i].eng != o.eng)
            for w in o.writes:
                if w.last_w is not None:
                    d.add(w.last_w)
                d.update(w.readers)
            for w in o.acc:
                if w.last_w is not None and ops[w.last_w].eng != o.eng:
                    d.add(w.last_w)
                d.update(w.readers)
            d.discard(o.idx)
            deps[o.idx] = d
            for r in o.reads:
                r.readers.append(o.idx)
            for w in o.writes:
                w.last_w = o.idx
                w.readers = []
            for w in o.acc:
                w.last_w = o.idx
                w.readers = []
            for i in d:
                ops[i].mark = True
        esem = {en: self.es.enter_context(nc.semaphore("sem_" + en)) for en in ("pe", "act", "dve", "pool")}
        ecount = {en: 0 for en in esem}
        seen = {en: {} for en in self.ENGS}
        for o in ops:
            need = {}
            for i in deps[o.idx]:
                p = ops[i]
                if p.dsem is not None:
                    key, sem, val = id(p.dsem), p.dsem.sem, 16 * p.dsem.count
                else:
                    key, sem, val = p.eng, esem[p.eng], p.signal
                if key not in need or need[key][1] < val:
                    need[key] = (sem, val)
            sn = seen[o.eng]
            for key, (sem, val) in need.items():
                if sn.get(key, 0) >= val:
                    continue
                sn[key] = val
                o.waits.append((sem, val))
            if o.dsem is not None:
                o.dsem.count += 1
            elif o.mark:
                ecount[o.eng] += 1
                o.signal = ecount[o.eng]
        finals = [(d.sem, 16 * d.count) for d in final_dsems if d.count > 0]
        engmap = {"pe": "tensor", "act": "scalar", "dve": "vector", "pool": "gpsimd", "sync": "sync"}
        with nc.Block() as block:
            for en in self.ENGS:
                myops = [o for o in ops if o.eng == en]

                def body(e, myops=myops, en=en):
                    for o in myops:
                        for sem, val in o.waits:
                            e.wait_ge(sem, val)
                        n0 = nc.n_instructions()
                        ins = o.fn(e)
                        o.ninst = nc.n_instructions() - n0
                        if o.dsem is not None:
                            ins.then_inc(o.dsem.sem, 16)
                        elif o.mark:
                            ins.then_inc(esem[en], 1)
                    if en == "sync":
                        for sem, val in finals:
                            e.wait_ge(sem, val)
                getattr(block, engmap[en])(body)
        self.stats = {en: sum(1 for o in ops if o.eng == en) for en in self.ENGS}
        self.stats["marks"] = dict(ecount)
        self.es.close()

    def mm(self, out, pairs, reads, writes=(), acc=(), start=True, stop=True):
        pairs = list(pairs)

        def fn(e):
            n = len(pairs)
            ins = None
            for i, (l, r) in enumerate(pairs):
                ins = e.matmul(out, l, r, start=(start and i == 0), stop=(stop and i == n - 1))
            return ins
        return self.op("pe", fn, reads, writes, acc)

    def tr(self, out, in_, ident, reads, writes):
        return self.op("pe", lambda e: e.transpose(out, in_, ident), reads, writes)

    def act(self, out, in_, func, reads, writes, scale=None, bias=None, eng="act"):
        kw = {}
        if scale is not None:
            kw["scale"] = scale
        if bias is not None:
            kw["bias"] = bias
        return self.op(eng, lambda e: e.activation(out=out, in_=in_, func=func, **kw), reads, writes)

    def tt(self, out, in0, in1, op, reads, writes, eng="dve"):
        return self.op(eng, lambda e: e.tensor_tensor(out=out, in0=in0, in1=in1, op=op), reads, writes)

    def ts(self, out, in0, s1, s2, op0, op1, reads, writes, eng="dve"):
        if op1 is None:
            return self.op(eng, lambda e: e.tensor_scalar(out=out, in0=in0, scalar1=s1, scalar2=None, op0=op0), reads, writes)
        return self.op(eng, lambda e: e.tensor_scalar(out=out, in0=in0, scalar1=s1, scalar2=s2, op0=op0, op1=op1), reads, writes)

    def stt(self, out, in0, scalar, in1, op0, op1, reads, writes):
        return self.op("dve", lambda e: e.scalar_tensor_tensor(out=out, in0=in0, scalar=scalar, in1=in1, op0=op0, op1=op1), reads, writes)

    def cp(self, out, in_, reads, writes, eng="dve"):
        if eng == "act":
            return self.op("act", lambda e: e.activation(out=out, in_=in_, func=AF.Copy), reads, writes)
        return self.op(eng, lambda e: e.tensor_copy(out=out, in_=in_), reads, writes)

    def scan(self, out, d0, d1, init, reads, writes):
        return self.op("dve", lambda e: e.tensor_tensor_scan(out=out, data0=d0, data1=d1, initial=init, op0=ALU.mult, op1=ALU.add), reads, writes)


def _gammas():
    return [1.0 - 2.0 ** (-5.0 - h) for h in range(4)]


def host_consts():
    c = {}
    c["ident"] = np.eye(128, dtype=np.float32)
    s = np.arange(128)[:, None]
    t = np.arange(512)[None, :]
    c["cmask"] = np.stack([(t >= s + 128 * v).astype(np.float32) for v in range(4)], axis=1)
    ss = np.arange(128)[:, None]
    tt = np.arange(128)[None, :]
    c["bdmask"] = ((ss // 16 == tt // 16) & (ss <= tt)).astype(np.float32)
    n = np.arange(8)[None, :, None]
    c["cexp"] = np.broadcast_to((ss[:, :, None] // 16 == n), (128, 8, 64)).astype(np.float32).copy()
    c["iota512"] = np.broadcast_to(np.arange(512, dtype=np.float32)[None, :], (128, 512)).copy()
    c["rst16"] = np.broadcast_to((np.arange(512) % 16 != 0).astype(np.float32)[None, :], (128, 512)).copy()
    rc = np.zeros((128, 2), np.float32)
    for r in range(64):
        j = r % 32
        rc[r, 0] = (10000.0 ** (-j / 32.0)) / TWO_PI
        rc[r, 1] = -TWO_PI if r < 32 else TWO_PI
    for r in range(64, 96):
        j = (r - 64) % 16
        rc[r, 0] = (10000.0 ** (-j / 16.0)) / TWO_PI
        rc[r, 1] = -TWO_PI if r < 80 else TWO_PI
    c["ropec"] = rc
    g = _gammas()
    gq = np.zeros((64, 4, 512), np.float32)
    gk = np.zeros((64, 4, 128), np.float32)
    for h in range(4):
        gq[:, h, :] = (g[h] ** np.arange(512, dtype=np.float64))[None, :]
        gk[:, h, :] = (g[h] ** (-np.arange(128, dtype=np.float64)))[None, :]
    c["gq"] = gq
    c["gk"] = gk
    gm = np.zeros((128, 8), np.float32)
    for r in range(128):
        gm[r, r // 16] = 1.0
    c["grpmask"] = gm
    sg = np.ones((128, 2), np.float32)
    sg[64:, 0] = -1.0
    sg[:, 1] = -1.0
    c["sgn"] = sg
    sel = np.zeros((64, 8, 128), np.float32)
    for e in range(8):
        sel[e, e, :] = 1.0
        sel[32 + e, e, :] = 1.0
    c["sel"] = sel
    sw = np.zeros((128, 128), np.float32)
    for r in range(128):
        sw[r, (r + 64) % 128] = 1.0
    c["swapid"] = sw
    return c


def host_weights(inp):
    w = {}
    w_in = inp["w_in"]
    wx = np.zeros((NL, D, NXC), np.float32)
    wx[:, :, 0:576] = w_in[:, :, 0:576]
    kr = w_in[:, :, 576:608]
    wx[:, :, C_KPE + 64:C_KPE + 96] = kr
    wx[:, :, C_KPER + 64:C_KPER + 80] = kr[:, :, 16:32]
    wx[:, :, C_KPER + 80:C_KPER + 96] = kr[:, :, 0:16]
    wx[:, :, C_HQ:C_HQ + 1024] = w_in[:, :, 608:1632]
    wx[:, :, C_RQ:C_RQ + 1024] = w_in[:, :, 1632:2656]
    for (src, dst) in ((1632, C_RQP), (1888, C_RKP)):
        blk = w_in[:, :, src:src + 256].reshape(NL, D, 4, 2, 32)
        wx[:, :, dst:dst + 256] = blk[:, :, :, ::-1, :].reshape(NL, D, 256)
    w["wx"] = wx
    w["wg"] = np.ascontiguousarray(w_in[:, :, 2656:])
    uq = inp["mla_w_uq"]
    uqr = np.zeros_like(uq)
    for h in range(4):
        uqr[:, :, h * 96 + 64:h * 96 + 80] = uq[:, :, h * 96 + 80:h * 96 + 96]
        uqr[:, :, h * 96 + 80:h * 96 + 96] = uq[:, :, h * 96 + 64:h * 96 + 80]
    def k2(a):
        o = np.zeros((NL, 128, 2, a.shape[2]), np.float32)
        o[:, :, 0, :] = a[:, 0:128, :]
        o[:, 0:64, 1, :] = a[:, 128:192, :]
        return o
    w["uq"] = k2(uq)
    w["uqr"] = k2(uqr)
    w["ukv"] = inp["mla_w_ukv"]
    are, aim, ldt = inp["s5_a_re"], inp["s5_a_im"], inp["s5_log_dt"]
    bre, bim = inp["s5_b_re"], inp["s5_b_im"]
    cre, cim = inp["s5_c_re"], inp["s5_c_im"]
    aB = np.zeros((NL, 128, 2, 2, 64), np.float32)
    bT = np.zeros((NL, 128, 2, 2, 64), np.float32)
    ldB = np.zeros((NL, 128, 2), np.float32)
    for g in range(16):
        cc, gi = g // 8, g % 8
        rows = slice(gi * 16, gi * 16 + 16)
        aB[:, rows, cc, 0, :] = are[:, g, None, :]
        aB[:, rows, cc, 1, :] = aim[:, g, None, :]
        ldB[:, rows, cc] = ldt[:, g, None]
        bT[:, rows, cc, 0, :] = bre[:, g].transpose(0, 2, 1)
        bT[:, rows, cc, 1, :] = bim[:, g].transpose(0, 2, 1)
    w["s5aB"], w["s5bT"], w["s5ldB"] = aB, bT, ldB
    aC = np.zeros((NL, 128, 2, 16), np.float32)
    aC[:, 0:64, 0, :] = are.transpose(0, 2, 1)
    aC[:, 64:128, 0, :] = are.transpose(0, 2, 1)
    aC[:, 0:64, 1, :] = aim.transpose(0, 2, 1)
    aC[:, 64:128, 1, :] = aim.transpose(0, 2, 1)
    w["s5aC"] = aC
    w["s5ldC"] = np.ascontiguousarray(np.broadcast_to(ldt[:, None, :], (NL, 128, 16)))
    C1 = np.zeros((NL, 128, 16, 128), np.float32)
    C2 = np.zeros((NL, 128, 16, 128), np.float32)
    for g in range(16):
        gi = g % 8
        cols = slice(gi * 16, gi * 16 + 16)
        C1[:, 0:64, g, cols] = cre[:, g].transpose(0, 2, 1)
        C1[:, 64:128, g, cols] = cim[:, g].transpose(0, 2, 1)
        C2[:, 0:64, g, cols] = cim[:, g].transpose(0, 2, 1)
        C2[:, 64:128, g, cols] = cre[:, g].transpose(0, 2, 1)
    w["s5C1"], w["s5C2"] = C1, C2
    w["s5glu"] = inp["s5_w_glu"]
    w["wbr"] = inp["w_branch"]
    w["wo"] = inp["w_o"]
    w["ffg"], w["ffu"], w["ffd"] = inp["ff_w_gate"], inp["ff_w_up"], inp["ff_w_down"]
    w["mog"], w["mou"], w["mod"], w["mor"] = inp["moe_w_gate"], inp["moe_w_up"], inp["moe_w_down"], inp["moe_router"]
    w["plg"], w["plp"] = inp["ple_w_gate"], inp["ple_w_proj"]
    return w


class K:
    pass


def build(cfg):
    nc = bass.Bass("TRN2", target_bir_lowering=False)
    P = Prog(nc)
    k = K()
    k.P, k.nc, k.cfg = P, nc, cfg
    k.dram = {}

    def din(name, shape, dtype=F32):
        k.dram[name] = nc.dram_tensor(name, list(shape), dtype, kind="ExternalInput").ap()
        return k.dram[name]

    def dout(name, shape, dtype=F32):
        k.dram[name] = nc.dram_tensor(name, list(shape), dtype, kind="ExternalOutput").ap()
        return k.dram[name]

    din("x", [T, D])
    din("pos", [1, T], I32)
    din("wx", [NL, D, NXC])
    din("uq", [NL, 128, 2, 384])
    din("uqr", [NL, 128, 2, 384])
    din("ukv", [NL, 128, 512])
    din("s5aB", [NL, 128, 2, 2, 64])
    din("s5bT", [NL, 128, 2, 2, 64])
    din("s5ldB", [NL, 128, 2])
    din("s5aC", [NL, 128, 2, 16])
    din("s5ldC", [NL, 128, 16])
    din("s5C1", [NL, 128, 16, 128])
    din("s5C2", [NL, 128, 16, 128])
    din("s5glu", [NL, 256, 256])
    din("wg", [NL, D, 4096])
    din("wbr", [NL, 4, 256, D])
    din("wo", [NL, D, D])
    din("ffg", [2, D, DFF]); din("ffu", [2, D, DFF]); din("ffd", [2, DFF, D])
    din("mog", [2, NE, D, DFF]); din("mou", [2, NE, D, DFF]); din("mod", [2, NE, DFF, D]); din("mor", [2, D, NE])
    din("plg", [NL, D, D]); din("plp", [NL, 256, D])
    din("p", [NL, T, 256])
    hc = host_consts()
    for nm, v in hc.items():
        din("c_" + nm, v.shape)
    dout("y", [T, D])
    k.taps = {}
    k.out_dsems = []

    k.XTb = P.sb([128, 8, T], BF16, "XTb")
    k.rXT = [[Reg(f"XT{c}_{j}") for j in range(4)] for c in range(8)]
    k.rXTb = [[Reg(f"XTb{c}_{j}") for j in range(4)] for c in range(8)]
    P.fence_scratch = P.sb([128, 8], F32, "fsc")
    k.ARENA = Reg("arena")
    P.extra_reads = [k.ARENA]
    k.PS = [P.ps(f"ps{i}") for i in range(8)]
    k.c = {}
    ds_c = P.new_dsem()
    ds_c2 = P.new_dsem()
    for nm, v in hc.items():
        if nm in ("cmask", "bdmask", "cexp", "sel"):
            b16 = P.sb(list(v.shape), BF16, nm + "16")
            P.dma(b16.h[:], k.dram["c_" + nm], writes=[b16], dsem=ds_c2, eng="pool", arena=False)
            k.c[nm + "16"] = b16
        else:
            k.c[nm] = P.sb(list(v.shape), F32, "c_" + nm)
            P.dma(k.c[nm].h[:], k.dram["c_" + nm], writes=[k.c[nm]], dsem=ds_c, arena=False)
    k.ones64 = P.sb([64, 64], F32, "ones64")
    P.op("dve", lambda e: e.memset(k.ones64.h[:], 1.0 / 64.0), [], [k.ones64])
    k.ones128 = P.sb([128, 128], F32, "ones128")
    P.op("dve", lambda e: e.memset(k.ones128.h[:], 1.0), [], [k.ones128])
    k.ones16 = P.sb([128, 128], BF16, "ones16")
    P.op("dve", lambda e: e.memset(k.ones16.h[:], 1.0), [], [k.ones16])
    k.prm = {}
    for nm, shp in PRM_SHAPES.items():
        din("p_" + nm, shp)
        k.prm[nm] = P.sb(list(shp), F32, "p_" + nm)
        P.dma(k.prm[nm].h[:], k.dram["p_" + nm], writes=[k.prm[nm]], dsem=ds_c, arena=False)
    hgrn_setup(k)
    k.xt_off = P.off
    k.XT = P.sb([128, 8, T], F32, "XT")
    k.up_off = P.off
    k.spill = nc.dram_tensor("xt_spill", [128, 8, T], F32, kind="Internal").ap()
    k.rSP = [Reg(f"sp{c}") for c in range(8)]
    k.ds_sp = P.new_dsem()
    k.ds_tap = P.new_dsem()
    k.out_dsems.append(k.ds_tap)

    if not cfg.get("skip_x"):
        load_x(k)
    nl = cfg.get("nl", 0)
    for l in range(nl):
        layer(k, l)
    if not cfg.get("skip_x") and not cfg.get("skip_store"):
        store_out(k)
    P.finalize(final_dsems=k.out_dsems)
    return nc, k


def tap(k, name, buf_ap, shape, reads, dtype=F32):
    P = k.P
    if not k.cfg.get("taps"):
        return
    d = k.nc.dram_tensor("tap_" + name, list(shape), dtype, kind="ExternalOutput").ap()
    P.dma(d, buf_ap, reads=reads, dsem=k.ds_tap)
    k.taps[name] = d


def load_x(k):
    P = k.P
    mark = P.off
    P.off = k.up_off
    xin = [P.sb([128, D], F32, f"xin{i}") for i in range(2)]
    for b in xin:
        b.dsem = P.new_dsem()
    ident = k.c["ident"]
    for i in range(16):
        xb = xin[i % 2]
        P.dma(xb.h[:], k.dram["x"][i * 128:(i + 1) * 128, :], writes=[xb], dsem=xb.dsem)
        j = i // 4
        for half in range(2):
            ps = k.PS[(2 * i + half) % 4]
            def fn(e, ps=ps, xb=xb, half=half):
                ins = None
                for q in range(4):
                    c = half * 4 + q
                    ins = e.transpose(ps.h[:, q * 128:(q + 1) * 128], xb.h[:, c * 128:(c + 1) * 128], ident.h[:])
                return ins
            P.op("pe", fn, [xb, ident], [ps])
            src = ps.h[:].rearrange("p (c t) -> p c t", c=4)
            cs = slice(half * 4, half * 4 + 4)
            regs32 = [k.rXT[c][j] for c in range(half * 4, half * 4 + 4)]
            regs16 = [k.rXTb[c][j] for c in range(half * 4, half * 4 + 4)]
            P.cp(k.XT.h[:, cs, i * 128:(i + 1) * 128], src, [ps], regs32, eng="dve")
            P.cp(k.XTb.h[:, cs, i * 128:(i + 1) * 128], src, [ps], regs16, eng="act")
    P.off = mark


def store_out(k):
    P = k.P
    mark = P.off
    P.off = k.up_off
    yo = [P.sb([128, D], F32, f"yo{i}") for i in range(2)]
    ident = k.c["ident"]
    for b in yo:
        b.dsem = P.new_dsem()
        k.out_dsems.append(b.dsem)
    for i in range(16):
        ob = yo[i % 2]
        j = i // 4
        for half in range(2):
            ps = k.PS[(2 * i + half) % 4]

            def fn(e, ps=ps, half=half, i=i):
                ins = None
                for q in range(4):
                    c = half * 4 + q
                    ins = e.transpose(ps.h[:, q * 128:(q + 1) * 128], k.XT.h[:, c, i * 128:(i + 1) * 128], ident.h[:])
                return ins
            P.op("pe", fn, [k.rXT[c][j] for c in range(half * 4, half * 4 + 4)] + [ident], [ps])
            P.cp(ob.h[:, half * 512:(half + 1) * 512], ps.h[:], [ps], [ob], eng=("dve" if half == 0 else "act"))
        P.dma(k.dram["y"][i * 128:(i + 1) * 128, :], ob.h[:], reads=[ob], dsem=ob.dsem)
    P.off = mark


PRM_SHAPES = {
    "retg": [64, 4, NL], "retb": [64, 4, NL],
    "qn": [128, 2, NL], "kvn": [128, NL],
    "s5d": [128, 2, NL],
    "lbraw": [64, 4, NL], "hgn": [64, 4, NL],
    "ln1g": [128, 8, NL], "ln1b": [128, 8, NL], "ln2g": [128, 8, NL], "ln2b": [128, 8, NL],
}


def host_params(inp):
    p = {}
    def hl(a):
        return np.ascontiguousarray(a.reshape(NL, 4, 64).transpose(2, 1, 0))
    p["retg"] = hl(inp["ret_gn_g"])
    p["retb"] = hl(inp["ret_gn_b"])
    qn = np.zeros((128, 2, NL), np.float32)
    qn[:, 0, :] = inp["mla_q_norm"][:, 0:128].T
    qn[0:64, 1, :] = inp["mla_q_norm"][:, 128:192].T
    p["qn"] = qn
    p["kvn"] = np.ascontiguousarray(inp["mla_kv_norm"].T)
    for nm in ("ln1_g", "ln1_b", "ln2_g", "ln2_b"):
        p[nm.replace("_", "")] = np.ascontiguousarray(inp[nm].reshape(NL, 8, 128).transpose(2, 1, 0))
    p["lbraw"] = hl(inp["hg_lb_raw"])
    p["hgn"] = hl(inp["hg_norm"])
    p["s5d"] = np.ascontiguousarray(inp["s5_d"].reshape(NL, 2, 128).transpose(2, 1, 0))
    return p


def interleave(gens):
    gens = list(gens)
    while gens:
        for g in list(gens):
            try:
                next(g)
            except StopIteration:
                gens.remove(g)


def wb_load(k, wb, l, c0, W, src="wx"):
    P = k.P
    ap = k.dram[src][l, :, c0:c0 + W].rearrange("(kk p) w -> p kk w", p=128)
    P.dma(wb.h[:, :, 0:W], ap, writes=[wb], dsem=wb.dsem, eng="pool", arena=False)


def xtb_regs(k, j):
    return [k.rXTb[kk][j] for kk in range(8)]


def inproj_fm(k, ps, M, wb, col, j):
    pairs = [(wb.h[:, kk, col:col + M], k.XTb.h[:, kk, j * 512:(j + 1) * 512]) for kk in range(8)]
    k.P.mm(ps.h[0:M, :], pairs, [wb] + xtb_regs(k, j), [ps])


def inproj_tm(k, ps, N, wb, col, i):
    pairs = [(k.XTb.h[:, kk, i * 128:(i + 1) * 128], wb.h[:, kk, col:col + N]) for kk in range(8)]
    k.P.mm(ps.h[:, 0:N], pairs, [wb] + xtb_regs(k, i // 4), [ps])


MIX_LOADS = {
    "ret": [(0, C_RQ, 256), (1, C_RK, 256), (2, C_RQP, 256), (3, C_RKP, 256)],
    "mla": [(0, C_CQ, 192), (1, C_CKV, 128), (2, C_KPE, 192)],
    "s5": [(0, C_U, 256)],
    "hg": [(0, C_HQ, 256), (1, C_HF, 256), (2, C_HI, 256), (3, C_HG, 256)],
}


def mixer_loads(k, l, name):
    if not k.cfg.get(name, 1):
        return
    for (i, c0, W) in MIX_LOADS[name]:
        wb_load(k, k.WB[i], l, c0, W)


def layer(k, l):
    P, cfg = k.P, k.cfg
    PS = k.PS
    if l == 0:
        for c in range(8):
            P.dma(k.spill[:, c, :], k.XT.h[:, c, :], reads=k.rXT[c], writes=[k.rSP[c]], dsem=k.ds_sp, arena=False)
    P.fence(k.ARENA, [r for row in k.rXT for r in row])
    P.off = k.up_off
    k.YT = [P.sb([128, 2, T], BF16, f"YT{m}") for m in range(4)]
    k.rYT = [[[Reg(f"YT{m}_{c}_{j}") for j in range(4)] for c in range(2)] for m in range(4)]
    if not hasattr(k, "WB"):
        k.WB = [P.sb([128, 8, 256], BF16, f"WB{i}") for i in range(4)]
        for b in k.WB:
            b.dsem = P.new_dsem()
        k.up2_off = P.off
        mixer_loads(k, l, "ret")
    elif not cfg.get("xpref", 0):
        mixer_loads(k, l, "ret")
    P.set_regions([(k.xt_off, k.up_off)])
    rope_tables(k)
    k.mix_regs = [(P.regions[0][0], k.up_off), (k.up2_off, SB_TOP)]
    if cfg.get("ret", 1):
        P.set_regions(k.mix_regs)
        P.phase = f"L{l}.ret"
        ret_mixer(k, l)
        tap(k, f"y_d{l}", k.YT[3].h[:], [128, 2, T], [r for cc in k.rYT[3] for r in cc], BF16)
    mixer_loads(k, l, "mla")
    if cfg.get("mla", 1):
        P.fence(k.ARENA)
        P.set_regions(k.mix_regs)
        P.phase = f"L{l}.mla"
        mla_mixer(k, l)
        tap(k, f"y_b{l}", k.YT[1].h[:], [128, 2, T], [r for cc in k.rYT[1] for r in cc], BF16)
    k.mix_regs2 = [(k.xt_off, k.up_off), (k.up2_off, SB_TOP)]
    mixer_loads(k, l, "s5")
    if cfg.get("s5", 1):
        P.fence(k.ARENA)
        P.set_regions(k.mix_regs2)
        P.phase = f"L{l}.s5"
        s5_mixer(k, l)
        tap(k, f"y_a{l}", k.YT[0].h[:], [128, 2, T], [r for cc in k.rYT[0] for r in cc], BF16)
    mixer_loads(k, l, "hg")
    if cfg.get("hg", 1):
        P.fence(k.ARENA)
        P.set_regions(k.mix_regs2)
        P.phase = f"L{l}.hg"
        hgrn_mixer(k, l)
        tap(k, f"y_c{l}", k.YT[2].h[:], [128, 2, T], [r for cc in k.rYT[2] for r in cc], BF16)
    P.set_regions(None)
    if cfg.get("gate", 1):
        P.fence(k.ARENA)
        P.phase = f"L{l}.gate"
        gate_phase(k, l)
        tap(k, f"x1{l}", k.XT.h[:], [128, 8, T], [r for row in k.rXT for r in row])
    if cfg.get("ffn", 1):
        P.fence(k.ARENA)
        P.phase = f"L{l}.ffn"
        ffn_phase(k, l)
        tap(k, f"x2{l}", k.XT.h[:], [128, 8, T], [r for row in k.rXT for r in row])
    if l + 1 < cfg.get("nl", 0) and cfg.get("xpref", 0):
        mixer_loads(k, l + 1, "ret")


def rope_tables(k):
    P, c = k.P, k.c
    k.COS = P.sb([128, T], F32, "COS")
    k.SIN = P.sb([128, T], F32, "SIN")
    save = [list(r) for r in P.regions]
    posi = P.sb([128, T], I32, "posi")
    if not hasattr(k, "ds_pos"):
        k.ds_pos = P.new_dsem()
    x0 = P.sb([128, T], F32, "rx0")
    x1 = P.sb([128, T], F32, "rx1")
    P.dma(posi.h[:], k.dram["pos"].broadcast_to([128, T]), writes=[posi], dsem=k.ds_pos)
    P.cp(x0.h[:], posi.h[:], [posi], [x0])
    rc = c["ropec"]
    P.ts(x0.h[:], x0.h[:], rc.h[:, 0:1], None, ALU.mult, None, [x0, rc], [x0])
    P.ts(x1.h[:], x0.h[:], MAGIC, MAGIC, ALU.add, ALU.subtract, [x0], [x1])
    P.tt(x1.h[:], x0.h[:], x1.h[:], ALU.subtract, [x0, x1], [x1])
    P.act(k.SIN.h[:], x1.h[:], AF.Sin, [x1, rc], [k.SIN], scale=rc.h[:, 1:2])
    P.ts(x0.h[:], x0.h[:], 0.25, None, ALU.add, None, [x0], [x0])
    P.ts(x1.h[:], x0.h[:], MAGIC, MAGIC, ALU.add, ALU.subtract, [x0, k.SIN], [x1])
    P.tt(x1.h[:], x0.h[:], x1.h[:], ALU.subtract, [x0, x1], [x1])
    P.act(k.COS.h[:], x1.h[:], AF.Sin, [x1], [k.COS], scale=TWO_PI)
    P.regions = save


def ln_feat(k, o_sb, npart, ones, ps_a, ps_b, tmp, out_fn):
    P = k.P
    sq, mean, rstd = tmp[2], tmp[0], tmp[1]
    P.tt(sq.h[0:npart, :], o_sb.h[0:npart, :], o_sb.h[0:npart, :], ALU.mult, [o_sb], [sq])
    P.mm(ps_a.h[0:npart, :], [(ones.h[0:npart, 0:npart], o_sb.h[0:npart, :])], [ones, o_sb], [ps_a])
    P.mm(ps_b.h[0:npart, :], [(ones.h[0:npart, 0:npart], sq.h[0:npart, :])], [ones, sq], [ps_b])
    P.cp(mean.h[0:npart, :], ps_a.h[0:npart, :], [ps_a], [mean], eng="act")
    P.tt(sq.h[0:npart, :], mean.h[0:npart, :], mean.h[0:npart, :], ALU.mult, [mean], [sq])
    P.tt(rstd.h[0:npart, :], ps_b.h[0:npart, :], sq.h[0:npart, :], ALU.subtract, [ps_b, sq], [rstd])
    P.act(rstd.h[0:npart, :], rstd.h[0:npart, :], AF.Ln, [rstd], [rstd], bias=EPS)
    P.act(rstd.h[0:npart, :], rstd.h[0:npart, :], AF.Exp, [rstd], [rstd], scale=-0.5)


def rms_bc(k, srcs, nfeat, ps, out_rstd):
    P = k.P
    pairs = []
    rd = [k.ones128]
    for (sq, nr) in srcs:
        pairs.append((k.ones128.h[0:nr, :], sq.h[0:nr, :]))
        rd.append(sq)
    P.mm(ps.h[:, :], pairs, rd, [ps])
    P.act(out_rstd.h[:], ps.h[:], AF.Ln, [ps], [out_rstd], scale=1.0 / nfeat, bias=EPS)
    P.act(out_rstd.h[:], out_rstd.h[:], AF.Exp, [out_rstd], [out_rstd], scale=-0.5)


def mla_mixer(k, l):
    P, PS, c = k.P, k.PS, k.c
    wcq, wckv, wkpe, _ = k.WB
    if not hasattr(k, "ds_mlaw"):
        k.ds_mlaw = P.new_dsem()
    uq32 = P.sb([128, 2, 384], F32, "uq32")
    uqr32 = P.sb([128, 2, 384], F32, "uqr32")
    ukv32 = P.sb([128, 512], F32, "ukv32")
    uq16 = P.sb([128, 2, 384], BF16, "uq16")
    uqr16 = P.sb([128, 2, 384], BF16, "uqr16")
    ukvK = P.sb([128, 256], BF16, "ukvK")
    ukvV = P.sb([128, 256], BF16, "ukvV")
    P.dma(uq32.h[:], k.dram["uq"][l], writes=[uq32], dsem=k.ds_mlaw)
    P.dma(uqr32.h[:], k.dram["uqr"][l], writes=[uqr32], dsem=k.ds_mlaw)
    P.dma(ukv32.h[:], k.dram["ukv"][l], writes=[ukv32], dsem=k.ds_mlaw)
    qn, kvn = k.prm["qn"], k.prm["kvn"]
    sc = 96.0 ** -0.5
    for cc in range(2):
        P.ts(uq16.h[:, cc, :], uq32.h[:, cc, :], qn.h[:, cc, l:l + 1], sc, ALU.mult, ALU.mult, [uq32, qn], [uq16])
        P.ts(uqr16.h[:, cc, :], uqr32.h[:, cc, :], qn.h[:, cc, l:l + 1], sc, ALU.mult, ALU.mult, [uqr32, qn], [uqr16])
    kv4 = ukv32.h[:].rearrange("p (h two d) -> p h two d", h=4, two=2)
    P.ts(ukvK.h[:].rearrange("p (h d) -> p h d", h=4), kv4[:, :, 0, :], kvn.h[:, l:l + 1], None, ALU.mult, None, [ukv32, kvn], [ukvK])
    P.ts(ukvV.h[:].rearrange("p (h d) -> p h d", h=4), kv4[:, :, 1, :], kvn.h[:, l:l + 1], None, ALU.mult, None, [ukv32, kvn], [ukvV])
    cqn = P.sb([128, 2, T], BF16, "cqn")
    ckvn = P.sb([128, T], BF16, "ckvn")
    KPE = P.sb([96, T], BF16, "KPE")
    Vtok = P.sb([128, 16, 256], BF16, "Vtok")
    QT = [P.sb([96, T], BF16, f"mQT{i}") for i in range(2)]
    KT = [P.sb([96, T], BF16, f"mKT{i}") for i in range(2)]
    A = [P.sb([128, 512], BF16, f"mA{i}") for i in range(3)]
    f0 = P.sb([128, 512], F32, "mf0")
    f1 = P.sb([128, 512], F32, "mf1")
    f2 = P.sb([128, 512], F32, "mf2")
    s0 = P.sb([128, 512], F32, "ms0")
    s1 = P.sb([128, 512], F32, "ms1")
    rs = P.sb([128, 512], F32, "mrs")
    ta = [P.sb([96, 512], F32, f"mta{i}") for i in range(2)]
    tb = [P.sb([96, 512], F32, f"mtb{i}") for i in range(2)]
    rec = [P.sb([64, 512], F32, f"mrec{i}") for i in range(2)]
    pend = [None]
    for j in range(4):
        js = slice(j * 512, (j + 1) * 512)
        inproj_fm(k, PS[4], 128, wcq, 0, j)
        inproj_fm(k, PS[5], 64, wcq, 128, j)
        P.cp(f0.h[:], PS[4].h[:], [PS[4]], [f0], eng="act")
        P.cp(f1.h[0:64, :], PS[5].h[0:64, :], [PS[5]], [f1], eng="act")
        P.tt(s0.h[:], f0.h[:], f0.h[:], ALU.mult, [f0], [s0])
        P.tt(s1.h[0:64, :], f1.h[0:64, :], f1.h[0:64, :], ALU.mult, [f1], [s1])
        rms_bc(k, [(s0, 128), (s1, 64)], 192.0, PS[6], rs)
        P.tt(cqn.h[:, 0, js], f0.h[:], rs.h[:], ALU.mult, [f0, rs], [cqn])
        P.tt(cqn.h[0:64, 1, js], f1.h[0:64, :], rs.h[0:64, :], ALU.mult, [f1, rs], [cqn])
        inproj_fm(k, PS[7], 128, wckv, 0, j)
        P.cp(f2.h[:], PS[7].h[:], [PS[7]], [f2], eng="act")
        P.tt(s0.h[:], f2.h[:], f2.h[:], ALU.mult, [f2], [s0])
        rms_bc(k, [(s0, 128)], 128.0, PS[6], rs)
        P.tt(ckvn.h[:, js], f2.h[:], rs.h[:], ALU.mult, [f2, rs], [ckvn])
        inproj_fm(k, PS[4], 96, wkpe, 0, j)
        inproj_fm(k, PS[5], 96, wkpe, 96, j)
        a1, a2 = ta[0], tb[0]
        P.tt(a1.h[64:96, :], PS[4].h[64:96, :], k.COS.h[64:96, js], ALU.mult, [PS[4], k.COS], [a1])
        P.tt(a2.h[64:96, :], PS[5].h[64:96, :], k.SIN.h[64:96, js], ALU.mult, [PS[5], k.SIN], [a2])
        P.tt(KPE.h[64:96, js], a1.h[64:96, :], a2.h[64:96, :], ALU.add, [a1, a2], [KPE])
    for i in range(16):
        ps = PS[4 + i % 2]
        P.mm(ps.h[:, 0:256], [(ckvn.h[:, i * 128:(i + 1) * 128], ukvV.h[:, :])], [ckvn, ukvV], [ps])
        P.cp(Vtok.h[:, i, :], ps.h[:, 0:256], [ps], [Vtok], eng="act")
    def gen(h):
        qt, kt = QT[h % 2], KT[h % 2]
        P.cp(kt.h[64:96, :], KPE.h[64:96, :], [KPE], [kt], eng="dve")
        for j in range(4):
            js = slice(j * 512, (j + 1) * 512)
            P.mm(PS[4].h[0:64, :], [(ukvK.h[:, h * 64:(h + 1) * 64], ckvn.h[:, js])], [ukvK, ckvn], [PS[4]])
            P.cp(kt.h[0:64, js], PS[4].h[0:64, :], [PS[4]], [kt], eng="act")
            yield
            hs = slice(h * 96, (h + 1) * 96)
            P.mm(PS[4].h[0:96, :], [(uq16.h[:, 0, hs], cqn.h[:, 0, js]), (uq16.h[0:64, 1, hs], cqn.h[0:64, 1, js])], [uq16, cqn], [PS[4]])
            P.mm(PS[5].h[0:96, :], [(uqr16.h[:, 0, hs], cqn.h[:, 0, js]), (uqr16.h[0:64, 1, hs], cqn.h[0:64, 1, js])], [uqr16, cqn], [PS[5]])
            a1, a2 = ta[j % 2], tb[j % 2]
            P.cp(qt.h[0:64, js], PS[4].h[0:64, :], [PS[4]], [qt], eng="act")
            P.tt(a1.h[64:96, :], PS[4].h[64:96, :], k.COS.h[64:96, js], ALU.mult, [PS[4], k.COS], [a1])
            P.tt(a2.h[64:96, :], PS[5].h[64:96, :], k.SIN.h[64:96, js], ALU.mult, [PS[5], k.SIN], [a2])
            yield
            P.tt(qt.h[64:96, js], a1.h[64:96, :], a2.h[64:96, :], ALU.add, [a1, a2], [qt])
            yield

    def attn(h):
        qt, kt = QT[h % 2], KT[h % 2]
        blocks = [(qb, kb) for qb in range(4) for kb in range(4 * qb + 4)]

        def emitS(i):
            qb, kb = blocks[i]
            S, a = PS[i % 2], A[i % 3]
            qs = slice(qb * 512, (qb + 1) * 512)
            P.mm(S.h[:, :], [(kt.h[:, kb * 128:(kb + 1) * 128], qt.h[:, qs])], [kt, qt], [S])
            P.act(a.h[:], S.h[:], AF.Exp, [S], [a])
            if kb >= 4 * qb:
                v = kb - 4 * qb
                P.tt(a.h[:], a.h[:], c["cmask16"].h[:, v, :], ALU.mult, [a, c["cmask16"]], [a])

        emitS(0)
        for i, (qb, kb) in enumerate(blocks):
            O, Dn = PS[2 + qb % 2], PS[6 + qb % 2]
            qs = slice(qb * 512, (qb + 1) * 512)
            nkb = 4 * qb + 4
            if i + 1 < len(blocks):
                emitS(i + 1)
            a = A[i % 3]
            st, sp = (kb == 0), (kb == nkb - 1)
            P.mm(O.h[0:64, :], [(Vtok.h[:, kb, h * 64:(h + 1) * 64], a.h[:])], [Vtok, a], acc=[O], start=st, stop=sp)
            P.mm(Dn.h[0:64, :], [(k.ones16.h[:, 0:64], a.h[:])], [k.ones16, a], acc=[Dn], start=st, stop=sp)
            if kb % 2 == 1 and kb != nkb - 1:
                yield
            if kb != nkb - 1:
                continue

            def post(O=O, Dn=Dn, qb=qb, qs=qs, h=h):
                r_ = rec[qb % 2]
                P.op("dve", lambda e, r_=r_, Dn=Dn: e.reciprocal(out=r_.h[:], in_=Dn.h[0:64, :]), [Dn], [r_])
                p0 = (h % 2) * 64
                P.tt(k.YT[1].h[p0:p0 + 64, h // 2, qs], O.h[0:64, :], r_.h[:], ALU.mult, [O, r_], [k.rYT[1][h // 2][qb]])
            if pend[0] is not None:
                pend[0]()
            pend[0] = post
            yield

    interleave([gen(0)])
    for h in range(4):
        gs = [attn(h)]
        if h + 1 < 4:
            gs.append(gen(h + 1))
        interleave(gs)
    if pend[0] is not None:
        pend[0]()


def frac_sincos(k, x, tmp, out_sin, out_cos, np_, reads):
    P = k.P
    r, f = tmp
    sl = slice(0, np_)
    P.ts(r.h[sl], x.h[sl], MAGIC, MAGIC, ALU.add, ALU.subtract, [x] + reads, [r])
    P.tt(f.h[sl], x.h[sl], r.h[sl], ALU.subtract, [x, r], [f])
    P.act(out_sin.h[sl], f.h[sl], AF.Sin, [f], [out_sin], scale=TWO_PI)
    P.ts(f.h[sl], x.h[sl], 0.25, None, ALU.add, None, [x, out_sin], [f])
    P.ts(r.h[sl], f.h[sl], MAGIC, MAGIC, ALU.add, ALU.subtract, [f], [r])
    P.tt(f.h[sl], f.h[sl], r.h[sl], ALU.subtract, [f, r], [f])
    P.act(out_cos.h[sl], f.h[sl], AF.Sin, [f], [out_cos], scale=TWO_PI)


def s5_mixer(k, l):
    P, PS, c = k.P, k.PS, k.c
    wu, wglu, wc1, wc2 = k.WB
    if not hasattr(k, "ds_s5"):
        k.ds_s5 = P.new_dsem()
    ds = k.ds_s5
    aB = P.sb([128, 2, 2, 64], F32, "s5aB")
    bT = P.sb([128, 2, 2, 64], F32, "s5bT")
    ldB = P.sb([128, 2], F32, "s5ldB")
    aC = P.sb([128, 2, 16], F32, "s5aC")
    ldC = P.sb([128, 16], F32, "s5ldC")
    C1 = P.sb([128, 16, 128], BF16, "s5C1")
    C2 = P.sb([128, 16, 128], BF16, "s5C2")
    G16 = P.sb([128, 2, 256], BF16, "s5glu")
    for buf, nm in ((aB, "s5aB"), (bT, "s5bT"), (ldB, "s5ldB"), (aC, "s5aC"), (ldC, "s5ldC")):
        P.dma(buf.h[:], k.dram[nm][l], writes=[buf], dsem=ds)
    P.dma(C1.h[:], k.dram["s5C1"][l], writes=[C1], dsem=wc1.dsem, eng="pool")
    P.dma(C2.h[:], k.dram["s5C2"][l], writes=[C2], dsem=wc2.dsem, eng="pool")
    P.dma(G16.h[:], k.dram["s5glu"][l].rearrange("(kk p) w -> p kk w", p=128), writes=[G16], dsem=wglu.dsem, eng="pool")
    sg = c["sgn"]
    P.ts(C1.h[:], C1.h[:], sg.h[:, 0:1], None, ALU.mult, None, [C1, sg], [C1])
    P.ts(C2.h[:], C2.h[:], -1.0, None, ALU.mult, None, [C2], [C2])
    def sm(name, shape=(128, 128)):
        return P.sb(list(shape), F32, name)
    dtB = sm("dtB", (128, 2))
    P.act(dtB.h[:], ldB.h[:], AF.Exp, [ldB], [dtB])
    bbr = P.sb([128, 2, 64], F32, "bbr")
    bbi = P.sb([128, 2, 64], F32, "bbi")
    t_ = [sm(f"s5t{i}", (128, 64)) for i in range(10)]
    for cc in range(2):
        lr, li = aB.h[:, cc, 0, :], aB.h[:, cc, 1, :]
        br, bi = bT.h[:, cc, 0, :], bT.h[:, cc, 1, :]
        dcol = dtB.h[:, cc:cc + 1]
        mag, tr, sn, cs, x0, x1, nre, nim, den, cre = t_
        P.act(mag.h[:], lr, AF.Exp, [aB, dtB], [mag], scale=dcol)
        P.ts(tr.h[:], li, dcol, 1.0 / TWO_PI, ALU.mult, ALU.mult, [aB, dtB], [tr])
        frac_sincos(k, tr, (x0, x1), sn, cs, 128, [])
        P.tt(nre.h[:], mag.h[:], cs.h[:], ALU.mult, [mag, cs], [nre])
        P.ts(nre.h[:], nre.h[:], -1.0, None, ALU.add, None, [nre], [nre])
        P.tt(nim.h[:], mag.h[:], sn.h[:], ALU.mult, [mag, sn], [nim])
        P.tt(den.h[:], lr, lr, ALU.mult, [aB], [den])
        P.tt(x0.h[:], li, li, ALU.mult, [aB, sn, cs], [x0])
        P.tt(den.h[:], den.h[:], x0.h[:], ALU.add, [den, x0], [den])
        P.op("dve", lambda e, den=den: e.reciprocal(out=den.h[:], in_=den.h[:]), [den], [den])
        P.tt(x0.h[:], nre.h[:], lr, ALU.mult, [nre, aB], [x0])
        P.tt(x1.h[:], nim.h[:], li, ALU.mult, [nim, aB], [x1])
        P.tt(cre.h[:], x0.h[:], x1.h[:], ALU.add, [x0, x1], [cre])
        P.tt(cre.h[:], cre.h[:], den.h[:], ALU.mult, [cre, den], [cre])
        P.tt(x0.h[:], nim.h[:], lr, ALU.mult, [nim, aB], [x0])
        P.tt(x1.h[:], nre.h[:], li, ALU.mult, [nre, aB], [x1])
        P.tt(x0.h[:], x0.h[:], x1.h[:], ALU.subtract, [x0, x1], [x0])
        P.tt(x0.h[:], x0.h[:], den.h[:], ALU.mult, [x0, den], [x0])
        P.tt(x1.h[:], cre.h[:], br, ALU.mult, [cre, bT], [x1])
        P.tt(mag.h[:], x0.h[:], bi, ALU.mult, [x0, bT], [mag])
        P.tt(bbr.h[:, cc, :], x1.h[:], mag.h[:], ALU.subtract, [x1, mag], [bbr])
        P.tt(x1.h[:], cre.h[:], bi, ALU.mult, [cre, bT], [x1])
        P.tt(mag.h[:], x0.h[:], br, ALU.mult, [x0, bT], [mag])
        P.tt(bbi.h[:, cc, :], x1.h[:], mag.h[:], ALU.add, [x1, mag], [bbi])
    LB = P.sb([128, 16, 128], BF16, "s5LB")
    LBs = P.sb([128, 16, 128], BF16, "s5LBs")
    gm = c["grpmask"]
    for g in range(16):
        cc, gi = g // 8, g % 8
        mcol = gm.h[:, gi:gi + 1]
        P.ts(LB.h[:, g, 0:64], bbr.h[:, cc, :], mcol, None, ALU.mult, None, [bbr, gm], [LB])
        P.ts(LB.h[:, g, 64:128], bbi.h[:, cc, :], mcol, None, ALU.mult, None, [bbi, gm], [LB])
        P.ts(LBs.h[:, g, 0:64], bbi.h[:, cc, :], mcol, None, ALU.mult, None, [bbi, gm], [LBs])
        P.ts(LBs.h[:, g, 64:128], bbr.h[:, cc, :], mcol, -1.0, ALU.mult, ALU.mult, [bbr, gm], [LBs])
    dtC = sm("dtC", (128, 16))
    rC = sm("rC", (128, 16))
    fC = sm("fC", (128, 16))
    fx = sm("fCx", (128, 16))
    P.act(dtC.h[:], ldC.h[:], AF.Exp, [ldC], [dtC])
    P.tt(rC.h[:], aC.h[:, 0, :], dtC.h[:], ALU.mult, [aC, dtC], [rC])
    P.act(rC.h[:], rC.h[:], AF.Exp, [rC], [rC])
    P.tt(fC.h[:], aC.h[:, 1, :], dtC.h[:], ALU.mult, [aC, dtC], [fC])
    P.ts(fC.h[:], fC.h[:], 1.0 / TWO_PI, None, ALU.mult, None, [fC], [fC])
    P.ts(fx.h[:], fC.h[:], MAGIC, MAGIC, ALU.add, ALU.subtract, [fC], [fx])
    P.tt(fC.h[:], fC.h[:], fx.h[:], ALU.subtract, [fC, fx], [fC])
    Dg = P.sb([128, 2, 128], BF16, "s5Dg")
    pd = k.prm["s5d"]
    for cc in range(2):
        P.ts(Dg.h[:, cc, :], c["ident"].h[:], pd.h[:, cc, l:l + 1], None, ALU.mult, None, [c["ident"], pd], [Dg])
    uT = P.sb([128, 2, T], BF16, "s5uT")
    gT = P.sb([128, 2, T], BF16, "s5gT")
    for cc in range(2):
        for j in range(4):
            ps = PS[4 + j % 2]
            inproj_fm(k, ps, 128, wu, cc * 128, j)
            P.cp(uT.h[:, cc, j * 512:(j + 1) * 512], ps.h[:], [ps], [uT], eng="act")
    onesf = sm("s5ones", (128, 512))
    P.op("dve", lambda e: e.memset(onesf.h[:], 1.0), [], [onesf])
    Rt = [sm(f"s5Rt{i}", (128, 512)) for i in range(2)]
    X = [sm(f"s5x{i}", (128, 512)) for i in range(2)]
    Xr = [sm(f"s5xr{i}", (128, 512)) for i in range(2)]
    Xf = [sm(f"s5xf{i}", (128, 512)) for i in range(2)]
    CO = [sm(f"s5co{i}", (128, 512)) for i in range(2)]
    SI = [sm(f"s5si{i}", (128, 512)) for i in range(2)]
    BT = [sm(f"s5bt{i}", (128, 512)) for i in range(2)]
    B2 = [sm(f"s5b2{i}", (128, 512)) for i in range(2)]
    ST = [sm(f"s5st{i}", (128, 512)) for i in range(2)]
    STS = [[ST[0], sm("s5st2", (128, 512))], [ST[1], sm("s5st3", (128, 512))]]
    Z1 = [P.sb([128, 512], BF16, f"s5z1{i}") for i in range(2)]
    Z2 = [P.sb([128, 512], BF16, f"s5z2{i}") for i in range(2)]
    def fsc(x, r, f, si, co, n=512):
        sl = slice(0, n)
        P.ts(r.h[:, sl], x.h[:, sl], MAGIC, MAGIC, ALU.add, ALU.subtract, [x], [r])
        P.tt(f.h[:, sl], x.h[:, sl], r.h[:, sl], ALU.subtract, [x, r], [f])
        P.act(si.h[:, sl], f.h[:, sl], AF.Sin, [f], [si], scale=TWO_PI)
        P.act(f.h[:, sl], x.h[:, sl], AF.Identity, [x, si], [f], bias=0.25)
        P.ts(r.h[:, sl], f.h[:, sl], MAGIC, MAGIC, ALU.add, ALU.subtract, [f], [r])
        P.tt(f.h[:, sl], f.h[:, sl], r.h[:, sl], ALU.subtract, [f, r], [f])
        P.act(co.h[:, sl], f.h[:, sl], AF.Sin, [f], [co], scale=TWO_PI)

    p512, c512, s512 = sm("s5p512", (128, 16)), sm("s5c512", (128, 16)), sm("s5s512", (128, 16))
    P.ts(p512.h[:], fC.h[:], 512.0, None, ALU.mult, None, [fC], [p512])
    fsc(p512, X[0], Xf[0], s512, c512, n=16)
    P.ts(s512.h[:], s512.h[:], c["sgn"].h[:, 0:1], None, ALU.mult, None, [s512, c["sgn"]], [s512])
    MROT = [sm(f"s5mrot{i}", (128, 128)) for i in range(2)]
    INI = [[sm(f"s5ini{b}{i}", (128, 1)) for i in range(2)] for b in range(2)]
    BTS = [[BT[0], CO[0]], [BT[1], CO[1]]]
    B2S = [[B2[0], SI[0]], [B2[1], SI[1]]]
    Z1S = [[Z1[0], P.sb([128, 512], BF16, "s5z1b")], [Z1[1], P.sb([128, 512], BF16, "s5z1c")]]
    Z2S = [[Z2[0], P.sb([128, 512], BF16, "s5z2b")], [Z2[1], P.sb([128, 512], BF16, "s5z2c")]]
    TABS = [sm(f"s5tabs{i}", (128, 512)) for i in range(2)]
    TABC = [sm(f"s5tabc{i}", (128, 512)) for i in range(2)]

    def lane(cc, gi, b):
        g = cc * 8 + gi
        rt, x, si, co, mrot = Rt[b], X[b], TABS[b], TABC[b], MROT[b]
        pa, pb = PS[4 + 2 * b], PS[5 + 2 * b]
        P.act(rt.h[:], onesf.h[:], AF.Copy, [onesf, rC], [rt], scale=rC.h[:, g:g + 1])
        P.act(x.h[:], c["iota512"].h[:], AF.Copy, [c["iota512"], fC], [x], scale=fC.h[:, g:g + 1])
        yield
        fsc(x, Xr[b], Xf[b], si, co)
        P.ts(mrot.h[:], c["ident"].h[:], c512.h[:, g:g + 1], None, ALU.mult, None, [c["ident"], c512], [mrot])
        yield
        P.stt(mrot.h[:], c["swapid"].h[:], s512.h[:, g:g + 1], mrot.h[:], ALU.mult, ALU.add, [c["swapid"], s512, mrot], [mrot])
        yield
        sts = {}

        def pre(j):
            js = slice(j * 512, (j + 1) * 512)
            bt, b2 = BTS[b][j % 2], B2S[b][j % 2]
            P.mm(pa.h[:, :], [(LB.h[:, g, :], uT.h[:, cc, js])], [LB, uT], [pa])
            P.mm(pb.h[:, :], [(LBs.h[:, g, :], uT.h[:, cc, js])], [LBs, uT], [pb])
            yield
            P.cp(bt.h[:], pa.h[:], [pa], [bt], eng="act")
            P.cp(b2.h[:], pb.h[:], [pb], [b2], eng="act")
            yield
            P.tt(bt.h[:], bt.h[:], co.h[:], ALU.mult, [bt, co], [bt])
            yield
            P.tt(b2.h[:], b2.h[:], si.h[:], ALU.mult, [b2, si], [b2])
            yield
            P.tt(bt.h[:], bt.h[:], b2.h[:], ALU.add, [bt, b2], [bt])
            yield

        def post(j):
            bt = BTS[b][j % 2]
            st = STS[b][j % 2]
            z1, z2 = Z1S[b][j % 2], Z2S[b][j % 2]
            if j == 0:
                P.scan(st.h[:], rt.h[:], bt.h[:], 0.0, [rt, bt], [st])
            else:
                prev = sts[j - 1]
                ini = INI[b][j % 2]
                P.mm(pa.h[:, 0:1], [(mrot.h[:, :], prev.h[:, 511:512])], [mrot, prev], [pa])
                P.cp(ini.h[:], pa.h[:, 0:1], [pa], [ini], eng="act")
                yield
                P.scan(st.h[:], rt.h[:], bt.h[:], ini.h[:, 0:1], [rt, bt, ini], [st])
            sts[j] = st
            yield
            P.tt(z1.h[:], st.h[:], co.h[:], ALU.mult, [st, co], [z1])
            P.tt(z2.h[:], st.h[:], si.h[:], ALU.mult, [st, si], [z2], eng="pool")
            yield
            P.mm(PS[j].h[:, :], [(C1.h[:, g, :], z1.h[:])], [C1, z1], acc=[PS[j]], start=(gi == 0), stop=False)
            P.mm(PS[j].h[:, :], [(C2.h[:, g, :], z2.h[:])], [C2, z2], acc=[PS[j]], start=False, stop=False)
            yield

        yield from pre(0)
        for j in range(4):
            if j + 1 < 4:
                yield from pre(j + 1)
            yield from post(j)

    for cc in range(2):
        for gp in range(4):
            interleave([lane(cc, 2 * gp, 0), lane(cc, 2 * gp + 1, 1)])
        for j in range(4):
            js = slice(j * 512, (j + 1) * 512)
            P.mm(PS[j].h[:, :], [(Dg.h[:, cc, :], uT.h[:, cc, js])], [Dg, uT], acc=[PS[j]], start=False, stop=True)
            y, t = BT[j % 2], B2[j % 2]
            P.cp(y.h[:], PS[j].h[:], [PS[j]], [y], eng="act")
            cg = math.sqrt(2.0 / math.pi)
            P.tt(t.h[:], y.h[:], y.h[:], ALU.mult, [y], [t])
            P.ts(t.h[:], t.h[:], 2.0 * cg * 0.044715, 2.0 * cg, ALU.mult, ALU.add, [t], [t])
            P.tt(t.h[:], t.h[:], y.h[:], ALU.mult, [t, y], [t])
            P.act(t.h[:], t.h[:], AF.Sigmoid, [t], [t])
            P.tt(gT.h[:, cc, js], t.h[:], y.h[:], ALU.mult, [t, y], [gT])
    for oc in range(2):
        for j in range(4):
            js = slice(j * 512, (j + 1) * 512)
            ps = PS[4 + j % 2]
            P.mm(ps.h[:, :], [(G16.h[:, kc, oc * 128:(oc + 1) * 128], gT.h[:, kc, js]) for kc in range(2)], [G16, gT], [ps])
            sgm = ST[j % 2]
            P.act(sgm.h[:], ps.h[:], AF.Sigmoid, [ps], [sgm])
            P.tt(k.YT[0].h[:, oc, js], gT.h[:, oc, js], sgm.h[:], ALU.mult, [gT, sgm], [k.rYT[0][oc][j]])


def hgrn_setup(k):
    P = k.P
    raw = k.prm["lbraw"]
    e = P.sb([64, 4, NL], F32, "lb_e")
    tot = P.sb([64, 4, 1], F32, "lb_tot")
    k.LBt = P.sb([64, 4, NL], F32, "LBt")
    k.OML = P.sb([64, 4, NL], F32, "OML")
    k.NOML = P.sb([64, 4, NL], F32, "NOML")
    P.act(e.h[:], raw.h[:], AF.Exp, [raw], [e])
    P.tt(tot.h[:, :, 0], e.h[:, :, 0], e.h[:, :, 1], ALU.add, [e], [tot])
    for l in range(2, NL):
        P.tt(tot.h[:, :, 0], tot.h[:, :, 0], e.h[:, :, l], ALU.add, [e, tot], [tot])
    P.op("dve", lambda ee: ee.reciprocal(out=tot.h[:, :, 0], in_=tot.h[:, :, 0]), [tot], [tot])
    P.op("dve", lambda ee: ee.memset(k.LBt.h[:, :, 0], 0.0), [], [k.LBt])
    for l in range(1, NL):
        if l == 1:
            P.cp(k.LBt.h[:, :, 1], e.h[:, :, 1], [e], [k.LBt])
        else:
            P.tt(k.LBt.h[:, :, l], k.LBt.h[:, :, l - 1], e.h[:, :, l], ALU.add, [e, k.LBt], [k.LBt])
    for l in range(1, NL):
        P.tt(k.LBt.h[:, :, l], k.LBt.h[:, :, l], tot.h[:, :, 0], ALU.mult, [k.LBt, tot], [k.LBt])
    P.ts(k.OML.h[:], k.LBt.h[:], -1.0, 1.0, ALU.mult, ALU.add, [k.LBt], [k.OML])
    P.ts(k.NOML.h[:], k.OML.h[:], -1.0, None, ALU.mult, None, [k.OML], [k.NOML])


def hgrn_mixer(k, l):
    P, PS, c = k.P, k.PS, k.c
    wq, wf, wi, wg = k.WB
    Vtok = P.sb([128, 16, 256], BF16, "hVtok")
    QT = P.sb([64, T], BF16, "hQT")
    KT = P.sb([64, T], BF16, "hKT")
    Khtok = P.sb([128, 16, 64], BF16, "hKhtok")
    US = P.sb([64, 64, 128], BF16, "hUS")
    dco = P.sb([64, 128], F32, "hdco")
    dco8 = P.sb([64, 8, 128], F32, "hdco8")

    def f(name):
        return P.sb([64, 512], F32, name)
    S1 = [[f(f"h{n}{i}") for n in ("A", "B", "C", "D", "E")] for i in range(4)]
    S4 = [[f(f"h{n}{i}") for n in ("osb", "sqo", "rstd", "sgs")] for i in range(2)]
    Vexp = [P.sb([128, 8, 64], BF16, f"hVexp{i}") for i in range(2)]
    A = [P.sb([128, 128], BF16, f"hA{i}") for i in range(2)]
    hgn = k.prm["hgn"]
    idn = c["ident"]
    for i in range(16):
        ps = PS[4 + i % 2]
        inproj_tm(k, ps, 256, wi, 0, i)
        P.cp(Vtok.h[:, i, :], ps.h[:, 0:256], [ps], [Vtok], eng="act")

    def ph1(h, j, b):
        A_, B_, C_, D_, E_ = S1[b]
        pf, pq = PS[2 * b], PS[2 * b + 1]
        pt = pf
        lbc, omlc, nomlc = k.LBt.h[:, h, l:l + 1], k.OML.h[:, h, l:l + 1], k.NOML.h[:, h, l:l + 1]
        prm_r = [k.LBt, k.OML, k.NOML]
        js = slice(j * 512, (j + 1) * 512)
        inproj_fm(k, pf, 64, wf, h * 64, j)
        inproj_fm(k, pq, 64, wq, h * 64, j)
        yield
        P.act(A_.h[:], pf.h[0:64, :], AF.Sigmoid, [pf], [A_])
        P.act(D_.h[:], pq.h[0:64, :], AF.Silu, [pq], [D_])
        yield
        P.act(B_.h[:], A_.h[:], AF.Ln, [A_] + prm_r, [B_], scale=omlc, bias=lbc)
        P.ts(C_.h[:], A_.h[:], nomlc, omlc, ALU.mult, ALU.add, [A_] + prm_r, [C_])
        yield
        P.scan(B_.h[:], c["rst16"].h[0:64, :], B_.h[:], 0.0, [c["rst16"], B_], [B_])
        yield
        P.act(A_.h[:], B_.h[:], AF.Exp, [B_, C_], [A_])
        b3 = B_.h[:].rearrange("p (n s) -> p n s", s=16)
        P.tt(E_.h[:].rearrange("p (n s) -> p n s", s=16), b3[:, :, 15:16].broadcast_to([64, 32, 16]), b3, ALU.subtract, [B_], [E_])
        yield
        P.tt(QT.h[:, js], D_.h[:], A_.h[:], ALU.mult, [D_, A_], [QT])
        P.act(E_.h[:], E_.h[:], AF.Exp, [E_], [E_])
        yield
        P.act(A_.h[:], B_.h[:], AF.Exp, [B_, QT], [A_], scale=-1.0)
        P.tt(E_.h[:], C_.h[:], E_.h[:], ALU.mult, [C_, E_], [E_])
        yield
        P.tt(KT.h[:, js], C_.h[:], A_.h[:], ALU.mult, [C_, A_], [KT])
        P.act(dco.h[:, j * 32:(j + 1) * 32], b3[:, :, 15], AF.Exp, [B_], [dco])

        def trf(e):
            ins = None
            for q in range(4):
                ins = e.transpose(pt.h[:, q * 64:(q + 1) * 64], E_.h[:, q * 128:(q + 1) * 128], idn.h[0:64, 0:64])
            return ins
        P.op("pe", trf, [E_, idn], [pt])
        yield
        P.cp(Khtok.h[:, j * 4:(j + 1) * 4, :], pt.h[:, 0:256].rearrange("p (q d) -> p q d", q=4), [pt], [Khtok], eng="act")
        yield

    def ph4(h, j, b):
        osb, sqo, rstd, sgs = S4[b]
        S, O, I, X = (PS[0], PS[2], PS[4], PS[6]) if b == 0 else (PS[1], PS[3], PS[5], PS[7])
        a = A[b]
        js = slice(j * 512, (j + 1) * 512)
        for q in range(4):
            i = j * 4 + q
            ts_ = slice(i * 128, (i + 1) * 128)
            P.mm(S.h[:, 0:128], [(KT.h[:, ts_], QT.h[:, ts_])], [KT, QT], [S])
            yield
            P.tt(a.h[:], S.h[:, 0:128], c["bdmask16"].h[:], ALU.mult, [S, c["bdmask16"]], [a])
            yield
            P.mm(O.h[0:64, q * 128:(q + 1) * 128], [(Vtok.h[:, i, h * 64:(h + 1) * 64], a.h[:])], [Vtok, a], acc=[O])
            yield
        if j == 0:
            P.op("dve", lambda e: e.memset(I.h[0:64, 0:16], 0.0), [], [], acc=[I])

        def interf(e):
            ins = None
            for nn in range(32):
                n = j * 32 + nn
                if n == 0:
                    continue
                ins = e.matmul(I.h[0:64, nn * 16:(nn + 1) * 16], US.h[:, :, n - 1], QT.h[:, n * 16:(n + 1) * 16], start=True, stop=True)
            return ins
        P.op("pe", interf, [US, QT], [], acc=[I])
        yield
        P.cp(osb.h[:], O.h[0:64, :], [O], [osb], eng="act")
        yield
        P.tt(osb.h[:], osb.h[:], I.h[0:64, :], ALU.add, [osb, I], [osb])
        yield
        P.tt(sqo.h[:], osb.h[:], osb.h[:], ALU.mult, [osb], [sqo])
        yield
        P.mm(X.h[0:64, :], [(k.ones64.h[:, :], sqo.h[:])], [k.ones64, sqo], [X])
        yield
        P.act(rstd.h[:], X.h[0:64, :], AF.Ln, [X], [rstd], bias=EPS)
        yield
        P.act(rstd.h[:], rstd.h[:], AF.Exp, [rstd], [rstd], scale=-0.5)
        inproj_fm(k, X, 64, wg, h * 64, j)
        yield
        P.stt(osb.h[:], osb.h[:], hgn.h[:, h, l:l + 1], rstd.h[:], ALU.mult, ALU.mult, [osb, hgn, rstd], [osb])
        P.act(sgs.h[:], X.h[0:64, :], AF.Silu, [X], [sgs])
        yield
        p0 = (h % 2) * 64
        P.tt(k.YT[2].h[p0:p0 + 64, h // 2, js], osb.h[:], sgs.h[:], ALU.mult, [osb, sgs], [k.rYT[2][h // 2][j]])
        yield

    for h in range(4):
        interleave([ph1(h, j, j) for j in range(4)])
        for i in range(16):
            ve = Vexp[i % 2]
            pu = PS[6 + i % 2]
            P.tt(ve.h[:], Vtok.h[:, i:i + 1, h * 64:(h + 1) * 64].broadcast_to([128, 8, 64]), c["cexp16"].h[:], ALU.mult,
                 [Vtok, c["cexp16"]], [ve])
            P.mm(pu.h[0:64, :], [(Khtok.h[:, i, :], ve.h[:].rearrange("p n v -> p (n v)"))], [Khtok, ve], [pu])
            P.cp(US.h[:, :, i * 8:(i + 1) * 8], pu.h[0:64, :].rearrange("p (n v) -> p v n", n=8), [pu], [US], eng="act")
        P.op("dve", lambda e: e.memset(dco.h[:, 0:1], 0.0), [], [dco])
        P.cp(dco8.h[:], dco.h[:, :].unsqueeze(1).broadcast_to([64, 8, 128]), [dco], [dco8], eng="dve")
        for v8 in range(8):
            vs = slice(v8 * 8, (v8 + 1) * 8)
            P.scan(US.h[:, vs, :].rearrange("p v n -> p (v n)"), dco8.h[:].rearrange("p v n -> p (v n)"),
                   US.h[:, vs, :].rearrange("p v n -> p (v n)"), 0.0, [dco8, US], [US])
        interleave([ph4(h, 0, 0), ph4(h, 1, 1)])
        interleave([ph4(h, 2, 0), ph4(h, 3, 1)])


def ln_chunks(k, j, src, src_regs, g_prm, b_prm, l, tmps, extra_reads=(), write_b16=True):
    P, PS = k.P, k.PS
    js = slice(j * 512, (j + 1) * 512)
    sqa, sqb, mean, rstd, nmr = tmps
    Ps, Pq = PS[6], PS[7]
    for c in range(8):
        sq = (sqa, sqb)[c % 2]
        P.tt(sq.h[:], src(c), src(c), ALU.mult, [src_regs[c]] + list(extra_reads), [sq])
        P.mm(Ps.h[:, :], [(k.ones128.h[:, :], src(c))], [k.ones128, src_regs[c]], acc=[Ps], start=(c == 0), stop=(c == 7))
        P.mm(Pq.h[:, :], [(k.ones128.h[:, :], sq.h[:])], [k.ones128, sq], acc=[Pq], start=(c == 0), stop=(c == 7))
    P.act(mean.h[:], Ps.h[:], AF.Copy, [Ps], [mean], scale=1.0 / D)
    P.tt(sqa.h[:], mean.h[:], mean.h[:], ALU.mult, [mean], [sqa])
    P.stt(rstd.h[:], Pq.h[:], 1.0 / D, sqa.h[:], ALU.mult, ALU.subtract, [Pq, sqa], [rstd])
    P.act(rstd.h[:], rstd.h[:], AF.Ln, [rstd], [rstd], bias=EPS)
    P.act(rstd.h[:], rstd.h[:], AF.Exp, [rstd], [rstd], scale=-0.5)
    P.stt(nmr.h[:], mean.h[:], -1.0, rstd.h[:], ALU.mult, ALU.mult, [mean, rstd], [nmr])
    for c in range(8):
        t = (sqa, sqb)[c % 2]
        P.tt(t.h[:], src(c), rstd.h[:], ALU.mult, [src_regs[c], rstd], [t])
        P.tt(t.h[:], t.h[:], nmr.h[:], ALU.add, [t, nmr], [t])
        P.act(k.XT.h[:, c, js], t.h[:], AF.Identity, [t, g_prm, b_prm], [k.rXT[c][j]], scale=g_prm.h[:, c, l:l + 1],
              bias=b_prm.h[:, c, l:l + 1])
        if write_b16:
            P.act(k.XTb.h[:, c, js], k.XT.h[:, c, js], AF.Copy, [k.rXT[c][j]], [k.rXTb[c][j]])


def gate_phase(k, l):
    P, PS, cfg = k.P, k.PS, k.cfg
    if not hasattr(k, "ds_g"):
        k.ds_g = [P.new_dsem() for _ in range(4)]
    P.set_regions([(k.xt_off, k.up_off)])
    WBR = P.sb([128, 4, 2, D], BF16, "WBR")
    WG = [P.sb([128, 4, 8, 128], BF16, f"WG{i}") for i in range(2)]
    WBR.dsem = k.ds_g[0]
    WG[0].dsem, WG[1].dsem = k.ds_g[1], k.ds_g[2]
    ACC = [P.sb([128, 512], F32, f"ACC{i}") for i in range(2)]
    sgt = [P.sb([128, 512], F32, f"sgt{i}") for i in range(2)]
    tmp = [P.sb([128, 512], F32, f"gtmp{i}") for i in range(2)]
    P.set_regions([(k.up2_off, SB_TOP)])
    MIXT = P.sb([128, 8, T], BF16, "MIXT")
    rMIX = [[Reg(f"mix{c}_{j}") for j in range(4)] for c in range(8)]
    P.dma(WBR.h[:], k.dram["wbr"][l].rearrange("n (kc p) d -> p n kc d", p=128), writes=[WBR], dsem=WBR.dsem, eng="pool")
    it = 0
    for c in range(8):
        wgb = WG[c % 2]
        for n in range(4):
            col0 = n * 1024 + c * 128
            P.dma(wgb.h[:, n, :, :], k.dram["wg"][l, :, col0:col0 + 128].rearrange("(kk p) w -> p kk w", p=128), writes=[wgb],
                  dsem=wgb.dsem, eng="pool")
        for j in range(4):
            js = slice(j * 512, (j + 1) * 512)
            acc = ACC[j % 2]
            for n in range(4):
                pg, pb = PS[2 * (it % 2)], PS[2 * (it % 2) + 1]
                s_, t_ = sgt[it % 2], tmp[it % 2]
                it += 1
                P.mm(pg.h[:, :], [(wgb.h[:, n, kk, :], k.XTb.h[:, kk, js]) for kk in range(8)], [wgb] + xtb_regs(k, j), [pg])
                P.mm(pb.h[:, :], [(WBR.h[:, n, kc, c * 128:(c + 1) * 128], k.YT[n].h[:, kc, js]) for kc in range(2)],
                     [WBR, k.rYT[n][0][j], k.rYT[n][1][j]], [pb])
                P.act(s_.h[:], pg.h[:], AF.Sigmoid, [pg], [s_])
                if n == 0:
                    P.tt(acc.h[:], s_.h[:], pb.h[:], ALU.mult, [s_, pb], [acc])
                else:
                    P.tt(t_.h[:], s_.h[:], pb.h[:], ALU.mult, [s_, pb], [t_])
                    if n < 3:
                        P.tt(acc.h[:], acc.h[:], t_.h[:], ALU.add, [acc, t_], [acc])
                    else:
                        P.tt(MIXT.h[:, c, js], acc.h[:], t_.h[:], ALU.add, [acc, t_], [rMIX[c][j]])
    if cfg.get("taps"):
        tap(k, f"mix{l}", MIXT.h[:], [128, 8, T], [r for row in rMIX for r in row], BF16)
    P.fence(k.ARENA, [r for row in k.rXT for r in row])
    P.phase = f"L{l}.wo"
    P.set_regions([(k.up_off, k.up_off + 32768)])
    tmps = [P.sb([128, 512], F32, f"lnA{i}") for i in range(5)]
    g1, b1 = k.prm["ln1g"], k.prm["ln1b"]
    it = 0
    for j in range(4):
        js = slice(j * 512, (j + 1) * 512)
        P.dma(k.XT.h[:, :, js], k.spill[:, :, js], reads=k.rSP, writes=[k.rXT[c][j] for c in range(8)], dsem=k.ds_g[3])
        for oh in range(4):
            wob = k.WB[it % 4]
            it += 1
            P.dma(wob.h[:], k.dram["wo"][l, :, oh * 256:(oh + 1) * 256].rearrange("(kk p) w -> p kk w", p=128), writes=[wob], dsem=wob.dsem,
                  eng="pool")
            for o2 in range(2):
                oc = oh * 2 + o2
                po = PS[4 + oc % 2]
                P.mm(po.h[:, :], [(wob.h[:, kk, o2 * 128:(o2 + 1) * 128], MIXT.h[:, kk, js]) for kk in range(8)],
                     [wob] + [rMIX[kk][j] for kk in range(8)], [po])
                P.stt(k.XT.h[:, oc, js], k.XT.h[:, oc, js], float(ALPHA), po.h[:], ALU.mult, ALU.add, [k.rXT[oc][j], po], [k.rXT[oc][j]])
        if j > 0:
            jp = j - 1
            jps = slice(jp * 512, (jp + 1) * 512)
            ln_chunks(k, jp, lambda cc, jps=jps: k.XT.h[:, cc, jps], [k.rXT[cc][jp] for cc in range(8)], g1, b1, l, tmps)
    js = slice(3 * 512, 4 * 512)
    ln_chunks(k, 3, lambda cc, js=js: k.XT.h[:, cc, js], [k.rXT[cc][3] for cc in range(8)], g1, b1, l, tmps)
    P.set_regions(None)


def ffn_phase(k, l):
    P, PS, cfg, c = k.P, k.PS, k.cfg, k.c
    moe = (l % 2 == 1)
    li = l // 2
    if not hasattr(k, "ds_f"):
        k.ds_f = [P.new_dsem() for _ in range(8)]
    P.set_regions([(k.up_off, k.up_off + 32768), (k.up2_off, SB_TOP), (k.up_off + 32768, k.up2_off)])
    Wg = [P.sb([128, 8, 512], BF16, f"Wg{i}") for i in range(2)]
    Wu = [P.sb([128, 8, 512], BF16, f"Wu{i}") for i in range(2)]
    Wd = [P.sb([128, 4, D], BF16, f"Wd{i}") for i in range(2)]
    for i in range(2):
        Wg[i].dsem, Wu[i].dsem, Wd[i].dsem = k.ds_f[3 * i], k.ds_f[3 * i + 1], k.ds_f[3 * i + 2]
    H = [P.sb([128, 4, 512], BF16, f"H{i}") for i in range(2)]
    sgb = [P.sb([128, 512], F32, f"fsg{i}") for i in range(2)]
    tb = P.sb([128, 512], F32, "ftb")
    nexp = NE if moe else 1
    if moe:
        WT = P.sb([8, T], F32, "WT")
        WT16 = P.sb([64, T], BF16, "WT16")
        WR = P.sb([128, 8, 8], F32, "WR")
        LG = P.sb([128, 16, 8], F32, "LG")
        M8 = P.sb([128, 16, 8], F32, "M8")
        MK = P.sb([128, 16, 8], F32, "MK")
        EX = P.sb([128, 16, 8], F32, "EX")
        DN = P.sb([128, 16, 1], F32, "DN")
        P.dma(WR.h[:], k.dram["mor"][li].rearrange("(kk p) e -> p kk e", p=128), writes=[WR], dsem=k.ds_f[6])
        for i in range(16):
            ps = PS[6 + i % 2]
            P.mm(ps.h[:, 0:8], [(k.XT.h[:, kk, i * 128:(i + 1) * 128], WR.h[:, kk, :]) for kk in range(8)],
                 [WR] + [k.rXT[kk][i // 4] for kk in range(8)], [ps])
            P.cp(LG.h[:, i, :], ps.h[:, 0:8], [ps], [LG], eng="act")
            P.op("dve", lambda e, i=i: e.max(out=M8.h[:, i, :], in_=LG.h[:, i, :]), [LG], [M8])
        P.tt(MK.h[:], LG.h[:], M8.h[:, :, 1:2].broadcast_to([128, 16, 8]), ALU.is_ge, [LG, M8], [MK])
        P.tt(EX.h[:], LG.h[:], M8.h[:, :, 0:1].broadcast_to([128, 16, 8]), ALU.subtract, [LG, M8], [EX])
        P.act(EX.h[:], EX.h[:], AF.Exp, [EX], [EX])
        P.tt(EX.h[:], EX.h[:], MK.h[:], ALU.mult, [EX, MK], [EX])
        P.op("dve", lambda e: e.tensor_reduce(out=DN.h[:, :, 0], in_=EX.h[:], axis=mybir.AxisListType.X, op=ALU.add), [EX], [DN])
        P.op("dve", lambda e: e.reciprocal(out=DN.h[:], in_=DN.h[:]), [DN], [DN])
        P.tt(EX.h[:], EX.h[:], DN.h[:].broadcast_to([128, 16, 8]), ALU.mult, [EX, DN], [EX])
        for i4 in range(4):
            ps = PS[6 + i4 % 2]

            def trf(e, i4=i4, ps=ps):
                ins = None
                for q in range(4):
                    ins = e.transpose(ps.h[0:8, q * 128:(q + 1) * 128], EX.h[:, i4 * 4 + q, :], c["ident"].h[:, :])
                return ins
            P.op("pe", trf, [EX, c["ident"]], [ps])
            P.cp(WT.h[:, i4 * 512:(i4 + 1) * 512], ps.h[0:8, :], [ps], [WT], eng="act")
        P.op("dve", lambda e: e.memset(WT16.h[:], 0.0), [], [WT16])
        P.cp(WT16.h[0:8, :], WT.h[:, :], [WT], [WT16])
        P.tt(WT.h[:, :], WT.h[:, :], WT16.h[0:8, :], ALU.subtract, [WT, WT16], [WT])
        P.cp(WT16.h[32:40, :], WT.h[:, :], [WT], [WT16])
    for cc in range(8):
        for j in range(4):
            js = slice(j * 512, (j + 1) * 512)
            if (cc + j) % 2 == 0:
                P.act(k.XT.h[:, cc, js], k.XT.h[:, cc, js], AF.Copy, [k.rXT[cc][j]], [k.rXT[cc][j]], scale=float(ALPHA))
            else:
                P.ts(k.XT.h[:, cc, js], k.XT.h[:, cc, js], float(ALPHA), None, ALU.mult, None, [k.rXT[cc][j]], [k.rXT[cc][j]])
    it = 0
    pending = None
    for e_ in range(nexp):
        if moe:
            gsrc, usrc, dsrc = k.dram["mog"][li, e_], k.dram["mou"][li, e_], k.dram["mod"][li, e_]
        else:
            gsrc, usrc, dsrc = k.dram["ffg"][li], k.dram["ffu"][li], k.dram["ffd"][li]
        for fb in range(7):
            b = it % 2
            it += 1
            wg_, wu_, wd_ = Wg[b], Wu[b], Wd[b]
            fs = slice(fb * 512, (fb + 1) * 512)
            P.dma(wg_.h[:], gsrc[:, fs].rearrange("(kk p) w -> p kk w", p=128), writes=[wg_], dsem=wg_.dsem, eng="pool")
            P.dma(wu_.h[:], usrc[:, fs].rearrange("(kk p) w -> p kk w", p=128), writes=[wu_], dsem=wu_.dsem, eng="pool")
            P.dma(wd_.h[:], dsrc[fs, :].rearrange("(fc p) d -> p fc d", p=128), writes=[wd_], dsem=wd_.dsem, eng="pool")
            for j in range(4):
                js = slice(j * 512, (j + 1) * 512)
                hb = H[j % 2]
                if moe:
                    pw = PS[6 + j % 2]
                    P.mm(pw.h[:, :], [(c["sel16"].h[:, e_, :], WT16.h[:, js])], [c["sel16"], WT16], [pw])
                for fc in range(4):
                    pg, pu = PS[2 * (fc % 2)], PS[2 * (fc % 2) + 1]
                    xr = xtb_regs(k, j)
                    P.mm(pg.h[:, :], [(wg_.h[:, kk, fc * 128:(fc + 1) * 128], k.XTb.h[:, kk, js]) for kk in range(8)], [wg_] + xr, [pg])
                    P.mm(pu.h[:, :], [(wu_.h[:, kk, fc * 128:(fc + 1) * 128], k.XTb.h[:, kk, js]) for kk in range(8)], [wu_] + xr, [pu])
                    s_ = sgb[fc % 2]
                    P.act(s_.h[:], pg.h[:], AF.Silu, [pg], [s_])
                    if moe:
                        P.tt(tb.h[:], s_.h[:], pw.h[:], ALU.mult, [s_, pw], [tb])
                        P.tt(hb.h[:, fc, :], tb.h[:], pu.h[:], ALU.mult, [tb, pu], [hb])
                    else:
                        P.tt(hb.h[:, fc, :], s_.h[:], pu.h[:], ALU.mult, [s_, pu], [hb])
                    if fc == 1 and pending is not None:
                        pending()
                        pending = None

                def down(wd_=wd_, hb=hb, j=j, js=js):
                    for oc in range(8):
                        pd = PS[4 + oc % 2]
                        P.mm(pd.h[:, :], [(wd_.h[:, fc, oc * 128:(oc + 1) * 128], hb.h[:, fc, :]) for fc in range(4)], [wd_, hb], [pd])
                        P.tt(k.XT.h[:, oc, js], k.XT.h[:, oc, js], pd.h[:], ALU.add, [k.rXT[oc][j], pd], [k.rXT[oc][j]])
                pending = down
    if pending is not None:
        pending()
    if cfg.get("taps"):
        tap(k, f"fpre{l}", k.XT.h[:], [128, 8, T], [r for row in k.rXT for r in row])
    P.fence(k.ARENA)
    P.phase = f"L{l}.ple"
    P.set_regions([(k.up_off, k.up_off + 32768), (k.up2_off, SB_TOP)])
    pT = P.sb([128, 2, T], BF16, "pT")
    WPP = P.sb([128, 2, D], BF16, "WPP")
    WPP.dsem = k.ds_f[0]
    ptl = [P.sb([128, 256], F32, f"ptl{i}") for i in range(4)]
    if not hasattr(k, "ds_ptl"):
        k.ds_ptl = [P.new_dsem() for _ in range(2)]
    ptl[0].dsem, ptl[1].dsem, ptl[2].dsem, ptl[3].dsem = k.ds_f[6], k.ds_f[7], k.ds_ptl[0], k.ds_ptl[1]
    sgp = [P.sb([128, 512], F32, f"psg{i}") for i in range(2)]
    tmps = [P.sb([128, 512], F32, f"lnB{i}") for i in range(5)]
    P.dma(WPP.h[:], k.dram["plp"][l].rearrange("(kc p) d -> p kc d", p=128), writes=[WPP], dsem=WPP.dsem, eng="pool")
    rpT = [Reg(f"pT{j}") for j in range(4)]

    def p_tiles(j):
        for i in range(4 * j, 4 * j + 4):
            pt = ptl[i % 4]
            P.dma(pt.h[:], k.dram["p"][l, i * 128:(i + 1) * 128, :], writes=[pt], dsem=pt.dsem)
            ps = PS[6 + i % 2]

            def trf(e, pt=pt, ps=ps):
                ins = None
                for q in range(2):
                    ins = e.transpose(ps.h[:, q * 128:(q + 1) * 128], pt.h[:, q * 128:(q + 1) * 128], c["ident"].h[:, :])
                return ins
            P.op("pe", trf, [pt, c["ident"]], [ps])
            P.cp(pT.h[:, :, i * 128:(i + 1) * 128], ps.h[:, 0:256].rearrange("p (q t) -> p q t", q=2), [ps], [rpT[j]], eng="act")
    p_tiles(0)
    g2, b2 = k.prm["ln2g"], k.prm["ln2b"]
    spill_next = (l + 1 < cfg.get("nl", 0))
    it = 0
    for j in range(4):
        js = slice(j * 512, (j + 1) * 512)
        for oh in range(4):
            wb = k.WB[it % 4]
            it += 1
            P.dma(wb.h[:], k.dram["plg"][l, :, oh * 256:(oh + 1) * 256].rearrange("(kk p) w -> p kk w", p=128), writes=[wb], dsem=wb.dsem,
                  eng="pool")
            for o2 in range(2):
                oc = oh * 2 + o2
                pa, pb = PS[2 * (oc % 2)], PS[2 * (oc % 2) + 1]
                P.mm(pa.h[:, :], [(wb.h[:, kk, o2 * 128:(o2 + 1) * 128], k.XTb.h[:, kk, js]) for kk in range(8)], [wb] + xtb_regs(k, j), [pa])
                P.mm(pb.h[:, :], [(WPP.h[:, kc, oc * 128:(oc + 1) * 128], pT.h[:, kc, js]) for kc in range(2)], [WPP, rpT[j]], [pb])
                s_ = sgp[oc % 2]
                P.act(s_.h[:], pa.h[:], AF.Sigmoid, [pa], [s_])
                P.tt(s_.h[:], s_.h[:], pb.h[:], ALU.mult, [s_, pb], [s_])
                P.tt(k.XT.h[:, oc, js], k.XT.h[:, oc, js], s_.h[:], ALU.add, [k.rXT[oc][j], s_], [k.rXT[oc][j]])
        if j + 1 < 4:
            p_tiles(j + 1)
        if j > 0:
            jp = j - 1
            jps = slice(jp * 512, (jp + 1) * 512)
            ln_chunks(k, jp, lambda cc, jps=jps: k.XT.h[:, cc, jps], [k.rXT[cc][jp] for cc in range(8)], g2, b2, l, tmps,
                      write_b16=spill_next)
            if spill_next:
                P.dma(k.spill[:, :, jps], k.XT.h[:, :, jps], reads=[k.rXT[cc][jp] for cc in range(8)], writes=k.rSP, dsem=k.ds_sp,
                      arena=False)
    js = slice(3 * 512, 4 * 512)
    ln_chunks(k, 3, lambda cc, js=js: k.XT.h[:, cc, js], [k.rXT[cc][3] for cc in range(8)], g2, b2, l, tmps, write_b16=spill_next)
    if spill_next:
        P.dma(k.spill[:, :, js], k.XT.h[:, :, js], reads=[k.rXT[cc][3] for cc in range(8)], writes=k.rSP, dsem=k.ds_sp, arena=False)
    P.set_regions(None)


def ret_mixer(k, l):
    P, PS, c = k.P, k.PS, k.c
    gam = _gammas()
    wq, wk, wqp, wkp = k.WB
    if not hasattr(k, "wbx_dsems"):
        k.wbx_dsems = [P.new_dsem() for _ in range(2)]
    wv = P.sb([128, 8, 256], BF16, "WBv")
    wg = P.sb([128, 8, 256], BF16, "WBg")
    wv.dsem, wg.dsem = k.wbx_dsems
    wb_load(k, wv, l, C_RV, 256)
    wb_load(k, wg, l, C_RG, 256)
    Vtok = P.sb([128, 16, 256], BF16, "Vtok")
    QT = [P.sb([64, T], BF16, f"QT{i}") for i in range(2)]
    KT = [P.sb([64, T], BF16, f"KT{i}") for i in range(2)]
    t1 = [P.sb([64, 512], F32, f"rt1_{i}") for i in range(2)]
    t2 = [P.sb([64, 512], F32, f"rt2_{i}") for i in range(2)]
    A = [P.sb([128, 512], BF16, f"A{i}") for i in range(3)]
    osb = [P.sb([64, 512], F32, f"osb{i}") for i in range(2)]
    tmp = [P.sb([64, 512], F32, f"lnt{i}") for i in range(3)]
    sgs = P.sb([64, 512], F32, "sgs")
    prg, prb = k.prm["retg"], k.prm["retb"]
    pend = [None]
    for i in range(16):
        ps = PS[4 + i % 2]
        inproj_tm(k, ps, 256, wv, 0, i)
        P.cp(Vtok.h[:, i, :], ps.h[:, 0:256], [ps], [Vtok], eng="act")
    def gen(h):
        qt, kt = QT[h % 2], KT[h % 2]
        for j in range(4):
            js = slice(j * 512, (j + 1) * 512)
            for which, (wa, wb_, dst) in enumerate(((wq, wqp, qt), (wk, wkp, kt))):
                pa, pb = PS[4 + 2 * which], PS[5 + 2 * which]
                inproj_fm(k, pa, 64, wa, h * 64, j)
                inproj_fm(k, pb, 64, wb_, h * 64, j)
                a1, a2 = t1[which], t2[which]
                P.tt(a1.h[:], pa.h[0:64, :], k.COS.h[0:64, js], ALU.mult, [pa, k.COS], [a1])
                P.tt(a2.h[:], pb.h[0:64, :], k.SIN.h[0:64, js], ALU.mult, [pb, k.SIN], [a2])
                yield
                P.tt(a1.h[:], a1.h[:], a2.h[:], ALU.add, [a1, a2], [a1])
                if which == 0:
                    P.tt(dst.h[:, js], a1.h[:], c["gq"].h[:, h, :], ALU.mult, [a1, c["gq"]], [dst])
                else:
                    gkb = c["gk"].h[:, h:h + 1, :].broadcast_to([64, 4, 128])
                    P.tt(dst.h[:, js].rearrange("p (a b) -> p a b", a=4), a1.h[:].rearrange("p (a b) -> p a b", a=4), gkb,
                         ALU.mult, [a1, c["gk"]], [dst])
                yield

    def attn(h):
        qt, kt = QT[h % 2], KT[h % 2]
        blocks = [(qb, kb) for qb in range(4) for kb in range(4 * qb + 4)]

        def emitS(i):
            qb, kb = blocks[i]
            S, a = PS[i % 2], A[i % 3]
            qs = slice(qb * 512, (qb + 1) * 512)
            P.mm(S.h[:, :], [(kt.h[:, kb * 128:(kb + 1) * 128], qt.h[:, qs])], [kt, qt], [S])
            bf = (gam[h] ** (512 * qb - 128 * kb)) * (64.0 ** -0.5)
            if kb < 4 * qb:
                P.act(a.h[:], S.h[:], AF.Copy, [S], [a], scale=float(bf))
            else:
                v = kb - 4 * qb
                P.stt(a.h[:], S.h[:], float(bf), c["cmask16"].h[:, v, :], ALU.mult, ALU.mult, [S, c["cmask16"]], [a])

        emitS(0)
        for i, (qb, kb) in enumerate(blocks):
            O = PS[2 + qb % 2]
            qs = slice(qb * 512, (qb + 1) * 512)
            nkb = 4 * qb + 4
            if i + 1 < len(blocks):
                emitS(i + 1)
            P.mm(O.h[0:64, :], [(Vtok.h[:, kb, h * 64:(h + 1) * 64], A[i % 3].h[:])], [Vtok, A[i % 3]], acc=[O],
                 start=(kb == 0), stop=(kb == nkb - 1))
            if kb % 2 == 1 and kb != nkb - 1:
                yield
            if kb != nkb - 1:
                continue

            def post(O=O, qb=qb, qs=qs, h=h):
                ob = osb[qb % 2]
                P.cp(ob.h[:], O.h[0:64, :], [O], [ob], eng="act")
                ln_feat(k, ob, 64, k.ones64, PS[6], PS[7], tmp, None)
                mean, rstd = tmp[0], tmp[1]
                P.tt(ob.h[:], ob.h[:], mean.h[:], ALU.subtract, [ob, mean], [ob])
                P.stt(ob.h[:], ob.h[:], prg.h[:, h, l:l + 1], rstd.h[:], ALU.mult, ALU.mult, [ob, prg, rstd], [ob])
                P.ts(ob.h[:], ob.h[:], prb.h[:, h, l:l + 1], None, ALU.add, None, [ob, prb], [ob])
                pg = PS[4]
                inproj_fm(k, pg, 64, wg, h * 64, qb)
                P.act(sgs.h[:], pg.h[0:64, :], AF.Silu, [pg], [sgs])
                p0 = (h % 2) * 64
                P.tt(k.YT[3].h[p0:p0 + 64, h // 2, qs], ob.h[:], sgs.h[:], ALU.mult, [ob, sgs], [k.rYT[3][h // 2][qb]])
            if pend[0] is not None:
                pend[0]()
            pend[0] = post
            yield

    interleave([gen(0)])
    for h in range(4):
        gs = [attn(h)]
        if h + 1 < 4:
            gs.append(gen(h + 1))
        interleave(gs)
    if pend[0] is not None:
        pend[0]()


def host_shared(inp):
    sh = {}
    for nm, v in host_consts().items():
        sh["c_" + nm] = v
    for nm, v in host_params(inp).items():
        sh["p_" + nm] = v
    sh.update(host_weights(inp))
    return sh


def core_inputs(inp, b, shared):
    m = dict(shared)
    m["x"] = np.ascontiguousarray(inp["x"][b])
    m["pos"] = np.ascontiguousarray(inp["positions"][b:b + 1]).astype(np.int32)
    m["p"] = np.ascontiguousarray(inp["p"][:, b])
    return m


def kernel(**inputs):
    inp = {kk: np.asarray(v) for kk, v in inputs.items()}
    nc, k = build({"nl": NL})
    shared = host_shared(inp)
    in_maps = []
    for b in range(8):
        m = core_inputs(inp, b, shared)
        in_maps.append({kk: v for kk, v in m.items() if kk in k.dram})
    res = run_bass_kernel_spmd(nc, in_maps, core_ids=list(range(8)))
    out = np.stack([np.asarray(r["y"]) for r in res.results], axis=0)
    return out.astype(np.float32)
```

```python
import math
from contextlib import ExitStack
import numpy as np
import concourse.bass as bass
import concourse.mybir as mybir
from concourse.bass_utils import run_bass_kernel_spmd

F32 = mybir.dt.float32
BF16 = mybir.dt.bfloat16
I32 = mybir.dt.int32
AF = mybir.ActivationFunctionType
ALU = mybir.AluOpType

T = 2048
D = 1024
NL = 4
DFF = 3584
NE = 8
ALPHA = (2 * NL) ** 0.25
EPS = 1e-5
MAGIC = 12582912.0
TWO_PI = 2.0 * math.pi
SB_BASE = 16512
SB_TOP = 229344
NXC = 3328
C_U, C_CQ, C_CKV, C_KPE, C_KPER = 0, 256, 448, 576, 672
C_HQ, C_HF, C_HI, C_HG = 768, 1024, 1280, 1536
C_RQ, C_RK, C_RV, C_RG, C_RQP, C_RKP = 1792, 2048, 2304, 2560, 2816, 3072


class Reg:
    __slots__ = ("name", "last_w", "readers", "excl")

    def __init__(self, name=""):
        self.name = name
        self.last_w = None
        self.readers = []
        self.excl = False


class DSem:
    __slots__ = ("sem", "count")

    def __init__(self, sem):
        self.sem = sem
        self.count = 0


class Buf:
    def __init__(self, h, name):
        self.h = h
        self.reg = Reg(name)
        self.dsem = None

    def __getitem__(self, idx):
        return self.h[idx]


class Op:
    __slots__ = ("eng", "fn", "reads", "writes", "acc", "dsem", "idx", "signal", "waits", "mark", "phase", "ninst")

    def __init__(self, eng, fn, reads, writes, acc, dsem):
        self.eng = eng
        self.fn = fn
        self.reads = reads
        self.writes = writes
        self.acc = acc
        self.dsem = dsem
        self.signal = None
        self.waits = []
        self.mark = False


def _regs(lst):
    out = []
    for x in lst:
        if x is None:
            continue
        if isinstance(x, (list, tuple)):
            out.extend(_regs(x))
        elif isinstance(x, Reg):
            out.append(x)
        else:
            out.append(x.reg)
    return out


class Prog:
    ENGS = ("pe", "act", "dve", "pool", "sync")

    def __init__(self, nc):
        self.nc = nc
        self.ops = []
        self.es = ExitStack()
        self.n = 0
        self.off = SB_BASE
        self.extra_reads = []
        self.regions = None

    def sb(self, shape, dtype, name=None):
        self.n += 1
        name = name or f"t{self.n}"
        nbytes = int(np.prod(shape[1:])) * (2 if dtype == BF16 else 4)
        nbytes = (nbytes + 31) // 32 * 32
        if self.regions is None:
            assert self.off + nbytes <= SB_TOP, f"SBUF overflow allocating {name}"
            off = self.off
            self.off += nbytes
        else:
            for rg in self.regions:
                if rg[0] + nbytes <= rg[1]:
                    off = rg[0]
                    rg[0] += nbytes
                    break
            else:
                raise AssertionError(f"SBUF arena overflow allocating {name} ({nbytes} B): {self.regions}")
        h = self.nc.alloc_sbuf_tensor_at(f"{name}_{self.n}", list(shape), dtype, offset=off)
        return Buf(h, name)

    def set_regions(self, regs):
        self.regions = [list(r) for r in regs] if regs is not None else None

    def ps(self, name):
        h = self.es.enter_context(self.nc.psum_tensor(name, [128, 512], F32))
        b = Buf(h, name)
        b.reg.excl = True
        return b

    def new_dsem(self):
        self.n += 1
        return DSem(self.es.enter_context(self.nc.semaphore(f"ds{self.n}")))

    def op(self, eng, fn, reads=(), writes=(), acc=(), dsem=None, arena=True):
        rd = _regs(reads)
        if arena:
            rd = rd + self.extra_reads
        o = Op(eng, fn, rd, _regs(writes), _regs(acc), dsem)
        o.phase = getattr(self, "phase", "")
        o.ninst = 0
        o.idx = len(self.ops)
        self.ops.append(o)
        return o

    def dma(self, out, in_, reads=(), writes=(), dsem=None, eng="sync", arena=True, **kw):
        assert dsem is not None
        return self.op(eng, lambda e: e.dma_start(out=out, in_=in_, **kw), reads, writes, (), dsem, arena)

    def fence(self, reg, extra_writes=()):
        self.op("pool", lambda e: e.memset(self.fence_scratch[0:1, 0:1], 0.0), [], [reg, self.fence_scratch] + list(extra_writes),
                arena=False)

    def finalize(self, final_dsems=()):
        nc = self.nc
        ops = self.ops
        deps = [None] * len(ops)
        for o in ops:
            d = set()
            for r in o.reads:
                if r.last_w is not None:
                    d.add(r.last_w)
                if r.excl:
                    d.update(i for i in r.readers if ops[i].eng != o.eng)
            for w in o.writes:
                if w.last_w is not None:
                    d.add(w.last_w)
                d.update(w.readers)
            for w in o.acc:
                if w.last_w is not None and ops[w.last_w].eng != o.eng:
                    d.add(w.last_w)
                d.update(w.readers)
            d.discard(o.idx)
            deps[o.idx] = d
            for r in o.reads:
                r.readers.append(o.idx)
            for w in o.writes:
                w.last_w = o.idx
                w.readers = []
            for w in o.acc:
                w.last_w = o.idx
                w.readers = []
            for i in d:
                ops[i].mark = True
        esem = {en: self.es.enter_context(nc.semaphore("sem_" + en)) for en in ("pe", "act", "dve", "pool")}
        ecount = {en: 0 for en in esem}
        seen = {en: {} for en in self.ENGS}
        for o in ops:
            need = {}
            for i in deps[o.idx]:
                p = ops[i]
                if p.dsem is not None:
                    key, sem, val = id(p.dsem), p.dsem.sem, 16 * p.dsem.count
                else:
                    key, sem, val = p.eng, esem[p.eng], p.signal
                if key not in need or need[key][1] < val:
                    need[key] = (sem, val)
            sn = seen[o.eng]
            for key, (sem, val) in need.items():
                if sn.get(key, 0) >= val:
                    continue
                sn[key] = val
                o.waits.append((sem, val))
            if o.dsem is not None:
                o.dsem.count += 1
            elif o.mark:
                ecount[o.eng] += 1
                o.signal = ecount[o.eng]
        finals = [(d.sem, 16 * d.count) for d in final_dsems if d.count > 0]
        engmap = {"pe": "tensor", "act": "scalar", "dve": "vector", "pool": "gpsimd", "sync": "sync"}
        with nc.Block() as block:
            for en in self.ENGS:
                myops = [o for o in ops if o.eng == en]

                def body(e, myops=myops, en=en):
                    for o in myops:
                        for sem, val in o.waits:
                            e.wait_ge(sem, val)
                        n0 = nc.n_instructions()
                        ins = o.fn(e)
                        o.ninst = nc.n_instructions() - n0
                        if o.dsem is not None:
                            ins.then_inc(o.dsem.sem, 16)
                        elif o.mark:
                            ins.then_inc(esem[en], 1)
                    if en == "sync":
                        for sem, val in finals:
                            e.wait_ge(sem, val)
                getattr(block, engmap[en])(body)
        self.stats = {en: sum(1 for o in ops if o.eng == en) for en in self.ENGS}
        self.stats["marks"] = dict(ecount)
        self.es.close()

    def mm(self, out, pairs, reads, writes=(), acc=(), start=True, stop=True):
        pairs = list(pairs)

        def fn(e):
            n = len(pairs)
            ins = None
            for i, (l, r) in enumerate(pairs):
                ins = e.matmul(out, l, r, start=(start and i == 0), stop=(stop and i == n - 1))
            return ins
        return self.op("pe", fn, reads, writes, acc)

    def tr(self, out, in_, ident, reads, writes):
        return self.op("pe", lambda e: e.transpose(out, in_, ident), reads, writes)

    def act(self, out, in_, func, reads, writes, scale=None, bias=None, eng="act"):
        kw = {}
        if scale is not None:
            kw["scale"] = scale
        if bias is not None:
            kw["bias"] = bias
        return self.op(eng, lambda e: e.activation(out=out, in_=in_, func=func, **kw), reads, writes)

    def tt(self, out, in0, in1, op, reads, writes, eng="dve"):
        return self.op(eng, lambda e: e.tensor_tensor(out=out, in0=in0, in1=in1, op=op), reads, writes)

    def ts(self, out, in0, s1, s2, op0, op1, reads, writes, eng="dve"):
        if op1 is None:
            return self.op(eng, lambda e: e.tensor_scalar(out=out, in0=in0, scalar1=s1, scalar2=None, op0=op0), reads, writes)
        return self.op(eng, lambda e: e.tensor_scalar(out=out, in0=in0, scalar1=s1, scalar2=s2, op0=op0, op1=op1), reads, writes)

    def stt(self, out, in0, scalar, in1, op0, op1, reads, writes):
        return self.op("dve", lambda e: e.scalar_tensor_tensor(out=out, in0=in0, scalar=scalar, in1=in1, op0=op0, op1=op1), reads, writes)

    def cp(self, out, in_, reads, writes, eng="dve"):
        if eng == "act":
            return self.op("act", lambda e: e.activation(out=out, in_=in_, func=AF.Copy), reads, writes)
        return self.op(eng, lambda e: e.tensor_copy(out=out, in_=in_), reads, writes)

    def scan(self, out, d0, d1, init, reads, writes):
        return self.op("dve", lambda e: e.tensor_tensor_scan(out=out, data0=d0, data1=d1, initial=init, op0=ALU.mult, op1=ALU.add), reads, writes)


def _gammas():
    return [1.0 - 2.0 ** (-5.0 - h) for h in range(4)]


def host_consts():
    c = {}
    c["ident"] = np.eye(128, dtype=np.float32)
    s = np.arange(128)[:, None]
    t = np.arange(512)[None, :]
    c["cmask"] = np.stack([(t >= s + 128 * v).astype(np.float32) for v in range(4)], axis=1)
    ss = np.arange(128)[:, None]
    tt = np.arange(128)[None, :]
    c["bdmask"] = ((ss // 16 == tt // 16) & (ss <= tt)).astype(np.float32)
    n = np.arange(8)[None, :, None]
    c["cexp"] = np.broadcast_to((ss[:, :, None] // 16 == n), (128, 8, 64)).astype(np.float32).copy()
    c["iota512"] = np.broadcast_to(np.arange(512, dtype=np.float32)[None, :], (128, 512)).copy()
    c["rst16"] = np.broadcast_to((np.arange(512) % 16 != 0).astype(np.float32)[None, :], (128, 512)).copy()
    rc = np.zeros((128, 2), np.float32)
    for r in range(64):
        j = r % 32
        rc[r, 0] = (10000.0 ** (-j / 32.0)) / TWO_PI
        rc[r, 1] = -TWO_PI if r < 32 else TWO_PI
    for r in range(64, 96):
        j = (r - 64) % 16
        rc[r, 0] = (10000.0 ** (-j / 16.0)) / TWO_PI
        rc[r, 1] = -TWO_PI if r < 80 else TWO_PI
    c["ropec"] = rc
    g = _gammas()
    gq = np.zeros((64, 4, 512), np.float32)
    gk = np.zeros((64, 4, 128), np.float32)
    for h in range(4):
        gq[:, h, :] = (g[h] ** np.arange(512, dtype=np.float64))[None, :]
        gk[:, h, :] = (g[h] ** (-np.arange(128, dtype=np.float64)))[None, :]
    c["gq"] = gq
    c["gk"] = gk
    gm = np.zeros((128, 8), np.float32)
    for r in range(128):
        gm[r, r // 16] = 1.0
    c["grpmask"] = gm
    sg = np.ones((128, 2), np.float32)
    sg[64:, 0] = -1.0
    sg[:, 1] = -1.0
    c["sgn"] = sg
    sel = np.zeros((64, 8, 128), np.float32)
    for e in range(8):
        sel[e, e, :] = 1.0
        sel[32 + e, e, :] = 1.0
    c["sel"] = sel
    sw = np.zeros((128, 128), np.float32)
    for r in range(128):
        sw[r, (r + 64) % 128] = 1.0
    c["swapid"] = sw
    return c


def host_weights(inp):
    w = {}
    w_in = inp["w_in"]
    wx = np.zeros((NL, D, NXC), np.float32)
    wx[:, :, 0:576] = w_in[:, :, 0:576]
    kr = w_in[:, :, 576:608]
    wx[:, :, C_KPE + 64:C_KPE + 96] = kr
    wx[:, :, C_KPER + 64:C_KPER + 80] = kr[:, :, 16:32]
    wx[:, :, C_KPER + 80:C_KPER + 96] = kr[:, :, 0:16]
    wx[:, :, C_HQ:C_HQ + 1024] = w_in[:, :, 608:1632]
    wx[:, :, C_RQ:C_RQ + 1024] = w_in[:, :, 1632:2656]
    for (src, dst) in ((1632, C_RQP), (1888, C_RKP)):
        blk = w_in[:, :, src:src + 256].reshape(NL, D, 4, 2, 32)
        wx[:, :, dst:dst + 256] = blk[:, :, :, ::-1, :].reshape(NL, D, 256)
    w["wx"] = wx
    w["wg"] = np.ascontiguousarray(w_in[:, :, 2656:])
    uq = inp["mla_w_uq"]
    uqr = np.zeros_like(uq)
    for h in range(4):
        uqr[:, :, h * 96 + 64:h * 96 + 80] = uq[:, :, h * 96 + 80:h * 96 + 96]
        uqr[:, :, h * 96 + 80:h * 96 + 96] = uq[:, :, h * 96 + 64:h * 96 + 80]
    def k2(a):
        o = np.zeros((NL, 128, 2, a.shape[2]), np.float32)
        o[:, :, 0, :] = a[:, 0:128, :]
        o[:, 0:64, 1, :] = a[:, 128:192, :]
        return o
    w["uq"] = k2(uq)
    w["uqr"] = k2(uqr)
    w["ukv"] = inp["mla_w_ukv"]
    are, aim, ldt = inp["s5_a_re"], inp["s5_a_im"], inp["s5_log_dt"]
    bre, bim = inp["s5_b_re"], inp["s5_b_im"]
    cre, cim = inp["s5_c_re"], inp["s5_c_im"]
    aB = np.zeros((NL, 128, 2, 2, 64), np.float32)
    bT = np.zeros((NL, 128, 2, 2, 64), np.float32)
    ldB = np.zeros((NL, 128, 2), np.float32)
    for g in range(16):
        cc, gi = g // 8, g % 8
        rows = slice(gi * 16, gi * 16 + 16)
        aB[:, rows, cc, 0, :] = are[:, g, None, :]
        aB[:, rows, cc, 1, :] = aim[:, g, None, :]
        ldB[:, rows, cc] = ldt[:, g, None]
        bT[:, rows, cc, 0, :] = bre[:, g].transpose(0, 2, 1)
        bT[:, rows, cc, 1, :] = bim[:, g].transpose(0, 2, 1)
    w["s5aB"], w["s5bT"], w["s5ldB"] = aB, bT, ldB
    aC = np.zeros((NL, 128, 2, 16), np.float32)
    aC[:, 0:64, 0, :] = are.transpose(0, 2, 1)
    aC[:, 64:128, 0, :] = are.transpose(0, 2, 1)
    aC[:, 0:64, 1, :] = aim.transpose(0, 2, 1)
    aC[:, 64:128, 1, :] = aim.transpose(0, 2, 1)
    w["s5aC"] = aC
    w["s5ldC"] = np.ascontiguousarray(np.broadcast_to(ldt[:, None, :], (NL, 128, 16)))
    C1 = np.zeros((NL, 128, 16, 128), np.float32)
    C2 = np.zeros((NL, 128, 16, 128), np.float32)
    for g in range(16):
        gi = g % 8
        cols = slice(gi * 16, gi * 16 + 16)
        C1[:, 0:64, g, cols] = cre[:, g].transpose(0, 2, 1)
        C1[:, 64:128, g, cols] = cim[:, g].transpose(0, 2, 1)
        C2[:, 0:64, g, cols] = cim[:, g].transpose(0, 2, 1)
        C2[:, 64:128, g, cols] = cre[:, g].transpose(0, 2, 1)
    w["s5C1"], w["s5C2"] = C1, C2
    w["s5glu"] = inp["s5_w_glu"]
    w["wbr"] = inp["w_branch"]
    w["wo"] = inp["w_o"]
    w["ffg"], w["ffu"], w["ffd"] = inp["ff_w_gate"], inp["ff_w_up"], inp["ff_w_down"]
    w["mog"], w["mou"], w["mod"], w["mor"] = inp["moe_w_gate"], inp["moe_w_up"], inp["moe_w_down"], inp["moe_router"]
    w["plg"], w["plp"] = inp["ple_w_gate"], inp["ple_w_proj"]
    return w


class K:
    pass


def build(cfg):
    nc = bass.Bass("TRN2", target_bir_lowering=False)
    P = Prog(nc)
    k = K()
    k.P, k.nc, k.cfg = P, nc, cfg
    k.dram = {}

    def din(name, shape, dtype=F32):
        k.dram[name] = nc.dram_tensor(name, list(shape), dtype, kind="ExternalInput").ap()
        return k.dram[name]

    def dout(name, shape, dtype=F32):
        k.dram[name] = nc.dram_tensor(name, list(shape), dtype, kind="ExternalOutput").ap()
        return k.dram[name]

    din("x", [T, D])
    din("pos", [1, T], I32)
    din("wx", [NL, D, NXC])
    din("uq", [NL, 128, 2, 384])
    din("uqr", [NL, 128, 2, 384])
    din("ukv", [NL, 128, 512])
    din("s5aB", [NL, 128, 2, 2, 64])
    din("s5bT", [NL, 128, 2, 2, 64])
    din("s5ldB", [NL, 128, 2])
    din("s5aC", [NL, 128, 2, 16])
    din("s5ldC", [NL, 128, 16])
    din("s5C1", [NL, 128, 16, 128])
    din("s5C2", [NL, 128, 16, 128])
    din("s5glu", [NL, 256, 256])
    din("wg", [NL, D, 4096])
    din("wbr", [NL, 4, 256, D])
    din("wo", [NL, D, D])
    din("ffg", [2, D, DFF]); din("ffu", [2, D, DFF]); din("ffd", [2, DFF, D])
    din("mog", [2, NE, D, DFF]); din("mou", [2, NE, D, DFF]); din("mod", [2, NE, DFF, D]); din("mor", [2, D, NE])
    din("plg", [NL, D, D]); din("plp", [NL, 256, D])
    din("p", [NL, T, 256])
    hc = host_consts()
    for nm, v in hc.items():
        din("c_" + nm, v.shape)
    dout("y", [T, D])
    k.taps = {}
    k.out_dsems = []

    k.XTb = P.sb([128, 8, T], BF16, "XTb")
    k.rXT = [[Reg(f"XT{c}_{j}") for j in range(4)] for c in range(8)]
    k.rXTb = [[Reg(f"XTb{c}_{j}") for j in range(4)] for c in range(8)]
    P.fence_scratch = P.sb([128, 8], F32, "fsc")
    k.ARENA = Reg("arena")
    P.extra_reads = [k.ARENA]
    k.PS = [P.ps(f"ps{i}") for i in range(8)]
    k.c = {}
    ds_c = P.new_dsem()
    ds_c2 = P.new_dsem()
    for nm, v in hc.items():
        if nm in ("cmask", "bdmask", "cexp", "sel"):
            b16 = P.sb(list(v.shape), BF16, nm + "16")
            P.dma(b16.h[:], k.dram["c_" + nm], writes=[b16], dsem=ds_c2, eng="pool", arena=False)
            k.c[nm + "16"] = b16
        else:
            k.c[nm] = P.sb(list(v.shape), F32, "c_" + nm)
            P.dma(k.c[nm].h[:], k.dram["c_" + nm], writes=[k.c[nm]], dsem=ds_c, arena=False)
    k.ones64 = P.sb([64, 64], F32, "ones64")
    P.op("dve", lambda e: e.memset(k.ones64.h[:], 1.0 / 64.0), [], [k.ones64])
    k.ones128 = P.sb([128, 128], F32, "ones128")
    P.op("dve", lambda e: e.memset(k.ones128.h[:], 1.0), [], [k.ones128])
    k.ones16 = P.sb([128, 128], BF16, "ones16")
    P.op("dve", lambda e: e.memset(k.ones16.h[:], 1.0), [], [k.ones16])
    k.prm = {}
    for nm, shp in PRM_SHAPES.items():
        din("p_" + nm, shp)
        k.prm[nm] = P.sb(list(shp), F32, "p_" + nm)
        P.dma(k.prm[nm].h[:], k.dram["p_" + nm], writes=[k.prm[nm]], dsem=ds_c, arena=False)
    hgrn_setup(k)
    k.xt_off = P.off
    k.XT = P.sb([128, 8, T], F32, "XT")
    k.up_off = P.off
    k.spill = nc.dram_tensor("xt_spill", [128, 8, T], F32, kind="Internal").ap()
    k.rSP = [Reg(f"sp{c}") for c in range(8)]
    k.ds_sp = P.new_dsem()
    k.ds_tap = P.new_dsem()
    k.out_dsems.append(k.ds_tap)

    if not cfg.get("skip_x"):
        load_x(k)
    nl = cfg.get("nl", 0)
    for l in range(nl):
        layer(k, l)
    if not cfg.get("skip_x") and not cfg.get("skip_store"):
        store_out(k)
    P.finalize(final_dsems=k.out_dsems)
    return nc, k


def tap(k, name, buf_ap, shape, reads, dtype=F32):
    P = k.P
    if not k.cfg.get("taps"):
        return
    d = k.nc.dram_tensor("tap_" + name, list(shape), dtype, kind="ExternalOutput").ap()
    P.dma(d, buf_ap, reads=reads, dsem=k.ds_tap)
    k.taps[name] = d


def load_x(k):
    P = k.P
    mark = P.off
    P.off = k.up_off
    xin = [P.sb([128, D], F32, f"xin{i}") for i in range(2)]
    for b in xin:
        b.dsem = P.new_dsem()
    ident = k.c["ident"]
    for i in range(16):
        xb = xin[i % 2]
        P.dma(xb.h[:], k.dram["x"][i * 128:(i + 1) * 128, :], writes=[xb], dsem=xb.dsem)
        j = i // 4
        for half in range(2):
            ps = k.PS[(2 * i + half) % 4]
            def fn(e, ps=ps, xb=xb, half=half):
                ins = None
                for q in range(4):
                    c = half * 4 + q
                    ins = e.transpose(ps.h[:, q * 128:(q + 1) * 128], xb.h[:, c * 128:(c + 1) * 128], ident.h[:])
                return ins
            P.op("pe", fn, [xb, ident], [ps])
            src = ps.h[:].rearrange("p (c t) -> p c t", c=4)
            cs = slice(half * 4, half * 4 + 4)
            regs32 = [k.rXT[c][j] for c in range(half * 4, half * 4 + 4)]
            regs16 = [k.rXTb[c][j] for c in range(half * 4, half * 4 + 4)]
            P.cp(k.XT.h[:, cs, i * 128:(i + 1) * 128], src, [ps], regs32, eng="dve")
            P.cp(k.XTb.h[:, cs, i * 128:(i + 1) * 128], src, [ps], regs16, eng="act")
        if i % 4 == 3 and k.cfg.get("nl", 0) > 0:
            js = slice(j * 512, (j + 1) * 512)
            P.dma(k.spill[:, :, js], k.XT.h[:, :, js], reads=[k.rXT[c][j] for c in range(8)], writes=k.rSP, dsem=k.ds_sp, arena=False)
            k.spilled0 = True
    P.off = mark


def store_out(k):
    P = k.P
    mark = P.off
    P.off = k.up_off
    yo = [P.sb([128, D], F32, f"yo{i}") for i in range(2)]
    ident = k.c["ident"]
    for b in yo:
        b.dsem = P.new_dsem()
        k.out_dsems.append(b.dsem)
    for i in range(16):
        ob = yo[i % 2]
        j = i // 4
        for half in range(2):
            ps = k.PS[(2 * i + half) % 4]

            def fn(e, ps=ps, half=half, i=i):
                ins = None
                for q in range(4):
                    c = half * 4 + q
                    ins = e.transpose(ps.h[:, q * 128:(q + 1) * 128], k.XT.h[:, c, i * 128:(i + 1) * 128], ident.h[:])
                return ins
            P.op("pe", fn, [k.rXT[c][j] for c in range(half * 4, half * 4 + 4)] + [ident], [ps])
            P.cp(ob.h[:, half * 512:(half + 1) * 512], ps.h[:], [ps], [ob], eng=("dve" if half == 0 else "act"))
        P.dma(k.dram["y"][i * 128:(i + 1) * 128, :], ob.h[:], reads=[ob], dsem=ob.dsem)
    P.off = mark


PRM_SHAPES = {
    "retg": [64, 4, NL], "retb": [64, 4, NL],
    "qn": [128, 2, NL], "kvn": [128, NL],
    "s5d": [128, 2, NL],
    "lbraw": [64, 4, NL], "hgn": [64, 4, NL],
    "ln1g": [128, 8, NL], "ln1b": [128, 8, NL], "ln2g": [128, 8, NL], "ln2b": [128, 8, NL],
}


def host_params(inp):
    p = {}
    def hl(a):
        return np.ascontiguousarray(a.reshape(NL, 4, 64).transpose(2, 1, 0))
    p["retg"] = hl(inp["ret_gn_g"])
    p["retb"] = hl(inp["ret_gn_b"])
    qn = np.zeros((128, 2, NL), np.float32)
    qn[:, 0, :] = inp["mla_q_norm"][:, 0:128].T
    qn[0:64, 1, :] = inp["mla_q_norm"][:, 128:192].T
    p["qn"] = qn
    p["kvn"] = np.ascontiguousarray(inp["mla_kv_norm"].T)
    for nm in ("ln1_g", "ln1_b", "ln2_g", "ln2_b"):
        p[nm.replace("_", "")] = np.ascontiguousarray(inp[nm].reshape(NL, 8, 128).transpose(2, 1, 0))
    p["lbraw"] = hl(inp["hg_lb_raw"])
    p["hgn"] = hl(inp["hg_norm"])
    p["s5d"] = np.ascontiguousarray(inp["s5_d"].reshape(NL, 2, 128).transpose(2, 1, 0))
    return p


def interleave(gens):
    gens = list(gens)
    while gens:
        for g in list(gens):
            try:
                next(g)
            except StopIteration:
                gens.remove(g)


def wb_load(k, wb, l, c0, W, src="wx"):
    P = k.P
    ap = k.dram[src][l, :, c0:c0 + W].rearrange("(kk p) w -> p kk w", p=128)
    P.dma(wb.h[:, :, 0:W], ap, writes=[wb], dsem=wb.dsem, eng="pool", arena=False)


def xtb_regs(k, j):
    return [k.rXTb[kk][j] for kk in range(8)]


def inproj_fm(k, ps, M, wb, col, j):
    pairs = [(wb.h[:, kk, col:col + M], k.XTb.h[:, kk, j * 512:(j + 1) * 512]) for kk in range(8)]
    k.P.mm(ps.h[0:M, :], pairs, [wb] + xtb_regs(k, j), [ps])


def inproj_tm(k, ps, N, wb, col, i):
    pairs = [(k.XTb.h[:, kk, i * 128:(i + 1) * 128], wb.h[:, kk, col:col + N]) for kk in range(8)]
    k.P.mm(ps.h[:, 0:N], pairs, [wb] + xtb_regs(k, i // 4), [ps])


MIX_LOADS = {
    "ret": [(0, C_RQ, 256), (1, C_RK, 256), (2, C_RQP, 256), (3, C_RKP, 256)],
    "mla": [(0, C_CQ, 192), (1, C_CKV, 128), (2, C_KPE, 192)],
    "s5": [(0, C_U, 256)],
    "hg": [(0, C_HQ, 256), (1, C_HF, 256), (2, C_HI, 256), (3, C_HG, 256)],
}


def mixer_loads(k, l, name):
    if not k.cfg.get(name, 1):
        return
    for (i, c0, W) in MIX_LOADS[name]:
        wb_load(k, k.WB[i], l, c0, W)


def layer(k, l):
    P, cfg = k.P, k.cfg
    PS = k.PS
    if l == 0 and not getattr(k, "spilled0", False):
        for c in range(8):
            P.dma(k.spill[:, c, :], k.XT.h[:, c, :], reads=k.rXT[c], writes=[k.rSP[c]], dsem=k.ds_sp, arena=False)
    P.fence(k.ARENA, [r for row in k.rXT for r in row])
    P.off = k.up_off
    k.YT = [P.sb([128, 2, T], BF16, f"YT{m}") for m in range(4)]
    k.rYT = [[[Reg(f"YT{m}_{c}_{j}") for j in range(4)] for c in range(2)] for m in range(4)]
    if not hasattr(k, "WB"):
        k.WB = [P.sb([128, 8, 256], BF16, f"WB{i}") for i in range(4)]
        for b in k.WB:
            b.dsem = P.new_dsem()
        k.up2_off = P.off
        mixer_loads(k, l, "ret")
    elif not cfg.get("xpref", 0):
        mixer_loads(k, l, "ret")
    P.set_regions([(k.xt_off, k.up_off)])
    rope_tables(k)
    k.mix_regs = [(P.regions[0][0], k.up_off), (k.up2_off, SB_TOP)]
    if cfg.get("ret", 1):
        P.set_regions(k.mix_regs)
        P.phase = f"L{l}.ret"
        ret_mixer(k, l)
        tap(k, f"y_d{l}", k.YT[3].h[:], [128, 2, T], [r for cc in k.rYT[3] for r in cc], BF16)
    mixer_loads(k, l, "mla")
    if cfg.get("mla", 1):
        P.fence(k.ARENA)
        P.set_regions(k.mix_regs)
        P.phase = f"L{l}.mla"
        mla_mixer(k, l)
        tap(k, f"y_b{l}", k.YT[1].h[:], [128, 2, T], [r for cc in k.rYT[1] for r in cc], BF16)
    k.mix_regs2 = [(k.xt_off, k.up_off), (k.up2_off, SB_TOP)]
    mixer_loads(k, l, "s5")
    if cfg.get("s5", 1):
        P.fence(k.ARENA)
        P.set_regions(k.mix_regs2)
        P.phase = f"L{l}.s5"
        s5_mixer(k, l)
        tap(k, f"y_a{l}", k.YT[0].h[:], [128, 2, T], [r for cc in k.rYT[0] for r in cc], BF16)
    mixer_loads(k, l, "hg")
    if cfg.get("hg", 1):
        P.fence(k.ARENA)
        P.set_regions(k.mix_regs2)
        P.phase = f"L{l}.hg"
        hgrn_mixer(k, l)
        tap(k, f"y_c{l}", k.YT[2].h[:], [128, 2, T], [r for cc in k.rYT[2] for r in cc], BF16)
    P.set_regions(None)
    if cfg.get("gate", 1):
        P.fence(k.ARENA)
        P.phase = f"L{l}.gate"
        gate_phase(k, l)
        tap(k, f"x1{l}", k.XT.h[:], [128, 8, T], [r for row in k.rXT for r in row])
    if cfg.get("ffn", 1):
        P.fence(k.ARENA)
        P.phase = f"L{l}.ffn"
        ffn_phase(k, l)
        tap(k, f"x2{l}", k.XT.h[:], [128, 8, T], [r for row in k.rXT for r in row])
    if l + 1 < cfg.get("nl", 0) and cfg.get("xpref", 0):
        mixer_loads(k, l + 1, "ret")


def rope_tables(k):
    P, c = k.P, k.c
    k.COS = P.sb([128, T], F32, "COS")
    k.SIN = P.sb([128, T], F32, "SIN")
    save = [list(r) for r in P.regions]
    posi = P.sb([128, T], I32, "posi")
    if not hasattr(k, "ds_pos"):
        k.ds_pos = P.new_dsem()
    x0 = P.sb([128, T], F32, "rx0")
    x1 = P.sb([128, T], F32, "rx1")
    P.dma(posi.h[:], k.dram["pos"].broadcast_to([128, T]), writes=[posi], dsem=k.ds_pos)
    P.cp(x0.h[:], posi.h[:], [posi], [x0])
    rc = c["ropec"]
    P.ts(x0.h[:], x0.h[:], rc.h[:, 0:1], None, ALU.mult, None, [x0, rc], [x0])
    P.ts(x1.h[:], x0.h[:], MAGIC, MAGIC, ALU.add, ALU.subtract, [x0], [x1])
    P.tt(x1.h[:], x0.h[:], x1.h[:], ALU.subtract, [x0, x1], [x1])
    P.act(k.SIN.h[:], x1.h[:], AF.Sin, [x1, rc], [k.SIN], scale=rc.h[:, 1:2])
    P.ts(x0.h[:], x0.h[:], 0.25, None, ALU.add, None, [x0], [x0])
    P.ts(x1.h[:], x0.h[:], MAGIC, MAGIC, ALU.add, ALU.subtract, [x0, k.SIN], [x1])
    P.tt(x1.h[:], x0.h[:], x1.h[:], ALU.subtract, [x0, x1], [x1])
    P.act(k.COS.h[:], x1.h[:], AF.Sin, [x1], [k.COS], scale=TWO_PI)
    P.regions = save


def ln_feat(k, o_sb, npart, ones, ps_a, ps_b, tmp, out_fn):
    P = k.P
    sq, mean, rstd = tmp[2], tmp[0], tmp[1]
    P.tt(sq.h[0:npart, :], o_sb.h[0:npart, :], o_sb.h[0:npart, :], ALU.mult, [o_sb], [sq])
    P.mm(ps_a.h[0:npart, :], [(ones.h[0:npart, 0:npart], o_sb.h[0:npart, :])], [ones, o_sb], [ps_a])
    P.mm(ps_b.h[0:npart, :], [(ones.h[0:npart, 0:npart], sq.h[0:npart, :])], [ones, sq], [ps_b])
    P.cp(mean.h[0:npart, :], ps_a.h[0:npart, :], [ps_a], [mean], eng="act")
    P.tt(sq.h[0:npart, :], mean.h[0:npart, :], mean.h[0:npart, :], ALU.mult, [mean], [sq])
    P.tt(rstd.h[0:npart, :], ps_b.h[0:npart, :], sq.h[0:npart, :], ALU.subtract, [ps_b, sq], [rstd])
    P.act(rstd.h[0:npart, :], rstd.h[0:npart, :], AF.Ln, [rstd], [rstd], bias=EPS)
    P.act(rstd.h[0:npart, :], rstd.h[0:npart, :], AF.Exp, [rstd], [rstd], scale=-0.5)


def rms_bc(k, srcs, nfeat, ps, out_rstd):
    P = k.P
    pairs = []
    rd = [k.ones128]
    for (sq, nr) in srcs:
        pairs.append((k.ones128.h[0:nr, :], sq.h[0:nr, :]))
        rd.append(sq)
    P.mm(ps.h[:, :], pairs, rd, [ps])
    P.act(out_rstd.h[:], ps.h[:], AF.Ln, [ps], [out_rstd], scale=1.0 / nfeat, bias=EPS)
    P.act(out_rstd.h[:], out_rstd.h[:], AF.Exp, [out_rstd], [out_rstd], scale=-0.5)


def mla_mixer(k, l):
    P, PS, c = k.P, k.PS, k.c
    wcq, wckv, wkpe, _ = k.WB
    if not hasattr(k, "ds_mlaw"):
        k.ds_mlaw = P.new_dsem()
    uq32 = P.sb([128, 2, 384], F32, "uq32")
    uqr32 = P.sb([128, 2, 384], F32, "uqr32")
    ukv32 = P.sb([128, 512], F32, "ukv32")
    uq16 = P.sb([128, 2, 384], BF16, "uq16")
    uqr16 = P.sb([128, 2, 384], BF16, "uqr16")
    ukvK = P.sb([128, 256], BF16, "ukvK")
    ukvV = P.sb([128, 256], BF16, "ukvV")
    P.dma(uq32.h[:], k.dram["uq"][l], writes=[uq32], dsem=k.ds_mlaw)
    P.dma(uqr32.h[:], k.dram["uqr"][l], writes=[uqr32], dsem=k.ds_mlaw)
    P.dma(ukv32.h[:], k.dram["ukv"][l], writes=[ukv32], dsem=k.ds_mlaw)
    qn, kvn = k.prm["qn"], k.prm["kvn"]
    sc = 96.0 ** -0.5
    for cc in range(2):
        P.ts(uq16.h[:, cc, :], uq32.h[:, cc, :], qn.h[:, cc, l:l + 1], sc, ALU.mult, ALU.mult, [uq32, qn], [uq16])
        P.ts(uqr16.h[:, cc, :], uqr32.h[:, cc, :], qn.h[:, cc, l:l + 1], sc, ALU.mult, ALU.mult, [uqr32, qn], [uqr16])
    kv4 = ukv32.h[:].rearrange("p (h two d) -> p h two d", h=4, two=2)
    P.ts(ukvK.h[:].rearrange("p (h d) -> p h d", h=4), kv4[:, :, 0, :], kvn.h[:, l:l + 1], None, ALU.mult, None, [ukv32, kvn], [ukvK])
    P.ts(ukvV.h[:].rearrange("p (h d) -> p h d", h=4), kv4[:, :, 1, :], kvn.h[:, l:l + 1], None, ALU.mult, None, [ukv32, kvn], [ukvV])
    cqn = P.sb([128, 2, T], BF16, "cqn")
    ckvn = P.sb([128, T], BF16, "ckvn")
    KPE = P.sb([96, T], BF16, "KPE")
    Vtok = P.sb([128, 16, 256], BF16, "Vtok")
    QT = [P.sb([96, T], BF16, f"mQT{i}") for i in range(2)]
    KT = [P.sb([96, T], BF16, f"mKT{i}") for i in range(2)]
    A = [P.sb([128, 512], BF16, f"mA{i}") for i in range(3)]
    f0 = P.sb([128, 512], F32, "mf0")
    f1 = P.sb([128, 512], F32, "mf1")
    f2 = P.sb([128, 512], F32, "mf2")
    s0 = P.sb([128, 512], F32, "ms0")
    s1 = P.sb([128, 512], F32, "ms1")
    rs = P.sb([128, 512], F32, "mrs")
    ta = [P.sb([96, 512], F32, f"mta{i}") for i in range(2)]
    tb = [P.sb([96, 512], F32, f"mtb{i}") for i in range(2)]
    rec = [P.sb([64, 512], F32, f"mrec{i}") for i in range(2)]
    pend = [None]
    for j in range(4):
        js = slice(j * 512, (j + 1) * 512)
        inproj_fm(k, PS[4], 128, wcq, 0, j)
        inproj_fm(k, PS[5], 64, wcq, 128, j)
        P.cp(f0.h[:], PS[4].h[:], [PS[4]], [f0], eng="act")
        P.cp(f1.h[0:64, :], PS[5].h[0:64, :], [PS[5]], [f1], eng="act")
        P.tt(s0.h[:], f0.h[:], f0.h[:], ALU.mult, [f0], [s0])
        P.tt(s1.h[0:64, :], f1.h[0:64, :], f1.h[0:64, :], ALU.mult, [f1], [s1])
        rms_bc(k, [(s0, 128), (s1, 64)], 192.0, PS[6], rs)
        P.tt(cqn.h[:, 0, js], f0.h[:], rs.h[:], ALU.mult, [f0, rs], [cqn])
        P.tt(cqn.h[0:64, 1, js], f1.h[0:64, :], rs.h[0:64, :], ALU.mult, [f1, rs], [cqn])
        inproj_fm(k, PS[7], 128, wckv, 0, j)
        P.cp(f2.h[:], PS[7].h[:], [PS[7]], [f2], eng="act")
        P.tt(s0.h[:], f2.h[:], f2.h[:], ALU.mult, [f2], [s0])
        rms_bc(k, [(s0, 128)], 128.0, PS[6], rs)
        P.tt(ckvn.h[:, js], f2.h[:], rs.h[:], ALU.mult, [f2, rs], [ckvn])
        inproj_fm(k, PS[4], 96, wkpe, 0, j)
        inproj_fm(k, PS[5], 96, wkpe, 96, j)
        a1, a2 = ta[0], tb[0]
        P.tt(a1.h[64:96, :], PS[4].h[64:96, :], k.COS.h[64:96, js], ALU.mult, [PS[4], k.COS], [a1])
        P.tt(a2.h[64:96, :], PS[5].h[64:96, :], k.SIN.h[64:96, js], ALU.mult, [PS[5], k.SIN], [a2])
        P.tt(KPE.h[64:96, js], a1.h[64:96, :], a2.h[64:96, :], ALU.add, [a1, a2], [KPE])
    for i in range(16):
        ps = PS[4 + i % 2]
        P.mm(ps.h[:, 0:256], [(ckvn.h[:, i * 128:(i + 1) * 128], ukvV.h[:, :])], [ckvn, ukvV], [ps])
        P.cp(Vtok.h[:, i, :], ps.h[:, 0:256], [ps], [Vtok], eng="act")
    def gen(h):
        qt, kt = QT[h % 2], KT[h % 2]
        P.cp(kt.h[64:96, :], KPE.h[64:96, :], [KPE], [kt], eng="dve")
        for j in range(4):
            js = slice(j * 512, (j + 1) * 512)
            P.mm(PS[4].h[0:64, :], [(ukvK.h[:, h * 64:(h + 1) * 64], ckvn.h[:, js])], [ukvK, ckvn], [PS[4]])
            P.cp(kt.h[0:64, js], PS[4].h[0:64, :], [PS[4]], [kt], eng="act")
            yield
            hs = slice(h * 96, (h + 1) * 96)
            P.mm(PS[4].h[0:96, :], [(uq16.h[:, 0, hs], cqn.h[:, 0, js]), (uq16.h[0:64, 1, hs], cqn.h[0:64, 1, js])], [uq16, cqn], [PS[4]])
            P.mm(PS[5].h[0:96, :], [(uqr16.h[:, 0, hs], cqn.h[:, 0, js]), (uqr16.h[0:64, 1, hs], cqn.h[0:64, 1, js])], [uqr16, cqn], [PS[5]])
            a1, a2 = ta[j % 2], tb[j % 2]
            P.cp(qt.h[0:64, js], PS[4].h[0:64, :], [PS[4]], [qt], eng="act")
            P.tt(a1.h[64:96, :], PS[4].h[64:96, :], k.COS.h[64:96, js], ALU.mult, [PS[4], k.COS], [a1])
            P.tt(a2.h[64:96, :], PS[5].h[64:96, :], k.SIN.h[64:96, js], ALU.mult, [PS[5], k.SIN], [a2])
            yield
            P.tt(qt.h[64:96, js], a1.h[64:96, :], a2.h[64:96, :], ALU.add, [a1, a2], [qt])
            yield

    def attn(h):
        qt, kt = QT[h % 2], KT[h % 2]
        blocks = [(qb, kb) for qb in range(4) for kb in range(4 * qb + 4)]

        def emitS(i):
            qb, kb = blocks[i]
            S, a = PS[i % 2], A[i % 3]
            qs = slice(qb * 512, (qb + 1) * 512)
            P.mm(S.h[:, :], [(kt.h[:, kb * 128:(kb + 1) * 128], qt.h[:, qs])], [kt, qt], [S])
            P.act(a.h[:], S.h[:], AF.Exp, [S], [a])
            if kb >= 4 * qb:
                v = kb - 4 * qb
                P.tt(a.h[:], a.h[:], c["cmask16"].h[:, v, :], ALU.mult, [a, c["cmask16"]], [a])

        emitS(0)
        for i, (qb, kb) in enumerate(blocks):
            O, Dn = PS[2 + qb % 2], PS[6 + qb % 2]
            qs = slice(qb * 512, (qb + 1) * 512)
            nkb = 4 * qb + 4
            if i + 1 < len(blocks):
                emitS(i + 1)
            a = A[i % 3]
            st, sp = (kb == 0), (kb == nkb - 1)
            P.mm(O.h[0:64, :], [(Vtok.h[:, kb, h * 64:(h + 1) * 64], a.h[:])], [Vtok, a], acc=[O], start=st, stop=sp)
            P.mm(Dn.h[0:64, :], [(k.ones16.h[:, 0:64], a.h[:])], [k.ones16, a], acc=[Dn], start=st, stop=sp)
            if kb % 2 == 1 and kb != nkb - 1:
                yield
            if kb != nkb - 1:
                continue

            def post(O=O, Dn=Dn, qb=qb, qs=qs, h=h):
                r_ = rec[qb % 2]
                P.op("dve", lambda e, r_=r_, Dn=Dn: e.reciprocal(out=r_.h[:], in_=Dn.h[0:64, :]), [Dn], [r_])
                p0 = (h % 2) * 64
                P.tt(k.YT[1].h[p0:p0 + 64, h // 2, qs], O.h[0:64, :], r_.h[:], ALU.mult, [O, r_], [k.rYT[1][h // 2][qb]])
            if pend[0] is not None:
                pend[0]()
            pend[0] = post
            yield

    interleave([gen(0)])
    for h in range(4):
        gs = [attn(h)]
        if h + 1 < 4:
            gs.append(gen(h + 1))
        interleave(gs)
    if pend[0] is not None:
        pend[0]()


def frac_sincos(k, x, tmp, out_sin, out_cos, np_, reads):
    P = k.P
    r, f = tmp
    sl = slice(0, np_)
    P.ts(r.h[sl], x.h[sl], MAGIC, MAGIC, ALU.add, ALU.subtract, [x] + reads, [r])
    P.tt(f.h[sl], x.h[sl], r.h[sl], ALU.subtract, [x, r], [f])
    P.act(out_sin.h[sl], f.h[sl], AF.Sin, [f], [out_sin], scale=TWO_PI)
    P.ts(f.h[sl], x.h[sl], 0.25, None, ALU.add, None, [x, out_sin], [f])
    P.ts(r.h[sl], f.h[sl], MAGIC, MAGIC, ALU.add, ALU.subtract, [f], [r])
    P.tt(f.h[sl], f.h[sl], r.h[sl], ALU.subtract, [f, r], [f])
    P.act(out_cos.h[sl], f.h[sl], AF.Sin, [f], [out_cos], scale=TWO_PI)


def s5_mixer(k, l):
    P, PS, c = k.P, k.PS, k.c
    wu, wglu, wc1, wc2 = k.WB
    if not hasattr(k, "ds_s5"):
        k.ds_s5 = P.new_dsem()
    ds = k.ds_s5
    aB = P.sb([128, 2, 2, 64], F32, "s5aB")
    bT = P.sb([128, 2, 2, 64], F32, "s5bT")
    ldB = P.sb([128, 2], F32, "s5ldB")
    aC = P.sb([128, 2, 16], F32, "s5aC")
    ldC = P.sb([128, 16], F32, "s5ldC")
    C1 = P.sb([128, 16, 128], BF16, "s5C1")
    C2 = P.sb([128, 16, 128], BF16, "s5C2")
    G16 = P.sb([128, 2, 256], BF16, "s5glu")
    for buf, nm in ((aB, "s5aB"), (bT, "s5bT"), (ldB, "s5ldB"), (aC, "s5aC"), (ldC, "s5ldC")):
        P.dma(buf.h[:], k.dram[nm][l], writes=[buf], dsem=ds)
    P.dma(C1.h[:], k.dram["s5C1"][l], writes=[C1], dsem=wc1.dsem, eng="pool")
    P.dma(C2.h[:], k.dram["s5C2"][l], writes=[C2], dsem=wc2.dsem, eng="pool")
    P.dma(G16.h[:], k.dram["s5glu"][l].rearrange("(kk p) w -> p kk w", p=128), writes=[G16], dsem=wglu.dsem, eng="pool")
    sg = c["sgn"]
    P.ts(C1.h[:], C1.h[:], sg.h[:, 0:1], None, ALU.mult, None, [C1, sg], [C1])
    P.ts(C2.h[:], C2.h[:], -1.0, None, ALU.mult, None, [C2], [C2])
    def sm(name, shape=(128, 128)):
        return P.sb(list(shape), F32, name)
    dtB = sm("dtB", (128, 2))
    P.act(dtB.h[:], ldB.h[:], AF.Exp, [ldB], [dtB])
    bbr = P.sb([128, 2, 64], F32, "bbr")
    bbi = P.sb([128, 2, 64], F32, "bbi")
    t_ = [sm(f"s5t{i}", (128, 64)) for i in range(10)]
    for cc in range(2):
        lr, li = aB.h[:, cc, 0, :], aB.h[:, cc, 1, :]
        br, bi = bT.h[:, cc, 0, :], bT.h[:, cc, 1, :]
        dcol = dtB.h[:, cc:cc + 1]
        mag, tr, sn, cs, x0, x1, nre, nim, den, cre = t_
        P.act(mag.h[:], lr, AF.Exp, [aB, dtB], [mag], scale=dcol)
        P.ts(tr.h[:], li, dcol, 1.0 / TWO_PI, ALU.mult, ALU.mult, [aB, dtB], [tr])
        frac_sincos(k, tr, (x0, x1), sn, cs, 128, [])
        P.tt(nre.h[:], mag.h[:], cs.h[:], ALU.mult, [mag, cs], [nre])
        P.ts(nre.h[:], nre.h[:], -1.0, None, ALU.add, None, [nre], [nre])
        P.tt(nim.h[:], mag.h[:], sn.h[:], ALU.mult, [mag, sn], [nim])
        P.tt(den.h[:], lr, lr, ALU.mult, [aB], [den])
        P.tt(x0.h[:], li, li, ALU.mult, [aB, sn, cs], [x0])
        P.tt(den.h[:], den.h[:], x0.h[:], ALU.add, [den, x0], [den])
        P.op("dve", lambda e, den=den: e.reciprocal(out=den.h[:], in_=den.h[:]), [den], [den])
        P.tt(x0.h[:], nre.h[:], lr, ALU.mult, [nre, aB], [x0])
        P.tt(x1.h[:], nim.h[:], li, ALU.mult, [nim, aB], [x1])
        P.tt(cre.h[:], x0.h[:], x1.h[:], ALU.add, [x0, x1], [cre])
        P.tt(cre.h[:], cre.h[:], den.h[:], ALU.mult, [cre, den], [cre])
        P.tt(x0.h[:], nim.h[:], lr, ALU.mult, [nim, aB], [x0])
        P.tt(x1.h[:], nre.h[:], li, ALU.mult, [nre, aB], [x1])
        P.tt(x0.h[:], x0.h[:], x1.h[:], ALU.subtract, [x0, x1], [x0])
        P.tt(x0.h[:], x0.h[:], den.h[:], ALU.mult, [x0, den], [x0])
        P.tt(x1.h[:], cre.h[:], br, ALU.mult, [cre, bT], [x1])
        P.tt(mag.h[:], x0.h[:], bi, ALU.mult, [x0, bT], [mag])
        P.tt(bbr.h[:, cc, :], x1.h[:], mag.h[:], ALU.subtract, [x1, mag], [bbr])
        P.tt(x1.h[:], cre.h[:], bi, ALU.mult, [cre, bT], [x1])
        P.tt(mag.h[:], x0.h[:], br, ALU.mult, [x0, bT], [mag])
        P.tt(bbi.h[:, cc, :], x1.h[:], mag.h[:], ALU.add, [x1, mag], [bbi])
    LB = P.sb([128, 16, 128], BF16, "s5LB")
    LBs = P.sb([128, 16, 128], BF16, "s5LBs")
    gm = c["grpmask"]
    for g in range(16):
        cc, gi = g // 8, g % 8
        mcol = gm.h[:, gi:gi + 1]
        P.ts(LB.h[:, g, 0:64], bbr.h[:, cc, :], mcol, None, ALU.mult, None, [bbr, gm], [LB])
        P.ts(LB.h[:, g, 64:128], bbi.h[:, cc, :], mcol, None, ALU.mult, None, [bbi, gm], [LB])
        P.ts(LBs.h[:, g, 0:64], bbi.h[:, cc, :], mcol, None, ALU.mult, None, [bbi, gm], [LBs])
        P.ts(LBs.h[:, g, 64:128], bbr.h[:, cc, :], mcol, -1.0, ALU.mult, ALU.mult, [bbr, gm], [LBs])
    dtC = sm("dtC", (128, 16))
    rC = sm("rC", (128, 16))
    fC = sm("fC", (128, 16))
    fx = sm("fCx", (128, 16))
    P.act(dtC.h[:], ldC.h[:], AF.Exp, [ldC], [dtC])
    P.tt(rC.h[:], aC.h[:, 0, :], dtC.h[:], ALU.mult, [aC, dtC], [rC])
    P.act(rC.h[:], rC.h[:], AF.Exp, [rC], [rC])
    P.tt(fC.h[:], aC.h[:, 1, :], dtC.h[:], ALU.mult, [aC, dtC], [fC])
    P.ts(fC.h[:], fC.h[:], 1.0 / TWO_PI, None, ALU.mult, None, [fC], [fC])
    P.ts(fx.h[:], fC.h[:], MAGIC, MAGIC, ALU.add, ALU.subtract, [fC], [fx])
    P.tt(fC.h[:], fC.h[:], fx.h[:], ALU.subtract, [fC, fx], [fC])
    Dg = P.sb([128, 2, 128], BF16, "s5Dg")
    pd = k.prm["s5d"]
    for cc in range(2):
        P.ts(Dg.h[:, cc, :], c["ident"].h[:], pd.h[:, cc, l:l + 1], None, ALU.mult, None, [c["ident"], pd], [Dg])
    uT = P.sb([128, 2, T], BF16, "s5uT")
    gT = P.sb([128, 2, T], BF16, "s5gT")
    for cc in range(2):
        for j in range(4):
            ps = PS[4 + j % 2]
            inproj_fm(k, ps, 128, wu, cc * 128, j)
            P.cp(uT.h[:, cc, j * 512:(j + 1) * 512], ps.h[:], [ps], [uT], eng="act")
    onesf = sm("s5ones", (128, 512))
    P.op("dve", lambda e: e.memset(onesf.h[:], 1.0), [], [onesf])
    Rt = [sm(f"s5Rt{i}", (128, 512)) for i in range(2)]
    X = [sm(f"s5x{i}", (128, 512)) for i in range(2)]
    Xr = [sm(f"s5xr{i}", (128, 512)) for i in range(2)]
    Xf = [sm(f"s5xf{i}", (128, 512)) for i in range(2)]
    CO = [sm(f"s5co{i}", (128, 512)) for i in range(2)]
    SI = [sm(f"s5si{i}", (128, 512)) for i in range(2)]
    BT = [sm(f"s5bt{i}", (128, 512)) for i in range(2)]
    B2 = [sm(f"s5b2{i}", (128, 512)) for i in range(2)]
    ST = [sm(f"s5st{i}", (128, 512)) for i in range(2)]
    STS = [[ST[0], sm("s5st2", (128, 512))], [ST[1], sm("s5st3", (128, 512))]]
    Z1 = [P.sb([128, 512], BF16, f"s5z1{i}") for i in range(2)]
    Z2 = [P.sb([128, 512], BF16, f"s5z2{i}") for i in range(2)]
    def fsc(x, r, f, si, co, n=512):
        sl = slice(0, n)
        P.ts(r.h[:, sl], x.h[:, sl], MAGIC, MAGIC, ALU.add, ALU.subtract, [x], [r])
        P.tt(f.h[:, sl], x.h[:, sl], r.h[:, sl], ALU.subtract, [x, r], [f])
        P.act(si.h[:, sl], f.h[:, sl], AF.Sin, [f], [si], scale=TWO_PI)
        P.act(f.h[:, sl], x.h[:, sl], AF.Identity, [x, si], [f], bias=0.25)
        P.ts(r.h[:, sl], f.h[:, sl], MAGIC, MAGIC, ALU.add, ALU.subtract, [f], [r])
        P.tt(f.h[:, sl], f.h[:, sl], r.h[:, sl], ALU.subtract, [f, r], [f])
        P.act(co.h[:, sl], f.h[:, sl], AF.Sin, [f], [co], scale=TWO_PI)

    p512, c512, s512 = sm("s5p512", (128, 16)), sm("s5c512", (128, 16)), sm("s5s512", (128, 16))
    P.ts(p512.h[:], fC.h[:], 512.0, None, ALU.mult, None, [fC], [p512])
    fsc(p512, X[0], Xf[0], s512, c512, n=16)
    P.ts(s512.h[:], s512.h[:], c["sgn"].h[:, 0:1], None, ALU.mult, None, [s512, c["sgn"]], [s512])
    MROT = [sm(f"s5mrot{i}", (128, 128)) for i in range(2)]
    INI = [[sm(f"s5ini{b}{i}", (128, 1)) for i in range(2)] for b in range(2)]
    BTS = [[BT[0], CO[0]], [BT[1], CO[1]]]
    B2S = [[B2[0], SI[0]], [B2[1], SI[1]]]
    Z1S = [[Z1[0], P.sb([128, 512], BF16, "s5z1b")], [Z1[1], P.sb([128, 512], BF16, "s5z1c")]]
    Z2S = [[Z2[0], P.sb([128, 512], BF16, "s5z2b")], [Z2[1], P.sb([128, 512], BF16, "s5z2c")]]
    TABS = [sm(f"s5tabs{i}", (128, 512)) for i in range(2)]
    TABC = [sm(f"s5tabc{i}", (128, 512)) for i in range(2)]

    def lane(cc, gi, b):
        g = cc * 8 + gi
        rt, x, si, co, mrot = Rt[b], X[b], TABS[b], TABC[b], MROT[b]
        pa, pb = PS[4 + 2 * b], PS[5 + 2 * b]
        P.act(rt.h[:], onesf.h[:], AF.Copy, [onesf, rC], [rt], scale=rC.h[:, g:g + 1])
        P.act(x.h[:], c["iota512"].h[:], AF.Copy, [c["iota512"], fC], [x], scale=fC.h[:, g:g + 1])
        yield
        fsc(x, Xr[b], Xf[b], si, co)
        P.ts(mrot.h[:], c["ident"].h[:], c512.h[:, g:g + 1], None, ALU.mult, None, [c["ident"], c512], [mrot])
        yield
        P.stt(mrot.h[:], c["swapid"].h[:], s512.h[:, g:g + 1], mrot.h[:], ALU.mult, ALU.add, [c["swapid"], s512, mrot], [mrot])
        yield
        sts = {}

        def pre(j):
            js = slice(j * 512, (j + 1) * 512)
            bt, b2 = BTS[b][j % 2], B2S[b][j % 2]
            P.mm(pa.h[:, :], [(LB.h[:, g, :], uT.h[:, cc, js])], [LB, uT], [pa])
            P.mm(pb.h[:, :], [(LBs.h[:, g, :], uT.h[:, cc, js])], [LBs, uT], [pb])
            yield
            P.cp(bt.h[:], pa.h[:], [pa], [bt], eng="act")
            P.cp(b2.h[:], pb.h[:], [pb], [b2], eng="act")
            yield
            P.tt(bt.h[:], bt.h[:], co.h[:], ALU.mult, [bt, co], [bt])
            yield
            P.tt(b2.h[:], b2.h[:], si.h[:], ALU.mult, [b2, si], [b2])
            yield
            P.tt(bt.h[:], bt.h[:], b2.h[:], ALU.add, [bt, b2], [bt])
            yield

        def post(j):
            bt = BTS[b][j % 2]
            st = STS[b][j % 2]
            z1, z2 = Z1S[b][j % 2], Z2S[b][j % 2]
            if j == 0:
                P.scan(st.h[:], rt.h[:], bt.h[:], 0.0, [rt, bt], [st])
            else:
                prev = sts[j - 1]
                ini = INI[b][j % 2]
                P.mm(pa.h[:, 0:1], [(mrot.h[:, :], prev.h[:, 511:512])], [mrot, prev], [pa])
                P.cp(ini.h[:], pa.h[:, 0:1], [pa], [ini], eng="act")
                yield
                P.scan(st.h[:], rt.h[:], bt.h[:], ini.h[:, 0:1], [rt, bt, ini], [st])
            sts[j] = st
            yield
            P.tt(z1.h[:], st.h[:], co.h[:], ALU.mult, [st, co], [z1])
            P.tt(z2.h[:], st.h[:], si.h[:], ALU.mult, [st, si], [z2], eng="pool")
            yield
            P.mm(PS[j].h[:, :], [(C1.h[:, g, :], z1.h[:])], [C1, z1], acc=[PS[j]], start=(gi == 0), stop=False)
            P.mm(PS[j].h[:, :], [(C2.h[:, g, :], z2.h[:])], [C2, z2], acc=[PS[j]], start=False, stop=False)
            yield

        yield from pre(0)
        for j in range(4):
            if j + 1 < 4:
                yield from pre(j + 1)
            yield from post(j)

    for cc in range(2):
        for gp in range(4):
            interleave([lane(cc, 2 * gp, 0), lane(cc, 2 * gp + 1, 1)])
        for j in range(4):
            js = slice(j * 512, (j + 1) * 512)
            P.mm(PS[j].h[:, :], [(Dg.h[:, cc, :], uT.h[:, cc, js])], [Dg, uT], acc=[PS[j]], start=False, stop=True)
            y, t = BT[j % 2], B2[j % 2]
            P.cp(y.h[:], PS[j].h[:], [PS[j]], [y], eng="act")
            cg = math.sqrt(2.0 / math.pi)
            P.tt(t.h[:], y.h[:], y.h[:], ALU.mult, [y], [t])
            P.ts(t.h[:], t.h[:], 2.0 * cg * 0.044715, 2.0 * cg, ALU.mult, ALU.add, [t], [t])
            P.tt(t.h[:], t.h[:], y.h[:], ALU.mult, [t, y], [t])
            P.act(t.h[:], t.h[:], AF.Sigmoid, [t], [t])
            P.tt(gT.h[:, cc, js], t.h[:], y.h[:], ALU.mult, [t, y], [gT])
    for oc in range(2):
        for j in range(4):
            js = slice(j * 512, (j + 1) * 512)
            ps = PS[4 + j % 2]
            P.mm(ps.h[:, :], [(G16.h[:, kc, oc * 128:(oc + 1) * 128], gT.h[:, kc, js]) for kc in range(2)], [G16, gT], [ps])
            sgm = ST[j % 2]
            P.act(sgm.h[:], ps.h[:], AF.Sigmoid, [ps], [sgm])
            P.tt(k.YT[0].h[:, oc, js], gT.h[:, oc, js], sgm.h[:], ALU.mult, [gT, sgm], [k.rYT[0][oc][j]])


def hgrn_setup(k):
    P = k.P
    raw = k.prm["lbraw"]
    e = P.sb([64, 4, NL], F32, "lb_e")
    tot = P.sb([64, 4, 1], F32, "lb_tot")
    k.LBt = P.sb([64, 4, NL], F32, "LBt")
    k.OML = P.sb([64, 4, NL], F32, "OML")
    k.NOML = P.sb([64, 4, NL], F32, "NOML")
    P.act(e.h[:], raw.h[:], AF.Exp, [raw], [e])
    P.tt(tot.h[:, :, 0], e.h[:, :, 0], e.h[:, :, 1], ALU.add, [e], [tot])
    for l in range(2, NL):
        P.tt(tot.h[:, :, 0], tot.h[:, :, 0], e.h[:, :, l], ALU.add, [e, tot], [tot])
    P.op("dve", lambda ee: ee.reciprocal(out=tot.h[:, :, 0], in_=tot.h[:, :, 0]), [tot], [tot])
    P.op("dve", lambda ee: ee.memset(k.LBt.h[:, :, 0], 0.0), [], [k.LBt])
    for l in range(1, NL):
        if l == 1:
            P.cp(k.LBt.h[:, :, 1], e.h[:, :, 1], [e], [k.LBt])
        else:
            P.tt(k.LBt.h[:, :, l], k.LBt.h[:, :, l - 1], e.h[:, :, l], ALU.add, [e, k.LBt], [k.LBt])
    for l in range(1, NL):
        P.tt(k.LBt.h[:, :, l], k.LBt.h[:, :, l], tot.h[:, :, 0], ALU.mult, [k.LBt, tot], [k.LBt])
    P.ts(k.OML.h[:], k.LBt.h[:], -1.0, 1.0, ALU.mult, ALU.add, [k.LBt], [k.OML])
    P.ts(k.NOML.h[:], k.OML.h[:], -1.0, None, ALU.mult, None, [k.OML], [k.NOML])


def hgrn_mixer(k, l):
    P, PS, c = k.P, k.PS, k.c
    wq, wf, wi, wg = k.WB
    Vtok = P.sb([128, 16, 256], BF16, "hVtok")
    QT = P.sb([64, T], BF16, "hQT")
    KT = P.sb([64, T], BF16, "hKT")
    Khtok = P.sb([128, 16, 64], BF16, "hKhtok")
    US = P.sb([64, 64, 128], BF16, "hUS")
    dco = P.sb([64, 128], F32, "hdco")
    dco8 = P.sb([64, 8, 128], F32, "hdco8")

    def f(name):
        return P.sb([64, 512], F32, name)
    S1 = [[f(f"h{n}{i}") for n in ("A", "B", "C", "D", "E")] for i in range(4)]
    S4 = [[f(f"h{n}{i}") for n in ("osb", "sqo", "rstd", "sgs")] for i in range(2)]
    Vexp = [P.sb([128, 8, 64], BF16, f"hVexp{i}") for i in range(2)]
    A = [P.sb([128, 128], BF16, f"hA{i}") for i in range(2)]
    hgn = k.prm["hgn"]
    idn = c["ident"]
    for i in range(16):
        ps = PS[4 + i % 2]
        inproj_tm(k, ps, 256, wi, 0, i)
        P.cp(Vtok.h[:, i, :], ps.h[:, 0:256], [ps], [Vtok], eng="act")

    def ph1(h, j, b):
        A_, B_, C_, D_, E_ = S1[b]
        pf, pq = PS[2 * b], PS[2 * b + 1]
        pt = pf
        lbc, omlc, nomlc = k.LBt.h[:, h, l:l + 1], k.OML.h[:, h, l:l + 1], k.NOML.h[:, h, l:l + 1]
        prm_r = [k.LBt, k.OML, k.NOML]
        js = slice(j * 512, (j + 1) * 512)
        inproj_fm(k, pf, 64, wf, h * 64, j)
        inproj_fm(k, pq, 64, wq, h * 64, j)
        yield
        P.act(A_.h[:], pf.h[0:64, :], AF.Sigmoid, [pf], [A_])
        P.act(D_.h[:], pq.h[0:64, :], AF.Silu, [pq], [D_])
        yield
        P.act(B_.h[:], A_.h[:], AF.Ln, [A_] + prm_r, [B_], scale=omlc, bias=lbc)
        P.ts(C_.h[:], A_.h[:], nomlc, omlc, ALU.mult, ALU.add, [A_] + prm_r, [C_])
        yield
        P.scan(B_.h[:], c["rst16"].h[0:64, :], B_.h[:], 0.0, [c["rst16"], B_], [B_])
        yield
        P.act(A_.h[:], B_.h[:], AF.Exp, [B_, C_], [A_])
        b3 = B_.h[:].rearrange("p (n s) -> p n s", s=16)
        P.tt(E_.h[:].rearrange("p (n s) -> p n s", s=16), b3[:, :, 15:16].broadcast_to([64, 32, 16]), b3, ALU.subtract, [B_], [E_])
        yield
        P.tt(QT.h[:, js], D_.h[:], A_.h[:], ALU.mult, [D_, A_], [QT])
        P.act(E_.h[:], E_.h[:], AF.Exp, [E_], [E_])
        yield
        P.act(A_.h[:], B_.h[:], AF.Exp, [B_, QT], [A_], scale=-1.0)
        P.tt(E_.h[:], C_.h[:], E_.h[:], ALU.mult, [C_, E_], [E_])
        yield
        P.tt(KT.h[:, js], C_.h[:], A_.h[:], ALU.mult, [C_, A_], [KT])
        P.act(dco.h[:, j * 32:(j + 1) * 32], b3[:, :, 15], AF.Exp, [B_], [dco])

        def trf(e):
            ins = None
            for q in range(4):
                ins = e.transpose(pt.h[:, q * 64:(q + 1) * 64], E_.h[:, q * 128:(q + 1) * 128], idn.h[0:64, 0:64])
            return ins
        P.op("pe", trf, [E_, idn], [pt])
        yield
        P.cp(Khtok.h[:, j * 4:(j + 1) * 4, :], pt.h[:, 0:256].rearrange("p (q d) -> p q d", q=4), [pt], [Khtok], eng="act")
        yield

    def ph4(h, j, b):
        osb, sqo, rstd, sgs = S4[b]
        S, O, I, X = (PS[0], PS[2], PS[4], PS[6]) if b == 0 else (PS[1], PS[3], PS[5], PS[7])
        a = A[b]
        js = slice(j * 512, (j + 1) * 512)
        for q in range(4):
            i = j * 4 + q
            ts_ = slice(i * 128, (i + 1) * 128)
            P.mm(S.h[:, 0:128], [(KT.h[:, ts_], QT.h[:, ts_])], [KT, QT], [S])
            yield
            P.tt(a.h[:], S.h[:, 0:128], c["bdmask16"].h[:], ALU.mult, [S, c["bdmask16"]], [a])
            yield
            P.mm(O.h[0:64, q * 128:(q + 1) * 128], [(Vtok.h[:, i, h * 64:(h + 1) * 64], a.h[:])], [Vtok, a], acc=[O])
            yield
        if j == 0:
            P.op("dve", lambda e: e.memset(I.h[0:64, 0:16], 0.0), [], [], acc=[I])

        def interf(e):
            ins = None
            for nn in range(32):
                n = j * 32 + nn
                if n == 0:
                    continue
                ins = e.matmul(I.h[0:64, nn * 16:(nn + 1) * 16], US.h[:, :, n - 1], QT.h[:, n * 16:(n + 1) * 16], start=True, stop=True)
            return ins
        P.op("pe", interf, [US, QT], [], acc=[I])
        yield
        P.cp(osb.h[:], O.h[0:64, :], [O], [osb], eng="act")
        yield
        P.tt(osb.h[:], osb.h[:], I.h[0:64, :], ALU.add, [osb, I], [osb])
        yield
        P.tt(sqo.h[:], osb.h[:], osb.h[:], ALU.mult, [osb], [sqo])
        yield
        P.mm(X.h[0:64, :], [(k.ones64.h[:, :], sqo.h[:])], [k.ones64, sqo], [X])
        yield
        P.act(rstd.h[:], X.h[0:64, :], AF.Ln, [X], [rstd], bias=EPS)
        yield
        P.act(rstd.h[:], rstd.h[:], AF.Exp, [rstd], [rstd], scale=-0.5)
        inproj_fm(k, X, 64, wg, h * 64, j)
        yield
        P.stt(osb.h[:], osb.h[:], hgn.h[:, h, l:l + 1], rstd.h[:], ALU.mult, ALU.mult, [osb, hgn, rstd], [osb])
        P.act(sgs.h[:], X.h[0:64, :], AF.Silu, [X], [sgs])
        yield
        p0 = (h % 2) * 64
        P.tt(k.YT[2].h[p0:p0 + 64, h // 2, js], osb.h[:], sgs.h[:], ALU.mult, [osb, sgs], [k.rYT[2][h // 2][j]])
        yield

    for h in range(4):
        interleave([ph1(h, j, j) for j in range(4)])
        for i in range(16):
            ve = Vexp[i % 2]
            pu = PS[6 + i % 2]
            P.tt(ve.h[:], Vtok.h[:, i:i + 1, h * 64:(h + 1) * 64].broadcast_to([128, 8, 64]), c["cexp16"].h[:], ALU.mult,
                 [Vtok, c["cexp16"]], [ve])
            P.mm(pu.h[0:64, :], [(Khtok.h[:, i, :], ve.h[:].rearrange("p n v -> p (n v)"))], [Khtok, ve], [pu])
            P.cp(US.h[:, :, i * 8:(i + 1) * 8], pu.h[0:64, :].rearrange("p (n v) -> p v n", n=8), [pu], [US], eng="act")
        P.op("dve", lambda e: e.memset(dco.h[:, 0:1], 0.0), [], [dco])
        P.cp(dco8.h[:], dco.h[:, :].unsqueeze(1).broadcast_to([64, 8, 128]), [dco], [dco8], eng="dve")
        for v8 in range(8):
            vs = slice(v8 * 8, (v8 + 1) * 8)
            P.scan(US.h[:, vs, :].rearrange("p v n -> p (v n)"), dco8.h[:].rearrange("p v n -> p (v n)"),
                   US.h[:, vs, :].rearrange("p v n -> p (v n)"), 0.0, [dco8, US], [US])
        interleave([ph4(h, 0, 0), ph4(h, 1, 1)])
        interleave([ph4(h, 2, 0), ph4(h, 3, 1)])


def ln_chunks(k, j, src, src_regs, g_prm, b_prm, l, tmps, extra_reads=(), write_b16=True):
    P, PS = k.P, k.PS
    js = slice(j * 512, (j + 1) * 512)
    sqa, sqb, mean, rstd, nmr = tmps
    Ps, Pq = PS[6], PS[7]
    for c in range(8):
        sq = (sqa, sqb)[c % 2]
        P.tt(sq.h[:], src(c), src(c), ALU.mult, [src_regs[c]] + list(extra_reads), [sq])
        P.mm(Ps.h[:, :], [(k.ones128.h[:, :], src(c))], [k.ones128, src_regs[c]], acc=[Ps], start=(c == 0), stop=(c == 7))
        P.mm(Pq.h[:, :], [(k.ones128.h[:, :], sq.h[:])], [k.ones128, sq], acc=[Pq], start=(c == 0), stop=(c == 7))
    P.act(mean.h[:], Ps.h[:], AF.Copy, [Ps], [mean], scale=1.0 / D)
    P.tt(sqa.h[:], mean.h[:], mean.h[:], ALU.mult, [mean], [sqa])
    P.stt(rstd.h[:], Pq.h[:], 1.0 / D, sqa.h[:], ALU.mult, ALU.subtract, [Pq, sqa], [rstd])
    P.act(rstd.h[:], rstd.h[:], AF.Ln, [rstd], [rstd], bias=EPS)
    P.act(rstd.h[:], rstd.h[:], AF.Exp, [rstd], [rstd], scale=-0.5)
    P.stt(nmr.h[:], mean.h[:], -1.0, rstd.h[:], ALU.mult, ALU.mult, [mean, rstd], [nmr])
    for c in range(8):
        t = (sqa, sqb)[c % 2]
        P.tt(t.h[:], src(c), rstd.h[:], ALU.mult, [src_regs[c], rstd], [t])
        P.tt(t.h[:], t.h[:], nmr.h[:], ALU.add, [t, nmr], [t])
        P.act(k.XT.h[:, c, js], t.h[:], AF.Identity, [t, g_prm, b_prm], [k.rXT[c][j]], scale=g_prm.h[:, c, l:l + 1],
              bias=b_prm.h[:, c, l:l + 1])
        if write_b16:
            P.act(k.XTb.h[:, c, js], k.XT.h[:, c, js], AF.Copy, [k.rXT[c][j]], [k.rXTb[c][j]])


def gate_phase(k, l):
    P, PS, cfg = k.P, k.PS, k.cfg
    if not hasattr(k, "ds_g"):
        k.ds_g = [P.new_dsem() for _ in range(4)]
    P.set_regions([(k.xt_off, k.up_off)])
    WBR = P.sb([128, 4, 2, D], BF16, "WBR")
    WG = [P.sb([128, 4, 8, 128], BF16, f"WG{i}") for i in range(2)]
    WBR.dsem = k.ds_g[0]
    WG[0].dsem, WG[1].dsem = k.ds_g[1], k.ds_g[2]
    ACC = [P.sb([128, 512], F32, f"ACC{i}") for i in range(2)]
    sgt = [P.sb([128, 512], F32, f"sgt{i}") for i in range(2)]
    tmp = [P.sb([128, 512], F32, f"gtmp{i}") for i in range(2)]
    P.set_regions([(k.up2_off, SB_TOP)])
    MIXT = P.sb([128, 8, T], BF16, "MIXT")
    rMIX = [[Reg(f"mix{c}_{j}") for j in range(4)] for c in range(8)]
    P.dma(WBR.h[:], k.dram["wbr"][l].rearrange("n (kc p) d -> p n kc d", p=128), writes=[WBR], dsem=WBR.dsem, eng="pool")
    it = 0
    for c in range(8):
        wgb = WG[c % 2]
        for n in range(4):
            col0 = n * 1024 + c * 128
            P.dma(wgb.h[:, n, :, :], k.dram["wg"][l, :, col0:col0 + 128].rearrange("(kk p) w -> p kk w", p=128), writes=[wgb],
                  dsem=wgb.dsem, eng="pool")
        for j in range(4):
            js = slice(j * 512, (j + 1) * 512)
            acc = ACC[j % 2]
            for n in range(4):
                pg, pb = PS[2 * (it % 2)], PS[2 * (it % 2) + 1]
                s_, t_ = sgt[it % 2], tmp[it % 2]
                it += 1
                P.mm(pg.h[:, :], [(wgb.h[:, n, kk, :], k.XTb.h[:, kk, js]) for kk in range(8)], [wgb] + xtb_regs(k, j), [pg])
                P.mm(pb.h[:, :], [(WBR.h[:, n, kc, c * 128:(c + 1) * 128], k.YT[n].h[:, kc, js]) for kc in range(2)],
                     [WBR, k.rYT[n][0][j], k.rYT[n][1][j]], [pb])
                P.act(s_.h[:], pg.h[:], AF.Sigmoid, [pg], [s_])
                if n == 0:
                    P.tt(acc.h[:], s_.h[:], pb.h[:], ALU.mult, [s_, pb], [acc])
                else:
                    P.tt(t_.h[:], s_.h[:], pb.h[:], ALU.mult, [s_, pb], [t_])
                    if n < 3:
                        P.tt(acc.h[:], acc.h[:], t_.h[:], ALU.add, [acc, t_], [acc])
                    else:
                        P.tt(MIXT.h[:, c, js], acc.h[:], t_.h[:], ALU.add, [acc, t_], [rMIX[c][j]])
    if cfg.get("taps"):
        tap(k, f"mix{l}", MIXT.h[:], [128, 8, T], [r for row in rMIX for r in row], BF16)
    P.fence(k.ARENA, [r for row in k.rXT for r in row])
    P.phase = f"L{l}.wo"
    P.set_regions([(k.up_off, k.up_off + 32768)])
    tmps = [P.sb([128, 512], F32, f"lnA{i}") for i in range(5)]
    g1, b1 = k.prm["ln1g"], k.prm["ln1b"]
    it = 0
    for j in range(4):
        js = slice(j * 512, (j + 1) * 512)
        P.dma(k.XT.h[:, :, js], k.spill[:, :, js], reads=k.rSP, writes=[k.rXT[c][j] for c in range(8)], dsem=k.ds_g[3])
        for oh in range(4):
            wob = k.WB[it % 4]
            it += 1
            P.dma(wob.h[:], k.dram["wo"][l, :, oh * 256:(oh + 1) * 256].rearrange("(kk p) w -> p kk w", p=128), writes=[wob], dsem=wob.dsem,
                  eng="pool")
            for o2 in range(2):
                oc = oh * 2 + o2
                po = PS[4 + oc % 2]
                P.mm(po.h[:, :], [(wob.h[:, kk, o2 * 128:(o2 + 1) * 128], MIXT.h[:, kk, js]) for kk in range(8)],
                     [wob] + [rMIX[kk][j] for kk in range(8)], [po])
                P.stt(k.XT.h[:, oc, js], k.XT.h[:, oc, js], float(ALPHA), po.h[:], ALU.mult, ALU.add, [k.rXT[oc][j], po], [k.rXT[oc][j]])
        if j > 0:
            jp = j - 1
            jps = slice(jp * 512, (jp + 1) * 512)
            ln_chunks(k, jp, lambda cc, jps=jps: k.XT.h[:, cc, jps], [k.rXT[cc][jp] for cc in range(8)], g1, b1, l, tmps)
    js = slice(3 * 512, 4 * 512)
    ln_chunks(k, 3, lambda cc, js=js: k.XT.h[:, cc, js], [k.rXT[cc][3] for cc in range(8)], g1, b1, l, tmps)
    P.set_regions(None)


def ffn_phase(k, l):
    P, PS, cfg, c = k.P, k.PS, k.cfg, k.c
    moe = (l % 2 == 1)
    li = l // 2
    if not hasattr(k, "ds_f"):
        k.ds_f = [P.new_dsem() for _ in range(8)]
    P.set_regions([(k.up_off, k.up_off + 32768), (k.up2_off, SB_TOP), (k.up_off + 32768, k.up2_off)])
    Wg = [P.sb([128, 8, 512], BF16, f"Wg{i}") for i in range(2)]
    Wu = [P.sb([128, 8, 512], BF16, f"Wu{i}") for i in range(2)]
    Wd = [P.sb([128, 4, D], BF16, f"Wd{i}") for i in range(2)]
    for i in range(2):
        Wg[i].dsem, Wu[i].dsem, Wd[i].dsem = k.ds_f[3 * i], k.ds_f[3 * i + 1], k.ds_f[3 * i + 2]
    H = [P.sb([128, 4, 512], BF16, f"H{i}") for i in range(2)]
    sgb = [P.sb([128, 512], F32, f"fsg{i}") for i in range(2)]
    tb = P.sb([128, 512], F32, "ftb")
    nexp = NE if moe else 1
    if moe:
        WT = P.sb([8, T], F32, "WT")
        WT16 = P.sb([64, T], BF16, "WT16")
        WR = P.sb([128, 8, 8], F32, "WR")
        LG = P.sb([128, 16, 8], F32, "LG")
        M8 = P.sb([128, 16, 8], F32, "M8")
        MK = P.sb([128, 16, 8], F32, "MK")
        EX = P.sb([128, 16, 8], F32, "EX")
        DN = P.sb([128, 16, 1], F32, "DN")
        P.dma(WR.h[:], k.dram["mor"][li].rearrange("(kk p) e -> p kk e", p=128), writes=[WR], dsem=k.ds_f[6])
        for i in range(16):
            ps = PS[6 + i % 2]
            P.mm(ps.h[:, 0:8], [(k.XT.h[:, kk, i * 128:(i + 1) * 128], WR.h[:, kk, :]) for kk in range(8)],
                 [WR] + [k.rXT[kk][i // 4] for kk in range(8)], [ps])
            P.cp(LG.h[:, i, :], ps.h[:, 0:8], [ps], [LG], eng="act")
            P.op("dve", lambda e, i=i: e.max(out=M8.h[:, i, :], in_=LG.h[:, i, :]), [LG], [M8])
        P.tt(MK.h[:], LG.h[:], M8.h[:, :, 1:2].broadcast_to([128, 16, 8]), ALU.is_ge, [LG, M8], [MK])
        P.tt(EX.h[:], LG.h[:], M8.h[:, :, 0:1].broadcast_to([128, 16, 8]), ALU.subtract, [LG, M8], [EX])
        P.act(EX.h[:], EX.h[:], AF.Exp, [EX], [EX])
        P.tt(EX.h[:], EX.h[:], MK.h[:], ALU.mult, [EX, MK], [EX])
        P.op("dve", lambda e: e.tensor_reduce(out=DN.h[:, :, 0], in_=EX.h[:], axis=mybir.AxisListType.X, op=ALU.add), [EX], [DN])
        P.op("dve", lambda e: e.reciprocal(out=DN.h[:], in_=DN.h[:]), [DN], [DN])
        P.tt(EX.h[:], EX.h[:], DN.h[:].broadcast_to([128, 16, 8]), ALU.mult, [EX, DN], [EX])
        for i4 in range(4):
            ps = PS[6 + i4 % 2]

            def trf(e, i4=i4, ps=ps):
                ins = None
                for q in range(4):
                    ins = e.transpose(ps.h[0:8, q * 128:(q + 1) * 128], EX.h[:, i4 * 4 + q, :], c["ident"].h[:, :])
                return ins
            P.op("pe", trf, [EX, c["ident"]], [ps])
            P.cp(WT.h[:, i4 * 512:(i4 + 1) * 512], ps.h[0:8, :], [ps], [WT], eng="act")
        P.op("dve", lambda e: e.memset(WT16.h[:], 0.0), [], [WT16])
        P.cp(WT16.h[0:8, :], WT.h[:, :], [WT], [WT16])
        P.tt(WT.h[:, :], WT.h[:, :], WT16.h[0:8, :], ALU.subtract, [WT, WT16], [WT])
        P.cp(WT16.h[32:40, :], WT.h[:, :], [WT], [WT16])
    for cc in range(8):
        for j in range(4):
            js = slice(j * 512, (j + 1) * 512)
            if (cc + j) % 2 == 0:
                P.act(k.XT.h[:, cc, js], k.XT.h[:, cc, js], AF.Copy, [k.rXT[cc][j]], [k.rXT[cc][j]], scale=float(ALPHA))
            else:
                P.ts(k.XT.h[:, cc, js], k.XT.h[:, cc, js], float(ALPHA), None, ALU.mult, None, [k.rXT[cc][j]], [k.rXT[cc][j]])
    it = 0
    pending = None
    for e_ in range(nexp):
        if moe:
            gsrc, usrc, dsrc = k.dram["mog"][li, e_], k.dram["mou"][li, e_], k.dram["mod"][li, e_]
        else:
            gsrc, usrc, dsrc = k.dram["ffg"][li], k.dram["ffu"][li], k.dram["ffd"][li]
        for fb in range(7):
            b = it % 2
            it += 1
            wg_, wu_, wd_ = Wg[b], Wu[b], Wd[b]
            fs = slice(fb * 512, (fb + 1) * 512)
            P.dma(wg_.h[:], gsrc[:, fs].rearrange("(kk p) w -> p kk w", p=128), writes=[wg_], dsem=wg_.dsem, eng="pool")
            P.dma(wu_.h[:], usrc[:, fs].rearrange("(kk p) w -> p kk w", p=128), writes=[wu_], dsem=wu_.dsem, eng="pool")
            P.dma(wd_.h[:], dsrc[fs, :].rearrange("(fc p) d -> p fc d", p=128), writes=[wd_], dsem=wd_.dsem, eng="pool")
            for j in range(4):
                js = slice(j * 512, (j + 1) * 512)
                hb = H[j % 2]
                if moe:
                    pw = PS[6 + j % 2]
                    P.mm(pw.h[:, :], [(c["sel16"].h[:, e_, :], WT16.h[:, js])], [c["sel16"], WT16], [pw])
                for fc in range(4):
                    pg, pu = PS[2 * (fc % 2)], PS[2 * (fc % 2) + 1]
                    xr = xtb_regs(k, j)
                    P.mm(pg.h[:, :], [(wg_.h[:, kk, fc * 128:(fc + 1) * 128], k.XTb.h[:, kk, js]) for kk in range(8)], [wg_] + xr, [pg])
                    P.mm(pu.h[:, :], [(wu_.h[:, kk, fc * 128:(fc + 1) * 128], k.XTb.h[:, kk, js]) for kk in range(8)], [wu_] + xr, [pu])
                    s_ = sgb[fc % 2]
                    P.act(s_.h[:], pg.h[:], AF.Silu, [pg], [s_])
                    if moe:
                        P.tt(tb.h[:], s_.h[:], pw.h[:], ALU.mult, [s_, pw], [tb])
                        P.tt(hb.h[:, fc, :], tb.h[:], pu.h[:], ALU.mult, [tb, pu], [hb])
                    else:
                        P.tt(hb.h[:, fc, :], s_.h[:], pu.h[:], ALU.mult, [s_, pu], [hb])
                    if fc == 1 and pending is not None:
                        pending()
                        pending = None

                def down(wd_=wd_, hb=hb, j=j, js=js):
                    for oc in range(8):
                        pd = PS[4 + oc % 2]
                        P.mm(pd.h[:, :], [(wd_.h[:, fc, oc * 128:(oc + 1) * 128], hb.h[:, fc, :]) for fc in range(4)], [wd_, hb], [pd])
                        P.tt(k.XT.h[:, oc, js], k.XT.h[:, oc, js], pd.h[:], ALU.add, [k.rXT[oc][j], pd], [k.rXT[oc][j]])
                pending = down
    if pending is not None:
        pending()
    if cfg.get("taps"):
        tap(k, f"fpre{l}", k.XT.h[:], [128, 8, T], [r for row in k.rXT for r in row])
    P.fence(k.ARENA)
    P.phase = f"L{l}.ple"
    P.set_regions([(k.up_off, k.up_off + 32768), (k.up2_off, SB_TOP)])
    pT = P.sb([128, 2, T], BF16, "pT")
    WPP = P.sb([128, 2, D], BF16, "WPP")
    WPP.dsem = k.ds_f[0]
    ptl = [P.sb([128, 256], F32, f"ptl{i}") for i in range(4)]
    if not hasattr(k, "ds_ptl"):
        k.ds_ptl = [P.new_dsem() for _ in range(2)]
    ptl[0].dsem, ptl[1].dsem, ptl[2].dsem, ptl[3].dsem = k.ds_f[6], k.ds_f[7], k.ds_ptl[0], k.ds_ptl[1]
    sgp = [P.sb([128, 512], F32, f"psg{i}") for i in range(2)]
    tmps = [P.sb([128, 512], F32, f"lnB{i}") for i in range(5)]
    P.dma(WPP.h[:], k.dram["plp"][l].rearrange("(kc p) d -> p kc d", p=128), writes=[WPP], dsem=WPP.dsem, eng="pool")
    rpT = [Reg(f"pT{j}") for j in range(4)]

    def p_tiles(j):
        for i in range(4 * j, 4 * j + 4):
            pt = ptl[i % 4]
            P.dma(pt.h[:], k.dram["p"][l, i * 128:(i + 1) * 128, :], writes=[pt], dsem=pt.dsem)
            ps = PS[6 + i % 2]

            def trf(e, pt=pt, ps=ps):
                ins = None
                for q in range(2):
                    ins = e.transpose(ps.h[:, q * 128:(q + 1) * 128], pt.h[:, q * 128:(q + 1) * 128], c["ident"].h[:, :])
                return ins
            P.op("pe", trf, [pt, c["ident"]], [ps])
            P.cp(pT.h[:, :, i * 128:(i + 1) * 128], ps.h[:, 0:256].rearrange("p (q t) -> p q t", q=2), [ps], [rpT[j]], eng="act")
    p_tiles(0)
    g2, b2 = k.prm["ln2g"], k.prm["ln2b"]
    spill_next = (l + 1 < cfg.get("nl", 0))
    it = 0
    for j in range(4):
        js = slice(j * 512, (j + 1) * 512)
        for oh in range(4):
            wb = k.WB[it % 4]
            it += 1
            P.dma(wb.h[:], k.dram["plg"][l, :, oh * 256:(oh + 1) * 256].rearrange("(kk p) w -> p kk w", p=128), writes=[wb], dsem=wb.dsem,
                  eng="pool")
            for o2 in range(2):
                oc = oh * 2 + o2
                pa, pb = PS[2 * (oc % 2)], PS[2 * (oc % 2) + 1]
                P.mm(pa.h[:, :], [(wb.h[:, kk, o2 * 128:(o2 + 1) * 128], k.XTb.h[:, kk, js]) for kk in range(8)], [wb] + xtb_regs(k, j), [pa])
                P.mm(pb.h[:, :], [(WPP.h[:, kc, oc * 128:(oc + 1) * 128], pT.h[:, kc, js]) for kc in range(2)], [WPP, rpT[j]], [pb])
                s_ = sgp[oc % 2]
                P.act(s_.h[:], pa.h[:], AF.Sigmoid, [pa], [s_])
                P.tt(s_.h[:], s_.h[:], pb.h[:], ALU.mult, [s_, pb], [s_])
                P.tt(k.XT.h[:, oc, js], k.XT.h[:, oc, js], s_.h[:], ALU.add, [k.rXT[oc][j], s_], [k.rXT[oc][j]])
        if j + 1 < 4:
            p_tiles(j + 1)
        if j > 0:
            jp = j - 1
            jps = slice(jp * 512, (jp + 1) * 512)
            ln_chunks(k, jp, lambda cc, jps=jps: k.XT.h[:, cc, jps], [k.rXT[cc][jp] for cc in range(8)], g2, b2, l, tmps,
                      write_b16=spill_next)
            if spill_next:
                P.dma(k.spill[:, :, jps], k.XT.h[:, :, jps], reads=[k.rXT[cc][jp] for cc in range(8)], writes=k.rSP, dsem=k.ds_sp,
                      arena=False)
    js = slice(3 * 512, 4 * 512)
    ln_chunks(k, 3, lambda cc, js=js: k.XT.h[:, cc, js], [k.rXT[cc][3] for cc in range(8)], g2, b2, l, tmps, write_b16=spill_next)
    if spill_next:
        P.dma(k.spill[:, :, js], k.XT.h[:, :, js], reads=[k.rXT[cc][3] for cc in range(8)], writes=k.rSP, dsem=k.ds_sp, arena=False)
    P.set_regions(None)


def ret_mixer(k, l):
    P, PS, c = k.P, k.PS, k.c
    gam = _gammas()
    wq, wk, wqp, wkp = k.WB
    if not hasattr(k, "wbx_dsems"):
        k.wbx_dsems = [P.new_dsem() for _ in range(2)]
    wv = P.sb([128, 8, 256], BF16, "WBv")
    wg = P.sb([128, 8, 256], BF16, "WBg")
    wv.dsem, wg.dsem = k.wbx_dsems
    wb_load(k, wv, l, C_RV, 256)
    wb_load(k, wg, l, C_RG, 256)
    Vtok = P.sb([128, 16, 256], BF16, "Vtok")
    QT = [P.sb([64, T], BF16, f"QT{i}") for i in range(2)]
    KT = [P.sb([64, T], BF16, f"KT{i}") for i in range(2)]
    t1 = [P.sb([64, 512], F32, f"rt1_{i}") for i in range(2)]
    t2 = [P.sb([64, 512], F32, f"rt2_{i}") for i in range(2)]
    A = [P.sb([128, 512], BF16, f"A{i}") for i in range(3)]
    osb = [P.sb([64, 512], F32, f"osb{i}") for i in range(2)]
    tmp = [P.sb([64, 512], F32, f"lnt{i}") for i in range(3)]
    sgs = P.sb([64, 512], F32, "sgs")
    prg, prb = k.prm["retg"], k.prm["retb"]
    pend = [None]
    for i in range(16):
        ps = PS[4 + i % 2]
        inproj_tm(k, ps, 256, wv, 0, i)
        P.cp(Vtok.h[:, i, :], ps.h[:, 0:256], [ps], [Vtok], eng="act")
    def gen(h):
        qt, kt = QT[h % 2], KT[h % 2]
        for j in range(4):
            js = slice(j * 512, (j + 1) * 512)
            for which, (wa, wb_, dst) in enumerate(((wq, wqp, qt), (wk, wkp, kt))):
                pa, pb = PS[4 + 2 * which], PS[5 + 2 * which]
                inproj_fm(k, pa, 64, wa, h * 64, j)
                inproj_fm(k, pb, 64, wb_, h * 64, j)
                a1, a2 = t1[which], t2[which]
                P.tt(a1.h[:], pa.h[0:64, :], k.COS.h[0:64, js], ALU.mult, [pa, k.COS], [a1])
                P.tt(a2.h[:], pb.h[0:64, :], k.SIN.h[0:64, js], ALU.mult, [pb, k.SIN], [a2])
                yield
                P.tt(a1.h[:], a1.h[:], a2.h[:], ALU.add, [a1, a2], [a1])
                if which == 0:
                    P.tt(dst.h[:, js], a1.h[:], c["gq"].h[:, h, :], ALU.mult, [a1, c["gq"]], [dst])
                else:
                    gkb = c["gk"].h[:, h:h + 1, :].broadcast_to([64, 4, 128])
                    P.tt(dst.h[:, js].rearrange("p (a b) -> p a b", a=4), a1.h[:].rearrange("p (a b) -> p a b", a=4), gkb,
                         ALU.mult, [a1, c["gk"]], [dst])
                yield

    def attn(h):
        qt, kt = QT[h % 2], KT[h % 2]
        blocks = [(qb, kb) for qb in range(4) for kb in range(4 * qb + 4)]

        def emitS(i):
            qb, kb = blocks[i]
            S, a = PS[i % 2], A[i % 3]
            qs = slice(qb * 512, (qb + 1) * 512)
            P.mm(S.h[:, :], [(kt.h[:, kb * 128:(kb + 1) * 128], qt.h[:, qs])], [kt, qt], [S])
            bf = (gam[h] ** (512 * qb - 128 * kb)) * (64.0 ** -0.5)
            if kb < 4 * qb:
                P.act(a.h[:], S.h[:], AF.Copy, [S], [a], scale=float(bf))
            else:
                v = kb - 4 * qb
                P.stt(a.h[:], S.h[:], float(bf), c["cmask16"].h[:, v, :], ALU.mult, ALU.mult, [S, c["cmask16"]], [a])

        emitS(0)
        for i, (qb, kb) in enumerate(blocks):
            O = PS[2 + qb % 2]
            qs = slice(qb * 512, (qb + 1) * 512)
            nkb = 4 * qb + 4
            if i + 1 < len(blocks):
                emitS(i + 1)
            P.mm(O.h[0:64, :], [(Vtok.h[:, kb, h * 64:(h + 1) * 64], A[i % 3].h[:])], [Vtok, A[i % 3]], acc=[O],
                 start=(kb == 0), stop=(kb == nkb - 1))
            if kb % 2 == 1 and kb != nkb - 1:
                yield
            if kb != nkb - 1:
                continue

            def post(O=O, qb=qb, qs=qs, h=h):
                ob = osb[qb % 2]
                P.cp(ob.h[:], O.h[0:64, :], [O], [ob], eng="act")
                ln_feat(k, ob, 64, k.ones64, PS[6], PS[7], tmp, None)
                mean, rstd = tmp[0], tmp[1]
                P.tt(ob.h[:], ob.h[:], mean.h[:], ALU.subtract, [ob, mean], [ob])
                P.stt(ob.h[:], ob.h[:], prg.h[:, h, l:l + 1], rstd.h[:], ALU.mult, ALU.mult, [ob, prg, rstd], [ob])
                P.ts(ob.h[:], ob.h[:], prb.h[:, h, l:l + 1], None, ALU.add, None, [ob, prb], [ob])
                pg = PS[4]
                inproj_fm(k, pg, 64, wg, h * 64, qb)
                P.act(sgs.h[:], pg.h[0:64, :], AF.Silu, [pg], [sgs])
                p0 = (h % 2) * 64
                P.tt(k.YT[3].h[p0:p0 + 64, h // 2, qs], ob.h[:], sgs.h[:], ALU.mult, [ob, sgs], [k.rYT[3][h // 2][qb]])
            if pend[0] is not None:
                pend[0]()
            pend[0] = post
            yield

    interleave([gen(0)])
    for h in range(4):
        gs = [attn(h)]
        if h + 1 < 4:
            gs.append(gen(h + 1))
        interleave(gs)
    if pend[0] is not None:
        pend[0]()


def host_shared(inp):
    sh = {}
    for nm, v in host_consts().items():
        sh["c_" + nm] = v
    for nm, v in host_params(inp).items():
        sh["p_" + nm] = v
    sh.update(host_weights(inp))
    return sh


def core_inputs(inp, b, shared):
    m = dict(shared)
    m["x"] = np.ascontiguousarray(inp["x"][b])
    m["pos"] = np.ascontiguousarray(inp["positions"][b:b + 1]).astype(np.int32)
    m["p"] = np.ascontiguousarray(inp["p"][:, b])
    return m


def kernel(**inputs):
    inp = {kk: np.asarray(v) for kk, v in inputs.items()}
    nc, k = build({"nl": NL})
    shared = host_shared(inp)
    in_maps = []
    for b in range(8):
        m = core_inputs(inp, b, shared)
        in_maps.append({kk: v for kk, v in m.items() if kk in k.dram})
    res = run_bass_kernel_spmd(nc, in_maps, core_ids=list(range(8)))
    out = np.stack([np.asarray(r["y"]) for r in res.results], axis=0)
    return out.astype(np.float32)
```
